# Optimizing a Trainium2 kernel written in Bass

```python
import math
import jax, jax.numpy as jnp
from jax import lax
import numpy as np

D_MODEL = 1024
BATCH = 8
SEQ = 4096
DEPTH = 4

N_MIXERS = 3
RMS_EPS = 1e-6
NEG_INF = -1e30
TINY = 1e-30
FORCE_SCORE = 1e9

N_BUCKETS = 32
REL_MAX_DISTANCE = 2048
N_BIAS_HEADS = 16

A_GROUPS = ((128, 1), (512, 4), (2048, 16))
A_HEADS = 16
A_HEAD_DIM = D_MODEL // A_HEADS
A_Q_BLOCK = 128
A_IN_WIDTH = len(A_GROUPS) * 3 * A_HEADS * A_HEAD_DIM

B_HEADS = 16
B_KV_HEADS = 4
B_HEAD_DIM = 64
B_CMP_LEN = 32
B_CMP_STRIDE = 16
B_CMP_HIDDEN = 256
B_SEL_BLOCK = 64
B_TOP_N = 16
B_WINDOW = 512
B_Q_BLOCK = 64
B_WIN_Q_BLOCK = 128
B_IN_WIDTH = B_HEADS * B_HEAD_DIM + 6 * B_KV_HEADS * B_HEAD_DIM + 3 * B_HEADS

C_HEADS = 8
C_HEAD_DIM = 128
C_WIDTH = C_HEADS * C_HEAD_DIM
C_CONV = 4
C_CHUNK = 64
C_IN_WIDTH = 4 * C_WIDTH + 2 * C_HEADS

FFN_HIDDEN = (8 * D_MODEL + 3 * 256 - 1) // (3 * 256) * 256

kernel_name = 'hybrid_dilated_nsa_gdn_trunk'


def rms_norm(x, gain):
    xf = x.astype(jnp.float32)
    y = xf * lax.rsqrt(jnp.mean(xf * xf, axis=-1, keepdims=True) + RMS_EPS)
    return (y * gain.astype(jnp.float32)).astype(x.dtype)


def l2_normalize(x):
    return x * lax.rsqrt(jnp.sum(x * x, axis=-1, keepdims=True) + RMS_EPS)


def rel_bucket(dist):
    dist = jnp.maximum(dist, 0)
    max_exact = N_BUCKETS // 2
    d_f = jnp.maximum(dist, 1).astype(jnp.float32)
    large = max_exact + (jnp.log(d_f / max_exact) / math.log(REL_MAX_DISTANCE / max_exact)
                         * (N_BUCKETS - max_exact)).astype(jnp.int32)
    return jnp.where(dist < max_exact, dist, jnp.minimum(large, N_BUCKETS - 1))


def masked_softmax(logits, mask):
    logits = jnp.where(mask, logits.astype(jnp.float32), NEG_INF)
    m = jnp.max(logits, axis=-1, keepdims=True)
    e = jnp.where(mask, jnp.exp(logits - m), 0.0)
    z = jnp.maximum(jnp.sum(e, axis=-1, keepdims=True), TINY)
    return e / z, (m + jnp.log(z))[..., 0]


def banded_causal_attention(q, k, v, bias_by_dist, window, block):
    bx, n, g, L, dh = q.shape
    n_blocks = -(-L // block)
    Lp = n_blocks * block
    n_prev = -(-window // block)
    span = (n_prev + 1) * block
    q = jnp.pad(q, ((0, 0), (0, 0), (0, 0), (0, Lp - L), (0, 0)))
    kv_pad = ((0, 0), (0, 0), (n_prev * block, Lp - L), (0, 0))
    k = jnp.pad(k, kv_pad)
    v = jnp.pad(v, kv_pad)
    scale = dh ** -0.5

    def one_block(i):
        start = i * block
        qb = lax.dynamic_slice_in_dim(q, start, block, axis=3)
        kb = lax.dynamic_slice_in_dim(k, start, span, axis=2)
        vb = lax.dynamic_slice_in_dim(v, start, span, axis=2)
        q_pos = start + jnp.arange(block)
        k_pos = start - n_prev * block + jnp.arange(span)
        dist = q_pos[:, None] - k_pos[None, :]
        mask = (dist >= 0) & (dist <= window) & (k_pos >= 0)[None, :]
        s = (jnp.einsum('bngqd,bnkd->bngqk', qb, kb).astype(jnp.float32) * scale
             + bias_by_dist[:, :, jnp.clip(dist, 0, window)])
        p, lse = masked_softmax(s, mask)
        return jnp.einsum('bngqk,bnkd->bngqd', p.astype(vb.dtype), vb), lse

    o, lse = lax.map(one_block, jnp.arange(n_blocks))
    o = jnp.moveaxis(o, 0, 3).reshape(bx, n, g, Lp, dh)[:, :, :, :L]
    lse = jnp.moveaxis(lse, 0, 3).reshape(bx, n, g, Lp)[..., :L]
    return o, lse


def dilated_attention(h, w_in, q_gain, k_gain, w_out, rel_bias):
    B, S, _ = h.shape
    H, dh = A_HEADS, A_HEAD_DIM
    proj = (h @ w_in).reshape(B, S, len(A_GROUPS), 3, H, dh)
    outs, lses = [], []
    for gi, (window, dil) in enumerate(A_GROUPS):
        steps = window // dil
        L = S // dil

        def by_stride(t):
            t = t.reshape(B, L, dil, H, dh)
            return t.transpose(0, 2, 3, 1, 4).reshape(B * dil, H, L, dh)

        q = by_stride(rms_norm(proj[:, :, gi, 0], q_gain[gi]))
        k = by_stride(rms_norm(proj[:, :, gi, 1], k_gain[gi]))
        v = by_stride(proj[:, :, gi, 2])
        bias = rel_bias[rel_bucket(jnp.arange(steps + 1) * dil)].T[:, None, :]
        o, lse = banded_causal_attention(q[:, :, None], k, v, bias, steps, A_Q_BLOCK)
        outs.append(o[:, :, 0].reshape(B, dil, H, L, dh).transpose(0, 3, 1, 2, 4).reshape(B, S, H, dh))
        lses.append(lse[:, :, 0].reshape(B, dil, H, L).transpose(0, 3, 1, 2).reshape(B, S, H))
    w = jax.nn.softmax(jnp.stack(lses), axis=0)
    o = jnp.einsum('gbsh,gbshd->bshd', w, jnp.stack(outs).astype(jnp.float32))
    return o.reshape(B, S, H * dh).astype(h.dtype) @ w_out


def native_sparse_attention(h, w_in, q_gain, k_gain, cmp_pos, cmp_w1, cmp_w2, w_out, rel_bias):
    B, S, _ = h.shape
    H, N, dh = B_HEADS, B_KV_HEADS, B_HEAD_DIM
    G = H // N
    q, kv, gate = jnp.split(h @ w_in, [H * dh, H * dh + 6 * N * dh], axis=-1)
    q = rms_norm(q.reshape(B, S, N, G, dh), q_gain).transpose(0, 2, 3, 1, 4)
    kv = kv.reshape(B, S, 3, 2, N, dh).transpose(2, 3, 0, 4, 1, 5)
    gate = jax.nn.sigmoid(gate.astype(jnp.float32)).reshape(B, S, N, G, 3).transpose(4, 0, 2, 3, 1)
    tbl = rel_bias.reshape(N_BUCKETS, N, G).transpose(1, 2, 0)
    scale = dh ** -0.5

    n_cmp = (S - B_CMP_LEN) // B_CMP_STRIDE + 1
    tok = jnp.arange(n_cmp)[:, None] * B_CMP_STRIDE + jnp.arange(B_CMP_LEN)[None, :]

    def compress(t, pos, w1, w2):
        blocks = (t[:, :, tok] + pos).reshape(B, N, n_cmp, B_CMP_LEN * dh)
        return jax.nn.gelu(blocks @ w1) @ w2

    k_cmp = rms_norm(compress(kv[0, 0], cmp_pos[0], cmp_w1[0], cmp_w2[0]), k_gain[0])
    v_cmp = compress(kv[0, 1], cmp_pos[1], cmp_w1[1], cmp_w2[1]).astype(jnp.float32)
    cmp_end = jnp.arange(n_cmp) * B_CMP_STRIDE + B_CMP_LEN - 1

    n_sel = S // B_SEL_BLOCK
    top_n = min(B_TOP_N, n_sel)
    cmp_start = jnp.arange(n_cmp)[:, None] * B_CMP_STRIDE
    sel_start = jnp.arange(n_sel)[None, :] * B_SEL_BLOCK
    cmp_to_sel = ((cmp_start < sel_start + B_SEL_BLOCK)
                  & (cmp_start + B_CMP_LEN > sel_start)).astype(jnp.float32)
    k_sel = rms_norm(kv[1, 0], k_gain[1]).reshape(B, N, n_sel, B_SEL_BLOCK, dh)
    v_sel = kv[1, 1].reshape(B, N, n_sel, B_SEL_BLOCK, dh)
    b_ix = jnp.arange(B)[:, None, None, None]
    n_ix = jnp.arange(N)[None, :, None, None]
    n_ix5 = jnp.arange(N)[None, :, None, None, None]
    g_ix5 = jnp.arange(G)[None, None, :, None, None]
    blk = jnp.arange(n_sel)

    def query_block(i):
        start = i * B_Q_BLOCK
        t = start + jnp.arange(B_Q_BLOCK)
        qb = lax.dynamic_slice_in_dim(q, start, B_Q_BLOCK, axis=3)
        dist_c = t[:, None] - cmp_end[None, :]
        s_c = (jnp.einsum('bngqd,bncd->bngqc', qb, k_cmp).astype(jnp.float32) * scale
               + tbl[:, :, rel_bucket(dist_c)])
        p_c, _ = masked_softmax(s_c, dist_c >= 0)
        o_c = jnp.einsum('bngqc,bncd->bngqd', p_c, v_cmp)
        imp = jnp.einsum('bnqc,cj->bnqj', p_c.sum(axis=2), cmp_to_sel)
        cur = (t // B_SEL_BLOCK)[:, None]
        forced = (blk == 0) | (blk == cur) | (blk == cur - 1)
        imp = jnp.where(forced, FORCE_SCORE, jnp.where(blk * B_SEL_BLOCK <= t[:, None], imp, NEG_INF))
        _, sel = lax.top_k(imp, top_n)
        ks = k_sel[b_ix, n_ix, sel].reshape(B, N, B_Q_BLOCK, top_n * B_SEL_BLOCK, dh)
        vs = v_sel[b_ix, n_ix, sel].reshape(B, N, B_Q_BLOCK, top_n * B_SEL_BLOCK, dh)
        k_pos = (sel[..., None] * B_SEL_BLOCK + jnp.arange(B_SEL_BLOCK)).reshape(B, N, B_Q_BLOCK, -1)
        dist_s = t[:, None] - k_pos
        bias_s = tbl[n_ix5, g_ix5, rel_bucket(dist_s)[:, :, None]]
        s_s = jnp.einsum('bngqd,bnqkd->bngqk', qb, ks).astype(jnp.float32) * scale + bias_s
        p_s, _ = masked_softmax(s_s, (dist_s >= 0)[:, :, None])
        o_s = jnp.einsum('bngqk,bnqkd->bngqd', p_s, vs.astype(jnp.float32))
        return o_c, o_s

    o_c, o_s = lax.map(query_block, jnp.arange(S // B_Q_BLOCK))

    def unblock(o):
        return jnp.moveaxis(o, 0, 3).reshape(B, N, G, S, dh)

    k_win = rms_norm(kv[2, 0], k_gain[2])
    bias_w = tbl[:, :, rel_bucket(jnp.arange(B_WINDOW))]
    o_w, _ = banded_causal_attention(q, k_win, kv[2, 1], bias_w, B_WINDOW - 1, B_WIN_Q_BLOCK)

    o = (gate[0][..., None] * unblock(o_c) + gate[1][..., None] * unblock(o_s)
         + gate[2][..., None] * o_w.astype(jnp.float32))
    o = o.transpose(0, 3, 1, 2, 4).reshape(B, S, H * dh).astype(h.dtype)
    return o @ w_out


def causal_depthwise_conv(x, w):
    width, ch = w.shape
    return lax.conv_general_dilated(x, w[:, None, :].astype(x.dtype), window_strides=(1,),
                                    padding=[(width - 1, 0)], dimension_numbers=('NWC', 'WIO', 'NWC'),
                                    feature_group_count=ch)


def chunk_gated_delta_rule(q, k, v, g, beta):
    B, S, H, dk = q.shape
    dv = v.shape[-1]
    C = C_CHUNK
    nc = S // C

    def chunks(t):
        return jnp.moveaxis(t.reshape(B, nc, C, H, -1), 3, 1)

    q, k, v = chunks(q), chunks(k), chunks(v)
    beta = chunks(beta[..., None])[..., 0]
    g_cum = jnp.cumsum(chunks(g[..., None])[..., 0], axis=-1)
    causal = jnp.tril(jnp.ones((C, C), bool))
    strict = jnp.tril(jnp.ones((C, C), bool), -1)
    decay = jnp.exp(jnp.where(causal, g_cum[..., :, None] - g_cum[..., None, :], NEG_INF))
    k_beta = k * beta[..., None]
    lower = jnp.where(strict, jnp.einsum('bhnid,bhnjd->bhnij', k_beta, k) * decay, 0.0)
    eye = jnp.eye(C, dtype=jnp.float32)
    t_mat = lax.linalg.triangular_solve(eye + lower, jnp.broadcast_to(eye, lower.shape),
                                        left_side=True, lower=True, unit_diagonal=True)
    u = t_mat @ (v * beta[..., None])
    w = t_mat @ (k_beta * jnp.exp(g_cum)[..., None])
    attn = jnp.where(causal, jnp.einsum('bhnid,bhnjd->bhnij', q, k), 0.0) * decay

    def step(state, xs):
        q_i, k_i, u_i, w_i, a_i, g_i = xs
        v_new = u_i - w_i @ state
        o_i = (q_i * jnp.exp(g_i)[..., None]) @ state + a_i @ v_new
        g_last = g_i[..., -1:]
        state = (state * jnp.exp(g_last)[..., None]
                 + jnp.einsum('bhck,bhcv->bhkv', k_i * jnp.exp(g_last - g_i)[..., None], v_new))
        return state, o_i

    xs = tuple(jnp.moveaxis(t, 2, 0) for t in (q, k, u, w, attn, g_cum))
    _, o = lax.scan(step, jnp.zeros((B, H, dk, dv), jnp.float32), xs)
    return o.transpose(1, 0, 3, 2, 4).reshape(B, S, H, dv)


def gated_deltanet(h, w_in, conv_w, a_log, dt_bias, out_gain, w_out):
    B, S, _ = h.shape
    H, dk = C_HEADS, C_HEAD_DIM
    qkv, z, beta, a = jnp.split(h @ w_in, [3 * C_WIDTH, 4 * C_WIDTH, 4 * C_WIDTH + H], axis=-1)
    qkv = jax.nn.silu(causal_depthwise_conv(qkv, conv_w)).astype(jnp.float32)
    q, k, v = (t.reshape(B, S, H, dk) for t in jnp.split(qkv, 3, axis=-1))
    q = l2_normalize(q) * dk ** -0.5
    k = l2_normalize(k)
    beta = jax.nn.sigmoid(beta.astype(jnp.float32))
    g = -jnp.exp(a_log.astype(jnp.float32)) * jax.nn.softplus(a.astype(jnp.float32) + dt_bias.astype(jnp.float32))
    o = chunk_gated_delta_rule(q, k, v, g, beta)
    o = rms_norm(o, out_gain) * jax.nn.silu(z.astype(jnp.float32).reshape(B, S, H, dk))
    return o.reshape(B, S, C_WIDTH).astype(h.dtype) @ w_out


def swiglu(h, w_gate, w_up, w_down):
    return (jax.nn.silu(h @ w_gate) * (h @ w_up)) @ w_down


def setup_inputs(seed: int = 0) -> dict:
    key = jax.random.key(seed)
    keys = iter(jax.random.split(key, 96))

    def normal(shape, std):
        return std * jax.random.normal(next(keys), shape, jnp.float32)

    def dense(fan_in, shape):
        return normal(shape, fan_in ** -0.5)

    def gain(shape):
        return 1.0 + normal(shape, 0.05)

    p = {'x': normal((BATCH, SEQ, D_MODEL), 1.0),
         'rel_bias': normal((N_BUCKETS, N_BIAS_HEADS), 0.5)}
    for layer in range(DEPTH):
        pre = f'l{layer}_'
        p[pre + 'norm1'] = gain((D_MODEL,))
        kind = layer % N_MIXERS
        if kind == 0:
            p[pre + 'a_w_in'] = dense(D_MODEL, (D_MODEL, A_IN_WIDTH))
            p[pre + 'a_q_gain'] = gain((len(A_GROUPS), A_HEAD_DIM))
            p[pre + 'a_k_gain'] = gain((len(A_GROUPS), A_HEAD_DIM))
            p[pre + 'a_w_out'] = dense(A_HEADS * A_HEAD_DIM, (A_HEADS * A_HEAD_DIM, D_MODEL))
        elif kind == 1:
            p[pre + 'b_w_in'] = dense(D_MODEL, (D_MODEL, B_IN_WIDTH))
            p[pre + 'b_q_gain'] = gain((B_HEAD_DIM,))
            p[pre + 'b_k_gain'] = gain((3, B_HEAD_DIM))
            p[pre + 'b_cmp_pos'] = normal((2, B_CMP_LEN, B_HEAD_DIM), 0.02)
            p[pre + 'b_cmp_w1'] = dense(B_CMP_LEN * B_HEAD_DIM, (2, B_CMP_LEN * B_HEAD_DIM, B_CMP_HIDDEN))
            p[pre + 'b_cmp_w2'] = dense(B_CMP_HIDDEN, (2, B_CMP_HIDDEN, B_HEAD_DIM))
            p[pre + 'b_w_out'] = dense(B_HEADS * B_HEAD_DIM, (B_HEADS * B_HEAD_DIM, D_MODEL))
        else:
            p[pre + 'c_w_in'] = dense(D_MODEL, (D_MODEL, C_IN_WIDTH))
            p[pre + 'c_conv_w'] = dense(C_CONV, (C_CONV, 3 * C_WIDTH))
            p[pre + 'c_a_log'] = jnp.log(jax.random.uniform(next(keys), (C_HEADS,), jnp.float32, 1.0, 16.0))
            dt = jnp.exp(jax.random.uniform(next(keys), (C_HEADS,), jnp.float32,
                                            math.log(1e-3), math.log(1e-1)))
            p[pre + 'c_dt_bias'] = dt + jnp.log(-jnp.expm1(-dt))
            p[pre + 'c_out_gain'] = gain((C_HEAD_DIM,))
            p[pre + 'c_w_out'] = dense(C_WIDTH, (C_WIDTH, D_MODEL))
        p[pre + 'norm2'] = gain((D_MODEL,))
        p[pre + 'ffn_w_gate'] = dense(D_MODEL, (D_MODEL, FFN_HIDDEN))
        p[pre + 'ffn_w_up'] = dense(D_MODEL, (D_MODEL, FFN_HIDDEN))
        p[pre + 'ffn_w_down'] = dense(FFN_HIDDEN, (FFN_HIDDEN, D_MODEL))
    return p


def reference(x, rel_bias,
              l0_norm1, l0_a_w_in, l0_a_q_gain, l0_a_k_gain, l0_a_w_out,
              l0_norm2, l0_ffn_w_gate, l0_ffn_w_up, l0_ffn_w_down,
              l1_norm1, l1_b_w_in, l1_b_q_gain, l1_b_k_gain, l1_b_cmp_pos, l1_b_cmp_w1, l1_b_cmp_w2, l1_b_w_out,
              l1_norm2, l1_ffn_w_gate, l1_ffn_w_up, l1_ffn_w_down,
              l2_norm1, l2_c_w_in, l2_c_conv_w, l2_c_a_log, l2_c_dt_bias, l2_c_out_gain, l2_c_w_out,
              l2_norm2, l2_ffn_w_gate, l2_ffn_w_up, l2_ffn_w_down,
              l3_norm1, l3_a_w_in, l3_a_q_gain, l3_a_k_gain, l3_a_w_out,
              l3_norm2, l3_ffn_w_gate, l3_ffn_w_up, l3_ffn_w_down):
    layers = (
        (l0_norm1, (l0_a_w_in, l0_a_q_gain, l0_a_k_gain, l0_a_w_out),
         l0_norm2, (l0_ffn_w_gate, l0_ffn_w_up, l0_ffn_w_down)),
        (l1_norm1, (l1_b_w_in, l1_b_q_gain, l1_b_k_gain, l1_b_cmp_pos, l1_b_cmp_w1, l1_b_cmp_w2, l1_b_w_out),
         l1_norm2, (l1_ffn_w_gate, l1_ffn_w_up, l1_ffn_w_down)),
        (l2_norm1, (l2_c_w_in, l2_c_conv_w, l2_c_a_log, l2_c_dt_bias, l2_c_out_gain, l2_c_w_out),
         l2_norm2, (l2_ffn_w_gate, l2_ffn_w_up, l2_ffn_w_down)),
        (l3_norm1, (l3_a_w_in, l3_a_q_gain, l3_a_k_gain, l3_a_w_out),
         l3_norm2, (l3_ffn_w_gate, l3_ffn_w_up, l3_ffn_w_down)),
    )
    for layer in range(DEPTH):
        norm1, mixer_args, norm2, ffn_args = layers[layer]
        kind = layer % N_MIXERS
        h = rms_norm(x, norm1)
        if kind == 0:
            y = dilated_attention(h, *mixer_args, rel_bias)
        elif kind == 1:
            y = native_sparse_attention(h, *mixer_args, rel_bias)
        else:
            y = gated_deltanet(h, *mixer_args)
        x = x + y
        x = x + swiglu(rms_norm(x, norm2), *ffn_args)
    return x
```

```python
import os
import numpy as np
from contextlib import ExitStack
import concourse.bass as bass
import concourse.mybir as mybir
from concourse.ap import AP
from concourse.bass_utils import run_bass_kernel_spmd

F32, BF16 = mybir.dt.float32, mybir.dt.bfloat16
ALU, AF, AX = mybir.AluOpType, mybir.ActivationFunctionType, mybir.AxisListType

S = 4096
D = 1024
NT = S // 128
FF = 2816
NDMA = 32
EPS = 1e-6
PHASE_LIMIT = 20000


class Sched:
    CE = ('pe', 'act', 'dve', 'pool')

    def __init__(self, nc, st):
        self.nc = nc
        self.e = {'pe': nc.tensor, 'act': nc.scalar, 'dve': nc.vector, 'pool': nc.gpsimd, 'sp': nc.sync}
        self.sets = []
        for s in range(2):
            d = {k: st.enter_context(nc.semaphore(f"s{s}_{k}")) for k in self.CE}
            self.sets.append(d)
        self.dsem = {('d', i): st.enter_context(nc.semaphore(f"dq{i}")) for i in range(NDMA)}
        self.dcnt = [0] * NDMA
        self.skey = {}
        self.spsem = st.enter_context(nc.semaphore('spg'))
        self.phase_no = 0
        self.cur = 0
        self._reset()

    def _reset(self):
        self.cnt = {k: 0 for k in self.CE}
        self.lastw = {}
        self.readers = {}
        self.seen = {k: {} for k in self.CE + ('sp',)}

    def _sem(self, k):
        return self.dsem[k] if isinstance(k, tuple) else self.sets[self.cur][k]

    def _wait(self, eng, reads, writes):
        need = {}
        for r in reads:
            t = self.lastw.get(r)
            if t:
                need[t[0]] = max(need.get(t[0], 0), t[1])
        for w in writes:
            t = self.lastw.get(w)
            if t:
                need[t[0]] = max(need.get(t[0], 0), t[1])
            for k, v in self.readers.get(w, {}).items():
                need[k] = max(need.get(k, 0), v)
        for k, v in need.items():
            if k == 'pe' and eng == 'pe':
                continue
            if isinstance(k, tuple):
                v = 16 * self.dcnt[k[1]]
            if self.seen[eng].get(k, 0) < v:
                self.e[eng].wait_ge(self._sem(k), v)
                self.seen[eng][k] = v

    def _commit(self, tok, reads, writes):
        for r in reads:
            d = self.readers.setdefault(r, {})
            d[tok[0]] = max(d.get(tok[0], 0), tok[1])
        for w in writes:
            self.lastw[w] = tok
            self.readers[w] = {}

    def op(self, eng, fn, reads=(), writes=()):
        self._wait(eng, reads, writes)
        self.cnt[eng] += 1
        fn(self.e[eng]).then_inc(self.sets[self.cur][eng], 1)
        self._commit((eng, self.cnt[eng]), reads, writes)

    def dma(self, q, out, in_, reads=(), writes=(), stream=None, **kw):
        key = reads[0] if (reads and (not writes or not isinstance(writes[0], tuple)) and isinstance(reads[0], tuple)) else (writes[0] if writes else reads[0])
        if key not in self.skey:
            self.skey[key] = len(self.skey) % NDMA
        stream = self.skey[key]
        self._wait(q, reads, writes)
        self.dcnt[stream] += 1
        self.e[q].dma_start(out=out, in_=in_, **kw).then_inc(self.dsem[('d', stream)], 16)
        self._commit((('d', stream), 16 * self.dcnt[stream]), reads, writes)

    def maybe_phase(self):
        if max(self.cnt.values()) > PHASE_LIMIT:
            self.phase()

    def phase(self):
        A = self.sets[self.cur]
        B = self.sets[1 - self.cur]
        sp = self.e['sp']
        for k in self.CE:
            if self.cnt[k]:
                sp.wait_ge(A[k], self.cnt[k])
        for i, c in enumerate(self.dcnt):
            if c:
                sp.wait_ge(self.dsem[('d', i)], 16 * c)
        for k in self.CE:
            sp.sem_clear(B[k])
        self.phase_no += 1
        sp.sem_inc(self.spsem, 1)
        for k in self.CE:
            self.e[k].wait_ge(self.spsem, self.phase_no)
        self.cur = 1 - self.cur
        self._reset()


def bc(ap, shape):
    return ap.to_broadcast(list(shape))


POOL_ELEMS = 15200000


class DT:
    def __init__(self, h, off, shape, dt, nf):
        self.h, self.off, self.shape, self.dt, self.nf = h, off, shape, dt, nf

    def ap(self):
        flat = AP(self.h, self.off, [[1, self.nf]])
        if self.dt != F32:
            flat = flat.bitcast(self.dt)
        names = [f"d{i}" for i in range(len(self.shape))]
        return flat.rearrange("(" + " ".join(names) + ") -> " + " ".join(names), **{nm: sz for nm, sz in zip(names[1:], self.shape[1:])})


class K:
    def __init__(self, nc, st, sc):
        self.nc, self.st, self.sc = nc, st, sc
        self.ps = [st.enter_context(nc.psum_tensor(f"ps{i}", [128, 512], F32)) for i in range(7)]
        self.psb = st.enter_context(nc.psum_tensor("psb", [128, 1024], BF16))
        self.n_sb = 0
        self.pool = nc.dram_tensor('pool', [POOL_ELEMS], F32, kind='Internal')
        self.doff = 0

    def sb(self, shape, dt, name=None):
        self.n_sb += 1
        return self.st.enter_context(self.nc.sbuf_tensor(name or f"t{self.n_sb}", list(shape), dt))

    def dram(self, shape, dt=F32):
        n = int(np.prod(shape))
        nf = n if dt == F32 else (n + 1) // 2
        d = DT(self.pool, self.doff, list(shape), dt, nf)
        self.doff += (nf + 63) // 64 * 64
        assert self.doff <= POOL_ELEMS, self.doff
        return d


def load_w_chunk(k, W, col0, ncols, wst, wbf, gainT, rs_st, rs_bf, kc_n=8, stream=1):
    sc = k.sc
    src = W.rearrange("(kc p) n -> p kc n", p=128)[:, :, col0:col0 + ncols]
    sc.dma('sp', wst[:, 0:kc_n, 0:ncols], src, writes=[rs_st], stream=stream)
    if gainT is None:
        sc.op('dve', lambda e: e.tensor_copy(out=wbf[:, 0:kc_n, 0:ncols], in_=wst[:, 0:kc_n, 0:ncols]),
              reads=[rs_st], writes=[rs_bf])
    else:
        sc.op('dve', lambda e: e.tensor_tensor(out=wbf[:, 0:kc_n, 0:ncols], in0=wst[:, 0:kc_n, 0:ncols],
                                               in1=bc(gainT[:, 0:kc_n].unsqueeze(2), [128, kc_n, ncols]), op=ALU.mult),
              reads=[rs_st, 'gain'], writes=[rs_bf])


def load_gainT(k, g_dram, gainT):
    k.sc.dma('sp', gainT[:, :], g_dram.rearrange("(kc p) -> p kc", p=128), writes=['gain'], stream=2,
             allow_slow_non_contiguous=True)


def norm_tiles(k, x_d, hT, tiles, tok_of_tile, C):
    sc = k.sc
    for slot, n in enumerate(tiles):
        b = slot % 2
        xt, sq, ss, xb = C['xt'][b], C['sq'][b], C['ss'][b], C['xb'][b]
        sc.dma('sp', xt[:, :], x_d[n * 128:(n + 1) * 128, :], writes=[('xt', b)], stream=0)
        sc.op('act', lambda e: e.activation(out=sq[:, :], in_=xt[:, :], func=AF.Square), reads=[('xt', b)], writes=[('sq', b)])
        sc.op('dve', lambda e: e.reduce_sum(out=ss[:, 0:1], in_=sq[:, :], axis=AX.X), reads=[('sq', b)], writes=[('ss', b)])
        sc.op('dve', lambda e: e.tensor_scalar(out=ss[:, 0:1], in0=ss[:, 0:1], scalar1=1.0 / D, scalar2=EPS, op0=ALU.mult, op1=ALU.add),
              reads=[('ss', b)], writes=[('ss', b)])
        sc.op('act', lambda e: e.activation(out=ss[:, 1:2], in_=ss[:, 0:1], func=AF.Sqrt), reads=[('ss', b)], writes=[('ss2', b)])
        sc.op('dve', lambda e: e.reciprocal(out=ss[:, 2:3], in_=ss[:, 1:2]), reads=[('ss2', b)], writes=[('ss3', b)])
        sc.op('act', lambda e: e.activation(out=xb[:, :], in_=xt[:, :], func=AF.Copy, scale=ss[:, 2:3]),
              reads=[('xt', b), ('ss3', b)], writes=[('xb', b)])
        for kc in range(8):
            sc.op('pe', lambda e, kc=kc: e.transpose(out=k.psb[:, kc * 128:(kc + 1) * 128], in_=xb[:, kc * 128:(kc + 1) * 128],
                                                      identity=C['identb'][:, :]),
                  reads=[('xb', b), 'const'], writes=['psb'])
        sc.op('dve', lambda e: e.tensor_copy(out=hT[:, :, slot * 128:(slot + 1) * 128],
                                             in_=k.psb[:, :].rearrange("p (kc t) -> p kc t", kc=8)),
              reads=['psb'], writes=[('hT', slot)])


def ffn_layer(k, x_d, xo_d, norm2, wg, wu, wd, C):
    sc = k.sc
    st2 = ExitStack()
    old = k.st
    k.st = st2
    hT = k.sb([128, 8, 1024], BF16)
    actT = k.sb([128, 22, 1024], BF16)
    wdb = k.sb([128, 22, 1024], BF16)
    C['sg'] = [k.sb([128, 512], F32) for _ in range(2)]
    load_gainT(k, norm2, C['gainT'])
    for hc in range(22):
        b = hc % 2
        sc.dma('sp', C['wst'][b][:, 0:8, :].rearrange("p a b -> p (a b)"), wd[hc * 128:(hc + 1) * 128, :], writes=[('wst', b)], stream=1)
        sc.op('act', lambda e, hc=hc, b=b: e.activation(out=wdb[:, hc, :], in_=C['wst'][b][:, 0:8, :].rearrange("p a b -> p (a b)"), func=AF.Copy),
              reads=[('wst', b)], writes=['wdb'])
    for tb in range(4):
        norm_tiles(k, x_d, hT, range(tb * 8, tb * 8 + 8), None, C)
        hres = [('hT', s) for s in range(8)]
        for hc in range(22):
            b = hc % 2
            load_w_chunk(k, wg, hc * 128, 128, C['wst'][b], C['wbf'][b], C['gainT'], ('wst', b), ('wbf', b))
            load_w_chunk(k, wu, hc * 128, 128, C['wst2'][b], C['wbf2'][b], C['gainT'], ('wst2', b), ('wbf2', b))
            for sb in range(2):
                pg, pu = k.ps[sb * 2], k.ps[sb * 2 + 1]
                for kc in range(8):
                    sc.op('pe', lambda e, kc=kc: e.matmul(pg[:, :], lhsT=C['wbf'][b][:, kc, :], rhs=hT[:, kc, sb * 512:(sb + 1) * 512],
                                                         start=(kc == 0), stop=(kc == 7)),
                          reads=[('wbf', b)] + hres[sb * 4:sb * 4 + 4], writes=[('ps', sb * 2)])
                for kc in range(8):
                    sc.op('pe', lambda e, kc=kc: e.matmul(pu[:, :], lhsT=C['wbf2'][b][:, kc, :], rhs=hT[:, kc, sb * 512:(sb + 1) * 512],
                                                         start=(kc == 0), stop=(kc == 7)),
                          reads=[('wbf2', b)] + hres[sb * 4:sb * 4 + 4], writes=[('ps', sb * 2 + 1)])
                sg = C['sg'][sb]
                sc.op('act', lambda e: e.activation(out=sg[:, :], in_=pg[:, :], func=AF.Silu), reads=[('ps', sb * 2)], writes=[('sg', sb)])
                sc.op('dve', lambda e: e.tensor_tensor(out=actT[:, hc, sb * 512:(sb + 1) * 512], in0=pu[:, :], in1=sg[:, :], op=ALU.mult),
                      reads=[('ps', sb * 2 + 1), ('sg', sb)], writes=[('actT', sb)])
        for tt in range(8):
            n = tb * 8 + tt
            b = tt % 2
            xt = C['xt'][b]
            sc.dma('sp', xt[:, :], x_d[n * 128:(n + 1) * 128, :], writes=[('xt', b)], stream=0)
            for half in range(2):
                po = k.ps[4 + half]
                for hc in range(22):
                    sc.op('pe', lambda e, hc=hc: e.matmul(po[:, :], lhsT=actT[:, hc, tt * 128:(tt + 1) * 128], rhs=wdb[:, hc, half * 512:(half + 1) * 512],
                                                         start=(hc == 0), stop=(hc == 21)),
                          reads=[('actT', tt // 4), 'wdb'], writes=[('ps', 4 + half)])
                sc.op('dve', lambda e: e.tensor_tensor(out=xt[:, half * 512:(half + 1) * 512], in0=po[:, :], in1=xt[:, half * 512:(half + 1) * 512], op=ALU.add),
                      reads=[('ps', 4 + half), ('xt', b)], writes=[('xt', b)])
            sc.dma('pool', xo_d[n * 128:(n + 1) * 128, :], xt[:, :], reads=[('xt', b)], writes=[], stream=3)
        sc.maybe_phase()
    sc.phase()
    k.st = old
    st2.close()


A_GROUPS = ((128, 1), (512, 4), (2048, 16))


def rel_bucket_np(dist):
    dist = np.maximum(dist, 0)
    d_f = np.maximum(dist, 1).astype(np.float32)
    large = 16 + (np.log(d_f / np.float32(16)) / np.float32(np.log(2048 / 16)) * np.float32(16)).astype(np.int32)
    return np.where(dist < 16, dist, np.minimum(large, 31))


def onehot_rows(dists, valid):
    n = len(dists)
    oh = np.zeros((33, n), np.float32)
    b = rel_bucket_np(np.asarray(dists))
    for i in range(n):
        oh[b[i] if valid[i] else 32, i] = 1.0
    return oh


def build_bias_rows(k, C, oh_d, width, tab_d):
    sc = k.sc
    u = 0
    for c0 in range(0, width, 512):
        n = min(512, width - c0)
        sc.dma('sp', C['oh'][:, 0:n], oh_d[:, c0:c0 + n], writes=['oh'], stream=2)
        for h in range(16):
            b = u % 2
            u += 1
            sc.op('pe', lambda e, h=h, b=b: e.matmul(k.ps[b][:, 0:n], lhsT=C['relrep'][:, h, :], rhs=C['oh'][:, 0:n], start=True, stop=True),
                  reads=['oh', 'relrep'], writes=[('ps', b)])
            sc.op('act', lambda e, b=b: e.activation(out=C['rowst'][b][:, 0:n], in_=k.ps[b][:, 0:n], func=AF.Exp), reads=[('ps', b)], writes=[('rowst', b)])
            sc.dma('pool', tab_d.ap()[h, :, c0:c0 + n], C['rowst'][b][:, 0:n], reads=[('rowst', b)], writes=['tab'], stream=4)


def skew_tab(tab_d, h, W, off, pstride, n):
    return AP(tab_d.h, tab_d.off + h * 128 * W + off, [[W - pstride, 128], [1, n]])


def mixer_a(k, x_d, xo_d, P, C, tabA):
    sc = k.sc
    st2 = ExitStack()
    old = k.st
    k.st = st2
    k.doff = k.dbase
    hT = k.sb([128, 8, S], BF16)
    qT = k.sb([128, S], BF16)
    kT = k.sb([128, S], BF16)
    V1 = k.sb([128, NT, 2, 65], BF16)
    OZ = k.sb([128, NT, 2, 65], F32)
    wob = k.sb([128, 8, D], BF16)
    gcol = k.sb([128, 6], F32)
    r1 = [k.sb([128, 512], F32) for _ in range(2)]
    sqb = [k.sb([128, 512], BF16) for _ in range(2)]
    Eb = [k.sb([128, 256], F32) for _ in range(2)]
    PT = [k.sb([128, 256], BF16) for _ in range(3)]
    tab = [k.sb([128, 256], F32) for _ in range(2)]
    mrg = [k.sb([128, 16, 65], F32) for _ in range(3)]
    rz = k.sb([128, 16], F32)
    ob = k.sb([128, 16, 64], BF16)
    oT = k.sb([128, 8, 128], BF16)
    ozd = [k.dram([S, 16 * 65]) for _ in range(3)]
    load_gainT(k, P['norm1'], C['gainT'])
    for half in range(2):
        sc.dma('sp', gcol[half * 64:(half + 1) * 64, 0:3], P['a_q_gain'].rearrange("g d -> d g"), writes=['gcol'], stream=2, allow_slow_non_contiguous=True)
        sc.dma('sp', gcol[half * 64:(half + 1) * 64, 3:6], P['a_k_gain'].rearrange("g d -> d g"), writes=['gcol'], stream=2, allow_slow_non_contiguous=True)
    sc.op('pool', lambda e: e.memset(V1[:, :, :, 64:65], 1.0), writes=['V1ones'])
    for kc in range(8):
        b = kc % 2
        wflat = C['wst'][b][:, :, :].rearrange("p a b -> p (a b)")
        sc.dma('sp', wflat, P['a_w_out'][kc * 128:(kc + 1) * 128, :], writes=[('wst', b)], stream=1)
        sc.op('act', lambda e, kc=kc, wflat=wflat: e.activation(out=wob[:, kc, :], in_=wflat, func=AF.Copy), reads=[('wst', b)], writes=['wob'])
    norm_tiles(k, x_d, hT, range(NT), None, C)
    hall = [('hT', s) for s in range(NT)]
    Win = P['a_w_in']
    for g, (window, dil) in enumerate(A_GROUPS):
        L = S // dil
        nb = L // 128
        cw = min(512, L)
        hTs = [hT[:, kc, :].rearrange("p (m d) -> p d m", d=dil) for kc in range(8)]
        for hp in range(8):
            cq = (g * 3 + 0) * 1024 + hp * 128
            load_w_chunk(k, Win, cq, 128, C['wst'][0], C['wbf'][0], C['gainT'], ('wst', 0), ('wbf', 0))
            load_w_chunk(k, Win, cq + 1024, 128, C['wst'][1], C['wbf'][1], C['gainT'], ('wst', 1), ('wbf', 1))
            load_w_chunk(k, Win, cq + 2048, 128, C['wst2'][0], C['wbf2'][0], C['gainT'], ('wst2', 0), ('wbf2', 0))
            for j in range(2):
                sc.dma('sp', tab[j][:, :], skew_tab(tabA[g], hp * 2 + j, 384, 127, 1, 256), writes=[('tab', j)], stream=5)
            for which, (wb, wres, dst, gc) in enumerate(((C['wbf'][0], ('wbf', 0), qT, g), (C['wbf'][1], ('wbf', 1), kT, 3 + g))):
                dname = 'qT' if which == 0 else 'kT'
                for c in range(S // cw):
                    r, m0 = (c * cw) // L, (c * cw) % L
                    b = c % 2
                    pq, pss = k.ps[4], k.ps[5]
                    for kc in range(8):
                        sc.op('pe', lambda e, kc=kc: e.matmul(pq[:, 0:cw], lhsT=wb[:, kc, :], rhs=hTs[kc][:, r, m0:m0 + cw], start=(kc == 0), stop=(kc == 7)),
                              reads=[wres] + hall, writes=[('ps', 4)])
                    sc.op('act', lambda e: e.activation(out=sqb[b][:, 0:cw], in_=pq[:, 0:cw], func=AF.Square), reads=[('ps', 4)], writes=[('sqb', b)])
                    sc.op('pe', lambda e: e.matmul(pss[:, 0:cw], lhsT=C['blk'][:, :], rhs=sqb[b][:, 0:cw], start=True, stop=True),
                          reads=[('sqb', b), 'const'], writes=[('ps', 5)])
                    sc.op('dve', lambda e: e.tensor_scalar(out=r1[b][:, 0:cw], in0=pss[:, 0:cw], scalar1=1.0 / 64, scalar2=EPS, op0=ALU.mult, op1=ALU.add),
                          reads=[('ps', 5)], writes=[('r1', b)])
                    sc.op('act', lambda e: e.activation(out=r1[b][:, 0:cw], in_=r1[b][:, 0:cw], func=AF.Sqrt), reads=[('r1', b)], writes=[('r1', b)])
                    sc.op('dve', lambda e: e.reciprocal(out=r1[b][:, 0:cw], in_=r1[b][:, 0:cw]), reads=[('r1', b)], writes=[('r1', b)])
                    tl = [(dname, t) for t in range(c * cw // 128, (c + 1) * cw // 128)]
                    sc.op('dve', lambda e: e.scalar_tensor_tensor(out=dst[:, c * cw:(c + 1) * cw], in0=pq[:, 0:cw], scalar=gcol[:, gc:gc + 1], in1=r1[b][:, 0:cw],
                                                                   op0=ALU.mult, op1=ALU.mult),
                          reads=[('ps', 4), ('r1', b), 'gcol'], writes=tl)
            for n in range(NT):
                r, m0 = (n * 128) // L, (n * 128) % L
                pv = k.ps[6]
                for kc in range(8):
                    sc.op('pe', lambda e, kc=kc: e.matmul(pv[:, 0:128], lhsT=hTs[kc][:, r, m0:m0 + 128], rhs=C['wbf2'][0][:, kc, :], start=(kc == 0), stop=(kc == 7)),
                          reads=[('wbf2', 0)] + hall, writes=[('ps', 6)])
                sc.op('act', lambda e: e.activation(out=V1[:, n, :, 0:64], in_=pv[:, 0:128].rearrange("p (h d) -> p h d", h=2), func=AF.Copy),
                      reads=[('ps', 6)], writes=[('V1', n)])
            u = 0
            for h2 in range(2):
                ps_ = slice(64 * h2, 64 * h2 + 64)
                for r in range(dil):
                    for j in range(nb):
                        n = r * nb + j
                        nq = 256 if j < nb - 1 else 128
                        pS, pO = k.ps[u % 2], k.ps[2 + u % 2]
                        eb, pt, ptp = Eb[u % 2], PT[u % 3], PT[(u - 1) % 3]
                        sc.op('pe', lambda e: e.matmul(pS[:, 0:nq], lhsT=kT[ps_, n * 128:(n + 1) * 128], rhs=qT[ps_, n * 128:n * 128 + nq], start=True, stop=True),
                              reads=[('kT', n), ('qT', n)] + ([('qT', n + 1)] if nq == 256 else []), writes=[('ps', u % 2)])
                        sc.op('act', lambda e: e.activation(out=eb[:, 0:nq], in_=pS[:, 0:nq], func=AF.Exp, scale=0.125), reads=[('ps', u % 2)], writes=[('Eb', u % 2)])
                        sc.op('dve', lambda e: e.tensor_tensor(out=pt[:, 0:nq], in0=eb[:, 0:nq], in1=tab[h2][:, 0:nq], op=ALU.mult),
                              reads=[('Eb', u % 2), ('tab', h2)], writes=[('PT', u % 3)])
                        if j > 0:
                            sc.op('pe', lambda e: e.matmul(pO[:, 0:65], lhsT=ptp[:, 128:256], rhs=V1[:, n - 1, h2, :], start=True, stop=False),
                                  reads=[('PT', (u - 1) % 3), ('V1', n - 1), 'V1ones'], writes=[('ps', 2 + u % 2)])
                        sc.op('pe', lambda e: e.matmul(pO[:, 0:65], lhsT=pt[:, 0:128], rhs=V1[:, n, h2, :], start=(j == 0), stop=True),
                              reads=[('PT', u % 3), ('V1', n), 'V1ones'], writes=[('ps', 2 + u % 2)])
                        sc.op('act', lambda e: e.activation(out=OZ[:, n, h2, :], in_=pO[:, 0:65], func=AF.Copy), reads=[('ps', 2 + u % 2)], writes=[('OZ', n)])
                        u += 1
            ozv = ozd[g].ap().rearrange("(m d) c -> d m c", d=dil)
            for n in range(NT):
                r, j = n // nb, n % nb
                sc.dma('pool', ozv[r, j * 128:(j + 1) * 128, hp * 130:(hp + 1) * 130], OZ[:, n, :, :].rearrange("p h c -> p (h c)"),
                       reads=[('OZ', n)], writes=['ozd'], stream=6)
            sc.maybe_phase()
    sc.phase()
    for n in range(NT):
        b = n % 2
        xt = C['xt'][b]
        sc.dma('sp', xt[:, :], x_d[n * 128:(n + 1) * 128, :], writes=[('xt', b)], stream=0)
        for g in range(3):
            sc.dma('sp', mrg[g][:, :, :].rearrange("p h c -> p (h c)"), ozd[g].ap()[n * 128:(n + 1) * 128, :], writes=[('mrg', g)], stream=7)
        sc.op('dve', lambda e: e.tensor_tensor(out=mrg[0][:, :, :], in0=mrg[0][:, :, :], in1=mrg[1][:, :, :], op=ALU.add), reads=[('mrg', 0), ('mrg', 1)], writes=[('mrg', 0)])
        sc.op('dve', lambda e: e.tensor_tensor(out=mrg[0][:, :, :], in0=mrg[0][:, :, :], in1=mrg[2][:, :, :], op=ALU.add), reads=[('mrg', 0), ('mrg', 2)], writes=[('mrg', 0)])
        sc.op('dve', lambda e: e.reciprocal(out=rz[:, :], in_=mrg[0][:, :, 64]), reads=[('mrg', 0)], writes=['rz'])
        sc.op('dve', lambda e: e.tensor_tensor(out=ob[:, :, :], in0=mrg[0][:, :, 0:64], in1=bc(rz[:, :].unsqueeze(2), [128, 16, 64]), op=ALU.mult),
              reads=[('mrg', 0), 'rz'], writes=['ob'])
        obf = ob[:, :, :].rearrange("p h d -> p (h d)")
        for kc in range(8):
            sc.op('pe', lambda e, kc=kc: e.transpose(out=k.psb[:, kc * 128:(kc + 1) * 128], in_=obf[:, kc * 128:(kc + 1) * 128], identity=C['identb'][:, :]),
                  reads=['ob', 'const'], writes=['psb'])
        sc.op('act', lambda e: e.activation(out=oT[:, :, :], in_=k.psb[:, :].rearrange("p (kc t) -> p kc t", kc=8), func=AF.Copy), reads=['psb'], writes=['oT'])
        for half in range(2):
            po = k.ps[4 + half]
            for kc in range(8):
                sc.op('pe', lambda e, kc=kc: e.matmul(po[:, :], lhsT=oT[:, kc, :], rhs=wob[:, kc, half * 512:(half + 1) * 512], start=(kc == 0), stop=(kc == 7)),
                      reads=['oT', 'wob'], writes=[('ps', 4 + half)])
            sc.op('dve', lambda e: e.tensor_tensor(out=xt[:, half * 512:(half + 1) * 512], in0=po[:, :], in1=xt[:, half * 512:(half + 1) * 512], op=ALU.add),
                  reads=[('ps', 4 + half), ('xt', b)], writes=[('xt', b)])
        sc.dma('pool', xo_d[n * 128:(n + 1) * 128, :], xt[:, :], reads=[('xt', b)], stream=3)
    sc.phase()
    k.st = old
    st2.close()


def out_tail(k, C, obf, wob, oT, x_d, xo_d, n, obres):
    sc = k.sc
    b = n % 2
    xt = C['xt'][b]
    sc.dma('sp', xt[:, :], x_d[n * 128:(n + 1) * 128, :], writes=[('xt', b)], stream=0)
    for kc in range(8):
        sc.op('pe', lambda e, kc=kc: e.transpose(out=k.psb[:, kc * 128:(kc + 1) * 128], in_=obf[:, kc * 128:(kc + 1) * 128], identity=C['identb'][:, :]),
              reads=[obres, 'const'], writes=['psb'])
    sc.op('act', lambda e: e.activation(out=oT[:, :, :], in_=k.psb[:, :].rearrange("p (kc t) -> p kc t", kc=8), func=AF.Copy), reads=['psb'], writes=['oT'])
    sc.op('act', lambda e: e.activation(out=C['ss'][0][:, 3:4], in_=C['ss'][1][:, 3:4], func=AF.Copy), reads=['oT'], writes=['oT'])
    for half in range(2):
        pb_ = int(os.environ.get('KTAILB', 4)) + half
        po = k.ps[pb_]
        for kc in range(8):
            sc.op('pe', lambda e, kc=kc: e.matmul(po[:, :], lhsT=oT[:, kc, :], rhs=wob[:, kc, half * 512:(half + 1) * 512], start=(kc == 0), stop=(kc == 7)),
                  reads=['oT', 'wob'], writes=[('ps', pb_)])
        sc.op('dve', lambda e: e.tensor_tensor(out=xt[:, half * 512:(half + 1) * 512], in0=po[:, :], in1=xt[:, half * 512:(half + 1) * 512], op=ALU.add),
              reads=[('ps', pb_), ('xt', b)], writes=[('xt', b)])
    sc.dma('pool', xo_d[n * 128:(n + 1) * 128, :], xt[:, :], reads=[('xt', b)], stream=3)


def load_wob(k, C, w_out, wob):
    sc = k.sc
    for kc in range(8):
        b = kc % 2
        wflat = C['wst'][b][:, :, :].rearrange("p a b -> p (a b)")
        sc.dma('sp', wflat, w_out[kc * 128:(kc + 1) * 128, :], writes=[('wst', b)], stream=1)
        sc.op('act', lambda e, kc=kc, wflat=wflat: e.activation(out=wob[:, kc, :], in_=wflat, func=AF.Copy), reads=[('wst', b)], writes=['wob'])


def copy_x(k, x_d, xo_d, C):
    for n in range(NT):
        b = n % 2
        k.sc.dma('sp', C['xt'][b][:, :], x_d[n * 128:(n + 1) * 128, :], writes=[('xt', b)], stream=0)
        k.sc.dma('pool', xo_d[n * 128:(n + 1) * 128, :], C['xt'][b][:, :], reads=[('xt', b)], stream=3)
    k.sc.phase()


STOP = os.environ.get('KSTOP', '')
KBR = os.environ.get('KBR', 'csw')


def mixer_c(k, x_d, xo_d, P, C, cd):
    sc = k.sc
    nc = k.nc
    st2 = ExitStack()
    old = k.st
    k.st = st2
    k.doff = k.dbase
    qTd, kTd = k.dram([8, 128, S], BF16), k.dram([8, 128, S], BF16)
    Kd, Vd = k.dram([S, 8, 128], BF16), k.dram([S, 8, 128], BF16)
    zd = k.dram([S, D], BF16)
    wob = k.sb([128, 8, D], BF16)
    oT = k.sb([128, 8, 128], BF16)
    beta = k.sb([128, NT, 8], F32)
    gg = k.sb([128, NT, 8], F32)
    ba = k.sb([128, NT, 16], F32)
    vec8 = k.sb([128, 4, 8], F32)
    og = k.sb([128, 128], F32)
    load_gainT(k, P['norm1'], C['gainT'])
    load_wob(k, C, P['c_w_out'], wob)
    sc.dma('sp', vec8[:, 0, :], P['c_a_log'].partition_broadcast(128), writes=['vec8'], stream=2)
    sc.dma('sp', vec8[:, 1, :], P['c_dt_bias'].partition_broadcast(128), writes=['vec8'], stream=2)
    sc.dma('sp', og[:, :], P['c_out_gain'].partition_broadcast(128), writes=['og'], stream=2)
    Win = P['c_w_in']
    with ExitStack() as st3:
        k.st = st3
        hT = k.sb([128, 8, S], BF16)
        raw = k.sb([128, S + 3], F32)
        acc = k.sb([128, S], F32)
        so = acc
        outb = k.sb([128, S], BF16)
        sqb = [k.sb([128, 512], BF16) for _ in range(2)]
        r1 = [k.sb([128, 512], F32) for _ in range(2)]
        cw = k.sb([128, 4], F32)
        tstage = [k.sb([128, 8, 128], BF16) for _ in range(2)]
        wz = k.sb([128, 8, D], BF16)
        zs = [k.sb([128, D], BF16) for _ in range(2)]
        norm_tiles(k, x_d, hT, range(NT), None, C)
        hall = [('hT', s_) for s_ in range(NT)]
        sc.op('pool', lambda e: e.memset(raw[:, 0:3], 0.0), writes=['raw0'])
        for j in range(3):
            for h in range(8):
                col = j * 1024 + h * 128
                load_w_chunk(k, Win, col, 128, C['wst'][0], C['wbf'][0], C['gainT'], ('wst', 0), ('wbf', 0))
                sc.dma('sp', cw[:, :], P['c_conv_w'][:, col:col + 128].rearrange("i c -> c i"), writes=['cw'], stream=2, allow_slow_non_contiguous=True)
                for c in range(8):
                    pq = k.ps[c % 2]
                    for kc in range(8):
                        sc.op('pe', lambda e, kc=kc: e.matmul(pq[:, :], lhsT=C['wbf'][0][:, kc, :], rhs=hT[:, kc, c * 512:(c + 1) * 512], start=(kc == 0), stop=(kc == 7)),
                              reads=[('wbf', 0)] + hall, writes=[('ps', c % 2)])
                    sc.op('act', lambda e: e.activation(out=raw[:, 3 + c * 512:3 + (c + 1) * 512], in_=pq[:, :], func=AF.Copy), reads=[('ps', c % 2)], writes=[('raw', c)])
                rall = [('raw', c) for c in range(8)] + ['raw0']
                for hf in range(2):
                    sl = slice(hf * 2048, (hf + 1) * 2048)
                    sc.op('dve', lambda e: e.tensor_scalar(out=acc[:, sl], in0=raw[:, 3 + hf * 2048:3 + (hf + 1) * 2048], scalar1=cw[:, 3:4], scalar2=None, op0=ALU.mult),
                          reads=rall + ['cw'], writes=[('acc', hf), ('so', hf)])
                    for i in range(3):
                        sc.op('dve', lambda e, i=i: e.scalar_tensor_tensor(out=acc[:, sl], in0=raw[:, i + hf * 2048:i + (hf + 1) * 2048], scalar=cw[:, i:i + 1], in1=acc[:, sl],
                                                                          op0=ALU.mult, op1=ALU.add),
                              reads=rall + ['cw', ('acc', hf)], writes=[('acc', hf)])
                    if j == 2:
                        sc.op('act', lambda e: e.activation(out=outb[:, sl], in_=acc[:, sl], func=AF.Silu), reads=[('acc', hf)], writes=[('outb', hf)])
                    else:
                        sc.op('act', lambda e: e.activation(out=so[:, sl], in_=acc[:, sl], func=AF.Silu), reads=[('acc', hf)], writes=[('so', hf), ('acc', hf)])
                if j < 2:
                    for c in range(8):
                        b = c % 2
                        cs = slice(c * 512, (c + 1) * 512)
                        pss = k.ps[2 + b]
                        sc.op('act', lambda e: e.activation(out=sqb[b][:, :], in_=so[:, cs], func=AF.Square), reads=[('so', c // 4)], writes=[('sqb', b)])
                        sc.op('pe', lambda e: e.matmul(pss[:, :], lhsT=C['onesb'][:, :], rhs=sqb[b][:, :], start=True, stop=True), reads=[('sqb', b), 'const'], writes=[('ps', 2 + b)])
                        sc.op('dve', lambda e: e.tensor_scalar(out=r1[b][:, :], in0=pss[:, :], scalar1=EPS, scalar2=None, op0=ALU.add), reads=[('ps', 2 + b)], writes=[('r1', b)])
                        sc.op('act', lambda e: e.activation(out=r1[b][:, :], in_=r1[b][:, :], func=AF.Sqrt), reads=[('r1', b)], writes=[('r1', b)])
                        sc.op('dve', lambda e: e.reciprocal(out=r1[b][:, :], in_=r1[b][:, :]), reads=[('r1', b)], writes=[('r1', b)])
                        sc.op('dve', lambda e: e.scalar_tensor_tensor(out=outb[:, cs], in0=so[:, cs], scalar=(128.0 ** -0.5 if j == 0 else 1.0), in1=r1[b][:, :], op0=ALU.mult, op1=ALU.mult),
                              reads=[('so', c // 4), ('r1', b)], writes=[('outb', c // 4)])
                    sc.dma('pool', (qTd if j == 0 else kTd).ap()[h, :, :], outb[:, :], reads=[('outb', 0), ('outb', 1)], writes=['qkTd'], stream=6)
                if j >= 1:
                    dst = Kd if j == 1 else Vd
                    dv_ = dst.ap().rearrange("(n p) h d -> p n h d", p=128)
                    for n8 in range(4):
                        b = n8 % 2
                        for t in range(8):
                            n = n8 * 8 + t
                            sc.op('pe', lambda e, t=t, n=n: e.transpose(out=k.psb[:, t * 128:(t + 1) * 128], in_=outb[:, n * 128:(n + 1) * 128], identity=C['identb'][:, :]),
                                  reads=[('outb', n // 16), 'const'], writes=['psb'])
                        sc.op('act', lambda e: e.activation(out=tstage[b][:, :, :], in_=k.psb[:, :].rearrange("p (a t) -> p a t", a=8), func=AF.Copy), reads=['psb'], writes=[('tst', b)])
                        sc.dma('pool', dv_[:, n8 * 8:(n8 + 1) * 8, h, :], tstage[b][:, :, :], reads=[('tst', b)], writes=['KVd'], stream=7)
                sc.maybe_phase()
        for pc in range(8):
            load_w_chunk(k, Win, 3072 + pc * 128, 128, C['wst'][pc % 2], C['wbf'][pc % 2], C['gainT'], ('wst', pc % 2), ('wbf', pc % 2))
            sc.op('act', lambda e, pc=pc: e.activation(out=wz[:, :, pc * 128:(pc + 1) * 128], in_=C['wbf'][pc % 2][:, :, :], func=AF.Copy), reads=[('wbf', pc % 2)], writes=['wz'])
        load_w_chunk(k, Win, 4096, 16, C['wst2'][0], C['wbf2'][0], C['gainT'], ('wst2', 0), ('wbf2', 0))
        for n in range(NT):
            b = n % 2
            for half in range(2):
                pz = k.ps[half]
                for kc in range(8):
                    sc.op('pe', lambda e, kc=kc: e.matmul(pz[:, :], lhsT=hT[:, kc, n * 128:(n + 1) * 128], rhs=wz[:, kc, half * 512:(half + 1) * 512], start=(kc == 0), stop=(kc == 7)),
                          reads=['wz', ('hT', n)], writes=[('ps', half)])
                sc.op('act', lambda e: e.activation(out=zs[b][:, half * 512:(half + 1) * 512], in_=pz[:, :], func=AF.Silu), reads=[('ps', half)], writes=[('zs', b)])
            sc.dma('pool', zd.ap()[n * 128:(n + 1) * 128, :], zs[b][:, :], reads=[('zs', b)], writes=['zd'], stream=6)
            pb = k.ps[2]
            for kc in range(8):
                sc.op('pe', lambda e, kc=kc: e.matmul(pb[:, 0:16], lhsT=hT[:, kc, n * 128:(n + 1) * 128], rhs=C['wbf2'][0][:, kc, 0:16], start=(kc == 0), stop=(kc == 7)),
                      reads=[('wbf2', 0), ('hT', n)], writes=[('ps', 2)])
            sc.op('dve', lambda e: e.tensor_copy(out=ba[:, n, :], in_=pb[:, 0:16]), reads=[('ps', 2)], writes=['ba'])
        tmp = raw[:, 0:NT * 8].rearrange("p (n h) -> p n h", h=8)
        tmp2 = acc[:, 0:NT * 8].rearrange("p (n h) -> p n h", h=8)
        tmp3 = acc[:, 2048:2048 + NT * 8].rearrange("p (n h) -> p n h", h=8)
        sc.op('act', lambda e: e.activation(out=beta[:, :, :], in_=ba[:, :, 0:8], func=AF.Sigmoid), reads=['ba'], writes=['beta'])
        sc.op('act', lambda e: e.activation(out=vec8[:, 2, :], in_=vec8[:, 0, :], func=AF.Exp), reads=['vec8'], writes=['vec8b'])
        sc.op('dve', lambda e: e.tensor_tensor(out=tmp, in0=ba[:, :, 8:16], in1=bc(vec8[:, 1:2, :], [128, NT, 8]), op=ALU.add),
              reads=['ba', 'vec8', 'raw0'] + [('raw', c) for c in range(8)], writes=['tmpx'])
        sc.op('act', lambda e: e.activation(out=tmp2, in_=tmp, func=AF.Abs), reads=['tmpx', ('acc', 0), ('so', 0)], writes=['tmp2'])
        sc.op('act', lambda e: e.activation(out=tmp2, in_=tmp2, func=AF.Exp, scale=-1.0), reads=['tmp2'], writes=['tmp2'])
        sc.op('dve', lambda e: e.tensor_scalar(out=tmp2, in0=tmp2, scalar1=1.0, scalar2=None, op0=ALU.add), reads=['tmp2'], writes=['tmp2'])
        sc.op('act', lambda e: e.activation(out=tmp2, in_=tmp2, func=AF.Ln), reads=['tmp2'], writes=['tmp2'])
        sc.op('dve', lambda e: e.tensor_scalar(out=tmp3, in0=tmp, scalar1=0.0, scalar2=None, op0=ALU.max), reads=['tmpx', ('so', 0), ('so', 1), ('acc', 1)], writes=['tmp3'])
        sc.op('dve', lambda e: e.tensor_tensor(out=tmp3, in0=tmp3, in1=tmp2, op=ALU.add), reads=['tmp3', 'tmp2'], writes=['tmp3'])
        sc.op('dve', lambda e: e.tensor_tensor(out=tmp3, in0=tmp3, in1=bc(vec8[:, 2:3, :], [128, NT, 8]), op=ALU.mult), reads=['tmp3', 'vec8b'], writes=['tmp3'])
        sc.op('dve', lambda e: e.tensor_scalar(out=gg[:, :, :], in0=tmp3, scalar1=-1.0, scalar2=None, op0=ALU.mult), reads=['tmp3'], writes=['gg'])
        sc.phase()
    k.st = st2
    if STOP == 'c1':
        copy_x(k, x_d, xo_d, C)
        k.st = old
        st2.close()
        return
    def t4(nm, dt=F32, n=4):
        return k.sb([128, n, 128], dt)
    qTt = [k.sb([128, 8, 128], BF16) for _ in range(2)]
    kTt = [k.sb([128, 8, 128], BF16) for _ in range(2)]
    Kt = [k.sb([128, 8, 128], BF16) for _ in range(2)]
    Vt = [k.sb([128, 8, 128], BF16) for _ in range(2)]
    zt = [k.sb([128, D], BF16) for _ in range(2)]
    Rm = k.sb([128, 8, 128], F32)
    sm = k.sb([128, 12, 8], F32)
    t0 = t4('t0'); tA = t4('tA'); DmT = t4('DmT'); Dms = t4('Dms')
    Xa, Xb, Ya, Yb = t4('Xa'), t4('Xb'), t4('Ya'), t4('Yb')
    Qm = t4('Qm'); Qb = t4('Qb', BF16)
    attnT = k.sb([128, 8, 128], BF16)
    Vb = k.sb([128, 8, 128], BF16); Kh = k.sb([128, 8, 128], BF16); Kti = k.sb([128, 8, 128], BF16)
    u = k.sb([128, 8, 128], F32); wTb = k.sb([128, 8, 128], BF16)
    vn = k.sb([128, 8, 128], BF16)
    Sf = k.sb([128, 8, 128], F32); Sb = k.sb([128, 8, 128], BF16)
    osb = k.sb([128, 8, 128], F32); tq = t4('tq')
    sqo = k.sb([128, 8, 128], F32)
    ob = k.sb([128, 8, 128], BF16)
    sc.op('pool', lambda e: e.memset(Sf[:, :, :], 0.0), writes=['Sf'])
    sc.op('pool', lambda e: e.memset(Sb[:, :, :], 0.0), writes=['Sb'])
    qv = qTd.ap().rearrange("h p t -> p h t")
    kv = kTd.ap().rearrange("h p t -> p h t")
    PS = k.ps
    for n in range(int(os.environ.get('KTILES', NT))):
        b = n % 2
        ts_ = slice(n * 128, (n + 1) * 128)
        sc.dma('sp', qTt[b][:, :, :], qv[:, :, ts_], writes=[('qTt', b)], stream=1)
        sc.dma('sp', kTt[b][:, :, :], kv[:, :, ts_], writes=[('kTt', b)], stream=1)
        sc.dma('sp', Kt[b][:, :, :], Kd.ap()[ts_, :, :], writes=[('Kt', b)], stream=4)
        sc.dma('sp', Vt[b][:, :, :], Vd.ap()[ts_, :, :], writes=[('Vt', b)], stream=4)
        sc.dma('sp', zt[b][:, :], zd.ap()[ts_, :], writes=[('zt', b)], stream=5)
        gt = gg[:, n, :]
        for idx, lh in ((0, cd['TB']), (1, cd['BLK']), (2, cd['H0']), (3, cd['H1'])):
            sc.op('pe', lambda e, lh=lh, idx=idx: e.matmul(PS[6][:, idx * 8:idx * 8 + 8], lhsT=lh[:, :], rhs=gt, start=True, stop=True), reads=['gg', 'cd'], writes=[('ps', 6)])
        sc.op('dve', lambda e: e.tensor_copy(out=sm[:, 0:4, :], in_=PS[6][:, 0:32].rearrange("p (a h) -> p a h", a=4)), reads=[('ps', 6)], writes=['sm'])
        sc.op('act', lambda e: e.activation(out=sm[:, 4, :], in_=sm[:, 0, :], func=AF.Exp), reads=['sm'], writes=['sm4'])
        sc.op('act', lambda e: e.activation(out=sm[:, 5:7, :], in_=sm[:, 2:4, :], func=AF.Exp), reads=['sm'], writes=['sm5'])
        sc.op('dve', lambda e: e.tensor_tensor(out=sm[:, 7, :], in0=sm[:, 1, :], in1=sm[:, 0, :], op=ALU.subtract), reads=['sm'], writes=['sm7'])
        sc.op('act', lambda e: e.activation(out=sm[:, 7, :], in_=sm[:, 7, :], func=AF.Exp), reads=['sm7'], writes=['sm7'])
        sc.op('dve', lambda e: e.tensor_tensor(out=sm[:, 8, :], in0=sm[:, 4, :], in1=beta[:, n, :], op=ALU.mult), reads=['sm4', 'beta'], writes=['sm8'])
        sc.op('dve', lambda e: e.tensor_scalar(out=sm[:, 9, :], in0=beta[:, n, :], scalar1=-1.0, scalar2=None, op0=ALU.mult), reads=['beta'], writes=['sm9'])
        sc.op('dve', lambda e: e.tensor_tensor(out=Rm[:, :, :], in0=bc(cd['TB'][:, :].unsqueeze(1), [128, 8, 128]), in1=bc(gt.unsqueeze(2), [128, 8, 128]), op=ALU.mult),
              reads=['gg', 'cd'], writes=['Rm'])
        sc.op('dve', lambda e: e.tensor_tensor(out=Vb[:, :, :], in0=Vt[b][:, :, :], in1=bc(beta[:, n, :].unsqueeze(2), [128, 8, 128]), op=ALU.mult), reads=[('Vt', b), 'beta'], writes=['Vb'])
        sc.op('dve', lambda e: e.tensor_tensor(out=Kh[:, :, :], in0=Kt[b][:, :, :], in1=bc(sm[:, 8, :].unsqueeze(2), [128, 8, 128]), op=ALU.mult), reads=[('Kt', b), 'sm8'], writes=['Kh'])
        sc.op('dve', lambda e: e.tensor_tensor(out=Kti[:, :, :], in0=Kt[b][:, :, :], in1=bc(sm[:, 7, :].unsqueeze(2), [128, 8, 128]), op=ALU.mult), reads=[('Kt', b), 'sm7'], writes=['Kti'])
        if STOP == 'small':
            continue
        for hg in range(2):
            hs = slice(4 * hg, 4 * hg + 4)
            H = range(4 * hg, 4 * hg + 4)
            def v4(p):
                return p[:, :].rearrange("p (a t) -> p a t", a=4)
            sc.op('pe', lambda e: e.matmul(PS[0][:, :], lhsT=cd['ONES'][:, :], rhs=Rm[:, hs, :].rearrange("p a t -> p (a t)"), start=True, stop=True), reads=['Rm', 'cd'], writes=[('ps', 0)])
            for i, h in enumerate(H):
                sc.op('pe', lambda e, i=i, h=h: e.matmul(PS[1][:, i * 128:(i + 1) * 128], lhsT=kTt[b][:, h, :], rhs=kTt[b][:, h, :], start=True, stop=True), reads=[('kTt', b)], writes=[('ps', 1)])
            for i, h in enumerate(H):
                sc.op('pe', lambda e, i=i, h=h: e.matmul(PS[2][:, i * 128:(i + 1) * 128], lhsT=kTt[b][:, h, :], rhs=qTt[b][:, h, :], start=True, stop=True), reads=[('kTt', b), ('qTt', b)], writes=[('ps', 2)])
            sc.op('dve', lambda e: e.tensor_tensor(out=t0[:, :, :], in0=v4(PS[0]), in1=bc(sm[:, 0, hs].unsqueeze(2), [128, 4, 128]), op=ALU.subtract), reads=[('ps', 0), 'sm'], writes=['t0'])
            if STOP == 'g1':
                continue
            sc.op('dve', lambda e: e.tensor_tensor(out=tA[:, :, :], in0=t0[:, :, :], in1=bc(cd['MT'][:, :].unsqueeze(1), [128, 4, 128]), op=ALU.add), reads=['t0', 'cd'], writes=['tA'])
            sc.op('act', lambda e: e.activation(out=DmT[:, :, :], in_=tA[:, :, :], func=AF.Exp), reads=['tA'], writes=['DmT'])
            sc.op('dve', lambda e: e.tensor_tensor(out=tA[:, :, :], in0=t0[:, :, :], in1=bc(cd['MBS'][:, :].unsqueeze(1), [128, 4, 128]), op=ALU.add), reads=['t0', 'cd', 'tA'], writes=['tA'])
            sc.op('act', lambda e: e.activation(out=Dms[:, :, :], in_=tA[:, :, :], func=AF.Exp, scale=-1.0), reads=['tA'], writes=['Dms'])
            if STOP == 'g2':
                continue
            sc.op('dve', lambda e: e.tensor_tensor(out=attnT[:, hs, :], in0=v4(PS[2]), in1=DmT[:, :, :], op=ALU.mult), reads=[('ps', 2), 'DmT'], writes=[('attnT', hg)])
            sc.op('dve', lambda e: e.tensor_tensor(out=Xa[:, :, :], in0=v4(PS[1]), in1=Dms[:, :, :], op=ALU.mult), reads=[('ps', 1), 'Dms'], writes=['Xa'])
            sc.op('dve', lambda e: e.tensor_tensor(out=Xa[:, :, :], in0=Xa[:, :, :], in1=bc(sm[:, 9, hs].unsqueeze(2), [128, 4, 128]), op=ALU.mult), reads=['Xa', 'sm9'], writes=['Xa'])
            if STOP == 'g3':
                continue
            for i in range(4):
                sc.op('pe', lambda e, i=i: e.matmul(PS[3][:, i * 128:(i + 1) * 128], lhsT=Xa[:, i, :], rhs=C['identf'][:, :], start=True, stop=True), reads=['Xa', 'const'], writes=[('ps', 3)])
            if STOP == 'g4':
                continue
            sc.op('act', lambda e: e.activation(out=Ya[:, :, :], in_=v4(PS[3]), func=AF.Copy), reads=[('ps', 3)], writes=['Ya'])
            if STOP == 'g5':
                continue
            sc.op('dve', lambda e: e.tensor_tensor(out=Qm[:, :, :], in0=Ya[:, :, :], in1=bc(C['identf'][:, :].unsqueeze(1), [128, 4, 128]), op=ALU.add), reads=['Ya', 'const'], writes=['Qm'])
            if STOP == 'pre':
                continue
            Xc, Yc, Xn, Yn, xc, yc, xn, yn = Xa, Ya, Xb, Yb, 'Xa', 'Ya', 'Xb', 'Yb'
            for step in range(5):
                for i in range(4):
                    sc.op('pe', lambda e, i=i, Xc=Xc, Yc=Yc: e.matmul(PS[0][:, i * 128:(i + 1) * 128], lhsT=Yc[:, i, :], rhs=Xc[:, i, :], start=True, stop=True), reads=[xc, yc], writes=[('ps', 0)])
                sc.op('act', lambda e, Xn=Xn: e.activation(out=Xn[:, :, :], in_=v4(PS[0]), func=AF.Copy), reads=[('ps', 0)], writes=[xn])
                if step < 4:
                    for i in range(4):
                        sc.op('pe', lambda e, i=i, Xc=Xc, Yc=Yc: e.matmul(PS[1][:, i * 128:(i + 1) * 128], lhsT=Xc[:, i, :], rhs=Yc[:, i, :], start=True, stop=True), reads=[xc, yc], writes=[('ps', 1)])
                    sc.op('act', lambda e, Yn=Yn: e.activation(out=Yn[:, :, :], in_=v4(PS[1]), func=AF.Copy), reads=[('ps', 1)], writes=[yn])
                for i in range(4):
                    sc.op('pe', lambda e, i=i, Xn=Xn: e.matmul(PS[2][:, i * 128:(i + 1) * 128], lhsT=Xn[:, i, :], rhs=Qm[:, i, :], start=True, stop=True), reads=[xn, 'Qm'], writes=[('ps', 2)])
                sc.op('dve', lambda e: e.tensor_tensor(out=Qm[:, :, :], in0=v4(PS[2]), in1=Qm[:, :, :], op=ALU.add), reads=[('ps', 2), 'Qm'], writes=['Qm'])
                Xc, Yc, Xn, Yn, xc, yc, xn, yn = Xn, Yn, Xc, Yc, xn, yn, xc, yc
            if STOP == 'inv':
                continue
            sc.op('act', lambda e: e.activation(out=Qb[:, :, :], in_=Qm[:, :, :], func=AF.Copy), reads=['Qm'], writes=['Qb'])
            for i, h in enumerate(H):
                sc.op('pe', lambda e, i=i, h=h: e.matmul(PS[3][:, i * 128:(i + 1) * 128], lhsT=Qb[:, i, :], rhs=Vb[:, h, :], start=True, stop=True), reads=['Qb', 'Vb'], writes=[('ps', 3)])
            sc.op('act', lambda e: e.activation(out=u[:, hs, :], in_=v4(PS[3]), func=AF.Copy), reads=[('ps', 3)], writes=[('u', hg)])
            for i, h in enumerate(H):
                sc.op('pe', lambda e, i=i, h=h: e.matmul(PS[4][:, i * 128:(i + 1) * 128], lhsT=Kh[:, h, :], rhs=Qb[:, i, :], start=True, stop=True), reads=['Qb', 'Kh'], writes=[('ps', 4)])
            sc.op('act', lambda e: e.activation(out=wTb[:, hs, :], in_=v4(PS[4]), func=AF.Copy), reads=[('ps', 4)], writes=[('wTb', hg)])
        if STOP in ('setup', 'pre', 'inv', 'g1', 'g2', 'g3', 'g4', 'g5'):
            continue
        for hf in range(2):
            Pp = slice(64 * hf, 64 * hf + 64)
            for hg in range(2):
                hs = slice(4 * hg, 4 * hg + 4)
                H = range(4 * hg, 4 * hg + 4)
                sres = ('Sb', hg)
                for i, h in enumerate(H):
                    sc.op('pe', lambda e, i=i, h=h: e.matmul(PS[0][:, i * 128:(i + 1) * 128], lhsT=wTb[:, h, :], rhs=Sb[:, h, :], start=True, stop=True), reads=[('wTb', hg), sres, 'Sb'], writes=[('ps', 0)])
                for i, h in enumerate(H):
                    sc.op('pe', lambda e, i=i, h=h: e.matmul(PS[1][:, i * 128:(i + 1) * 128], lhsT=qTt[b][:, h, :], rhs=Sb[:, h, :], start=True, stop=True), reads=[('qTt', b), sres, 'Sb'], writes=[('ps', 1)])
                sc.op('dve', lambda e: e.tensor_tensor(out=vn[Pp, hs, :], in0=u[Pp, hs, :], in1=v4(PS[0])[Pp, :, :], op=ALU.subtract), reads=[('u', hg), ('ps', 0)], writes=[('vn', hg)])
                sc.op('dve', lambda e: e.tensor_tensor(out=tq[Pp, :, :], in0=v4(PS[1])[Pp, :, :], in1=bc(sm[Pp, 4, hs].unsqueeze(2), [64, 4, 128]), op=ALU.mult), reads=[('ps', 1), 'sm4'], writes=['tq'])
                for i, h in enumerate(H):
                    sc.op('pe', lambda e, i=i, h=h: e.matmul(PS[2][:, i * 128:(i + 1) * 128], lhsT=Kti[Pp, h, :], rhs=vn[Pp, h, :], start=True, stop=True), reads=['Kti', ('vn', hg)], writes=[('ps', 2)])
                for i, h in enumerate(H):
                    sc.op('pe', lambda e, i=i, h=h: e.matmul(PS[3][:, i * 128:(i + 1) * 128], lhsT=attnT[Pp, h, :], rhs=vn[Pp, h, :], start=True, stop=True), reads=[('attnT', hg), ('vn', hg)], writes=[('ps', 3)])
                sc.op('dve', lambda e: e.tensor_tensor(out=Sf[:, hs, :], in0=Sf[:, hs, :], in1=bc(sm[:, 5 + hf, hs].unsqueeze(2), [128, 4, 128]), op=ALU.mult), reads=['Sf', 'sm5', ('Sf', hg)], writes=[('Sf', hg)])
                sc.op('dve', lambda e: e.tensor_tensor(out=Sf[:, hs, :], in0=Sf[:, hs, :], in1=v4(PS[2]), op=ALU.add), reads=[('Sf', hg), ('ps', 2)], writes=[('Sf', hg)])
                sc.op('act', lambda e: e.activation(out=Sb[:, hs, :], in_=Sf[:, hs, :], func=AF.Copy), reads=[('Sf', hg)], writes=[sres])
                sc.op('dve', lambda e: e.tensor_tensor(out=osb[Pp, hs, :], in0=tq[Pp, :, :], in1=v4(PS[3])[Pp, :, :], op=ALU.add), reads=['tq', ('ps', 3)], writes=[('osb', hg)])
        if STOP == 'seq':
            continue
        sc.op('act', lambda e: e.activation(out=sqo[:, :, :], in_=osb[:, :, :], func=AF.Square), reads=[('osb', 0), ('osb', 1)], writes=['sqo'])
        sc.op('dve', lambda e: e.tensor_reduce(out=sm[:, 10, :], in_=sqo[:, :, :], axis=AX.X, op=ALU.add), reads=['sqo'], writes=['sm10'])
        sc.op('dve', lambda e: e.tensor_scalar(out=sm[:, 10, :], in0=sm[:, 10, :], scalar1=1.0 / 128, scalar2=EPS, op0=ALU.mult, op1=ALU.add), reads=['sm10'], writes=['sm10'])
        sc.op('act', lambda e: e.activation(out=sm[:, 10, :], in_=sm[:, 10, :], func=AF.Sqrt), reads=['sm10'], writes=['sm10'])
        sc.op('dve', lambda e: e.reciprocal(out=sm[:, 11, :], in_=sm[:, 10, :]), reads=['sm10'], writes=['sm11'])
        sc.op('dve', lambda e: e.tensor_tensor(out=sqo[:, :, :], in0=osb[:, :, :], in1=bc(sm[:, 11, :].unsqueeze(2), [128, 8, 128]), op=ALU.mult), reads=[('osb', 0), ('osb', 1), 'sm11', 'sqo'], writes=['sqo'])
        sc.op('dve', lambda e: e.tensor_tensor(out=sqo[:, :, :], in0=sqo[:, :, :], in1=bc(og[:, :].unsqueeze(1), [128, 8, 128]), op=ALU.mult), reads=['sqo', 'og'], writes=['sqo'])
        sc.op('dve', lambda e: e.tensor_tensor(out=ob[:, :, :], in0=sqo[:, :, :], in1=zt[b][:, :].rearrange("p (h d) -> p h d", h=8), op=ALU.mult), reads=['sqo', ('zt', b)], writes=['ob'])
        out_tail(k, C, ob[:, :, :].rearrange("p h d -> p (h d)"), wob, oT, x_d, xo_d, n, 'ob')
        sc.maybe_phase()
    sc.phase()
    k.st = old
    st2.close()


def dense_attn(k, C, B, qTh, lhsT_fn, vaug_fn, ncol, tab_fn, kts_fn, qb0_fn, last_fn, evac_fn, pen_fn=None, qcs=range(8), tag=''):
    sc = k.sc
    PS = k.ps
    Eb, PT = B['Eb'], B['PT']

    def acc(qb):
        return PS[2 + qb][:, 0:ncol], ('ps', 2 + qb)

    u = 0
    for qc in (list(qcs)[::-1] if os.environ.get('KREV') else qcs):
        kts = kts_fn(qc)
        units = []
        for kt in kts:
            qb0 = qb0_fn(qc, kt)
            units.append((kt, qb0, qc * 512 + qb0 * 128, 512 - qb0 * 128))

        def emit_s(ui, un):
            kt, qb0, q0, nq = un
            pS = PS[ui % 2]
            lh, lres = lhsT_fn(kt)
            sc.op('pe', lambda e: e.matmul(pS[:, 0:nq], lhsT=lh, rhs=qTh[0:64, q0:q0 + nq], start=True, stop=(pen_fn is None)),
                  reads=[lres, 'qTh'], writes=[('ps', ui % 2)])
            if pen_fn is not None:
                pl, pr, pres = pen_fn(kt, q0, nq)
                sc.op('pe', lambda e: e.matmul(pS[:, 0:nq], lhsT=pl, rhs=pr, start=False, stop=True), reads=pres, writes=[('ps', ui % 2)])

        if units:
            emit_s(u, units[0])
        started = set()
        for i, un in enumerate(units):
            kt, qb0, q0, nq = un
            ui = u + i
            if i + 1 < len(units):
                emit_s(ui + 1, units[i + 1])
            pS, eb, pt = PS[ui % 2], Eb[ui % 2], PT[ui % 3]
            sc.op('act', lambda e: e.activation(out=eb[:, 0:nq], in_=pS[:, 0:nq], func=AF.Exp, scale=0.125), reads=[('ps', ui % 2)], writes=[('Eb', ui % 2)])
            tb, tres = tab_fn(kt, q0, nq)
            sc.op('dve', lambda e: e.tensor_tensor(out=pt[:, 0:nq], in0=eb[:, 0:nq], in1=tb, op=ALU.mult), reads=[('Eb', ui % 2), tres], writes=[('PT', ui % 3)])
            va, vres = vaug_fn(kt)
            for qb in range(qb0, 4):
                if kt > last_fn(qc, qb):
                    continue
                a, ares = acc(qb)
                sc.op('pe', lambda e, qb=qb, a=a: e.matmul(a, lhsT=pt[:, (qb - qb0) * 128:(qb - qb0 + 1) * 128], rhs=va, start=(qb not in started), stop=(kt == last_fn(qc, qb))),
                      reads=[('PT', ui % 3)] + vres, writes=[ares])
                started.add(qb)
        u += len(units)
        evac_fn(qc, acc)
        sc.maybe_phase()


class StopHere(Exception):
    pass


def chk(tag):
    if STOP == tag:
        raise StopHere()


def build_head_tab(k, C, oh_d, width, tab, hq, relh, ohs, stg, res):
    sc = k.sc
    sc.op('dve', lambda e: e.tensor_copy(out=relh[:, :], in_=bc(C['relb'][:, hq:hq + 1], [33, 128])), reads=['relb'], writes=['relh'])
    for ci, c0 in enumerate(range(0, width, 512)):
        n = min(512, width - c0)
        j = ci % 2
        sc.dma('sp', ohs[:, 0:n], oh_d[:, c0:c0 + n], writes=['ohs'], stream=2)
        sc.op('pe', lambda e: e.matmul(k.ps[6][:, 0:n], lhsT=relh[:, :], rhs=ohs[:, 0:n], start=True, stop=True), reads=['ohs', 'relh'], writes=[('ps', 6)])
        sc.op('act', lambda e: e.activation(out=stg[j][:, 0:n], in_=k.ps[6][:, 0:n], func=AF.Exp), reads=[('ps', 6)], writes=[('tslab', j)])
        sc.dma('pool', tab.ap()[:, c0:c0 + n], stg[j][:, 0:n], reads=[('tslab', j)], writes=[res], stream=9)


def mixer_b(k, x_d, xo_d, P, C, tabs, cdn):
    sc = k.sc
    st2 = ExitStack()
    old = k.st
    k.st = st2
    PS = k.ps
    k.doff = k.dbase
    ohC, ohD, ohW = tabs
    tabC, tabD, tabW = k.dram([128, 6144]), k.dram([128, 4224]), k.dram([128, 1280])
    relh = k.sb([33, 128], F32)
    ohs = k.sb([33, 512], F32)
    if STOP == 'b0':
        copy_x(k, x_d, xo_d, C)
        k.st = old
        st2.close()
        return
    WC, WD, WW = 6144, 4224, 1280
    qTd = k.dram([16, 64, S], BF16)
    kTd = [k.dram([4, 64, S], BF16) for _ in range(4)]
    Vd = [k.dram([S, 4, 64], BF16) for _ in range(2)]
    obd = k.dram([S, D], BF16)
    wob = k.sb([128, 8, D], BF16)
    oT = k.sb([128, 8, 128], BF16)
    gates = k.sb([128, NT, 48], F32)
    gcol = k.sb([128, 4], F32)
    load_gainT(k, P['norm1'], C['gainT'])
    load_wob(k, C, P['b_w_out'], wob)
    for half in range(2):
        sc.dma('sp', gcol[half * 64:(half + 1) * 64, 0:1], P['b_q_gain'].rearrange("(d o) -> d o", o=1), writes=['gcol'], stream=2, allow_slow_non_contiguous=True)
        sc.dma('sp', gcol[half * 64:(half + 1) * 64, 1:4], P['b_k_gain'].rearrange("g d -> d g"), writes=['gcol'], stream=2, allow_slow_non_contiguous=True)
    Win = P['b_w_in']
    with ExitStack() as st3:
        k.st = st3
        hT = k.sb([128, 8, S], BF16)
        outb = k.sb([128, S], BF16)
        sqb = [k.sb([128, 512], BF16) for _ in range(2)]
        r1 = [k.sb([128, 512], F32) for _ in range(2)]
        wv = k.sb([128, 8, 256], BF16)
        vst = [k.sb([128, 256], BF16) for _ in range(2)]
        norm_tiles(k, x_d, hT, range(NT), None, C)
        hall = [('hT', s_) for s_ in range(NT)]
        chunks = [(c * 128, qTd, 2 * c, 0) for c in range(8)]
        for br, kvi, dst, gc in ((0, 0, kTd[0], None), (0, 1, kTd[1], None), (1, 0, kTd[2], 2), (2, 0, kTd[3], 3)):
            for c2 in range(2):
                chunks.append((1024 + (br * 2 + kvi) * 256 + c2 * 128, dst, 2 * c2, gc))
        for (col, dst, h0, gc) in chunks[int(os.environ.get('KCH0', 0)):int(os.environ.get('KCH1', 99))]:
            load_w_chunk(k, Win, col, 128, C['wst'][0], C['wbf'][0], C['gainT'], ('wst', 0), ('wbf', 0))
            for c in range(8):
                b = c % 2
                cs = slice(c * 512, (c + 1) * 512)
                pq, pss = PS[4], PS[5]
                for kc in range(8):
                    sc.op('pe', lambda e, kc=kc: e.matmul(pq[:, :], lhsT=C['wbf'][0][:, kc, :], rhs=hT[:, kc, cs], start=(kc == 0), stop=(kc == 7)),
                          reads=[('wbf', 0)] + hall, writes=[('ps', 4)])
                if gc is None:
                    sc.op('act', lambda e: e.activation(out=outb[:, cs], in_=pq[:, :], func=AF.Copy), reads=[('ps', 4)], writes=[('outb', c)])
                    continue
                sc.op('act', lambda e: e.activation(out=sqb[b][:, :], in_=pq[:, :], func=AF.Square), reads=[('ps', 4)], writes=[('sqb', b)])
                sc.op('pe', lambda e: e.matmul(pss[:, :], lhsT=C['blk'][:, :], rhs=sqb[b][:, :], start=True, stop=True), reads=[('sqb', b), 'const'], writes=[('ps', 5)])
                sc.op('dve', lambda e: e.tensor_scalar(out=r1[b][:, :], in0=pss[:, :], scalar1=1.0 / 64, scalar2=EPS, op0=ALU.mult, op1=ALU.add), reads=[('ps', 5)], writes=[('r1', b)])
                sc.op('act', lambda e: e.activation(out=r1[b][:, :], in_=r1[b][:, :], func=AF.Sqrt), reads=[('r1', b)], writes=[('r1', b)])
                sc.op('dve', lambda e: e.reciprocal(out=r1[b][:, :], in_=r1[b][:, :]), reads=[('r1', b)], writes=[('r1', b)])
                sc.op('dve', lambda e: e.scalar_tensor_tensor(out=outb[:, cs], in0=pq[:, :], scalar=gcol[:, gc:gc + 1], in1=r1[b][:, :], op0=ALU.mult, op1=ALU.mult),
                      reads=[('ps', 4), ('r1', b), 'gcol'], writes=[('outb', c)])
            oall = [('outb', c) for c in range(8)]
            sc.dma('pool', dst.ap()[h0, :, :], outb[0:64, :], reads=oall, writes=['featd'], stream=6)
            sc.dma('pool', dst.ap()[h0 + 1, :, :], outb[64:128, :], reads=oall, writes=['featd'], stream=6)
            sc.maybe_phase()
        for vi, col in enumerate((1024 + 3 * 256, 1024 + 5 * 256)):
            if STOP == 'b1a':
                continue
            for pc in range(2):
                load_w_chunk(k, Win, col + pc * 128, 128, C['wst'][pc], C['wbf'][pc], C['gainT'], ('wst', pc), ('wbf', pc))
                sc.op('act', lambda e, pc=pc: e.activation(out=wv[:, :, pc * 128:(pc + 1) * 128], in_=C['wbf'][pc][:, :, :], func=AF.Copy), reads=[('wbf', pc)], writes=['wv'])
            for n in range(NT):
                b = n % 2
                for kc in range(8):
                    sc.op('pe', lambda e, kc=kc: e.matmul(PS[b][:, 0:256], lhsT=hT[:, kc, n * 128:(n + 1) * 128], rhs=wv[:, kc, :], start=(kc == 0), stop=(kc == 7)),
                          reads=['wv', ('hT', n)], writes=[('ps', b)])
                sc.op('act', lambda e: e.activation(out=vst[b][:, :], in_=PS[b][:, 0:256], func=AF.Copy), reads=[('ps', b)], writes=[('vst', b)])
                sc.dma('pool', Vd[vi].ap()[n * 128:(n + 1) * 128, :, :].rearrange("p h d -> p (h d)"), vst[b][:, :], reads=[('vst', b)], writes=['Vd'], stream=7)
        load_w_chunk(k, Win, 2560, 48, C['wst2'][0], C['wbf2'][0], C['gainT'], ('wst2', 0), ('wbf2', 0))
        for n in range(NT if STOP not in ('b1a', 'b1b') else 0):
            b = n % 2
            for kc in range(8):
                sc.op('pe', lambda e, kc=kc: e.matmul(PS[b][:, 0:48], lhsT=hT[:, kc, n * 128:(n + 1) * 128], rhs=C['wbf2'][0][:, kc, 0:48], start=(kc == 0), stop=(kc == 7)),
                      reads=[('wbf2', 0), ('hT', n)], writes=[('ps', b)])
            sc.op('act', lambda e: e.activation(out=gates[:, n, :], in_=PS[b][:, 0:48], func=AF.Sigmoid), reads=[('ps', b)], writes=['gates'])
        sc.phase()
    k.st = st2
    if STOP in ('b1', 'b1a', 'b1b'):
        for _ in range(int(os.environ.get('KXP2', 0))):
            sc.phase()
        copy_x(k, x_d, xo_d, C)
        k.st = old
        st2.close()
        return
    stopped = False
    kcT = k.sb([64, 4, 256], BF16)
    vaug = k.sb([128, 2, 4, 129], BF16)
    if not os.environ.get('KNOMS'):
        sc.op('dve', lambda e: e.memset(kcT[:, :, :], 0.0), writes=['kcT'])
    if not os.environ.get('KNOMS2'):
        sc.op('dve', lambda e: e.memset(vaug[:, :, :, :], 0.0), writes=['vaug'])
    with ExitStack() as st3:
        k.st = st3
        try:
            w1b = k.sb([64, 32, 256], BF16)
            w2b = k.sb([128, 2, 64], BF16)
            posT = k.sb([64, 32], BF16)
            posn = k.sb([32, 64], F32)
            w2f = k.sb([128, 2, 64], F32)
            tT = k.sb([64, S], BF16)
            biasc = k.sb([128, 2], F32)
            xg = [k.sb([128, 256], F32) for _ in range(4)]
            gT = k.sb([128, 2, 256], BF16)
            mstage = k.sb([128, 2, 64], F32)
            chk('c000')
            sc.op('dve', lambda e: e.memset(gT[:, :, :], 0.0), writes=['gT'])
            chk('c001')
            for ct in range(2):
                sc.dma('sp', mstage[:, ct, :], cdn['M'][ct * 128:(ct + 1) * 128, :], writes=['mstage'], stream=2)
            chk('c002')
            for n4 in range(4):
                sc.op('act', lambda e, n4=n4: e.activation(out=vaug[:, :, n4, 65:129], in_=mstage[:, :, :], func=AF.Copy), reads=['mstage', 'vaug'], writes=['vaug'])
            sc.op('dve', lambda e: e.memset(vaug[:, :, :, 64:65], 1.0), reads=['vaug'], writes=['vaug'])
            chk('c00')
            for kvi in range(2):
                w1v = P['b_cmp_w1'][kvi].rearrange("(l d) h -> d l h", d=64)
                for lq in range(8):
                    wstv = C['wst'][lq % 2][0:64, :, :].rearrange("p a b -> p (a b)").rearrange("p (l h) -> p l h", h=256)
                    sc.dma('sp', wstv, w1v[:, lq * 4:(lq + 1) * 4, :], writes=[('wst', lq % 2)], stream=1)
                    sc.op('act', lambda e, lq=lq, wstv=wstv: e.activation(out=w1b[:, lq * 4:(lq + 1) * 4, :], in_=wstv, func=AF.Copy), reads=[('wst', lq % 2)], writes=['w1b'])
                chk('c01')
                sc.dma('sp', w2f[:, :, :], P['b_cmp_w2'][kvi].rearrange("(c p) d -> p c d", p=128), writes=['w2f'], stream=2)
                sc.op('dve', lambda e: e.tensor_copy(out=w2b[:, :, :], in_=w2f[:, :, :]), reads=['w2f'], writes=['w2b'])
                chk('c02')
                sc.dma('sp', posf[0:32, 0:64].rearrange("p d -> p d") if False else posn[:, :], P['b_cmp_pos'][kvi], writes=['posn'], stream=2)
                sc.op('pe', lambda e: e.matmul(PS[6][0:64, 0:32], lhsT=posn[:, :], rhs=C['identf'][0:32, 0:32], start=True, stop=True), reads=['posn', 'const'], writes=[('ps', 6)])
                sc.op('dve', lambda e: e.tensor_copy(out=posT[:, :], in_=PS[6][0:64, 0:32]), reads=[('ps', 6)], writes=['posT'])
                chk('c0')
                for hc in range(2):
                    for l in range(32):
                        sc.op('pe', lambda e, l=l, hc=hc: e.matmul(PS[6][:, hc:hc + 1], lhsT=w1b[:, l, hc * 128:(hc + 1) * 128], rhs=posT[:, l:l + 1], start=(l == 0), stop=(l == 31)),
                              reads=['w1b', 'posT'], writes=[('ps', 6)])
                sc.op('dve', lambda e: e.tensor_copy(out=biasc[:, :], in_=PS[6][:, 0:2]), reads=[('ps', 6)], writes=['biasc'])
                chk('c1')
                for n4 in range(4):
                    sc.dma('sp', tT[:, :], kTd[kvi].ap()[n4, :, :], reads=['featd'], writes=['tT'], stream=4)
                    tv = tT[:, :].rearrange("p (c r) -> p c r", r=16)
                    for hc in range(2):
                        ph = PS[hc]
                        for l in range(32):
                            sc.op('pe', lambda e, l=l, hc=hc: e.matmul(ph[:, 0:255], lhsT=w1b[:, l, hc * 128:(hc + 1) * 128], rhs=tv[:, l // 16:l // 16 + 255, l % 16], start=(l == 0), stop=(l == 31)),
                                  reads=['w1b', 'tT'], writes=[('ps', hc)])
                        x0, x2, x3, th = xg
                        sc.op('dve', lambda e: e.tensor_scalar(out=x0[:, 0:255], in0=ph[:, 0:255], scalar1=biasc[:, hc:hc + 1], scalar2=None, op0=ALU.add), reads=[('ps', hc), 'biasc'], writes=['x0'])
                        sc.op('dve', lambda e: e.tensor_tensor(out=x2[:, 0:255], in0=x0[:, 0:255], in1=x0[:, 0:255], op=ALU.mult), reads=['x0'], writes=['x2'])
                        sc.op('dve', lambda e: e.tensor_tensor(out=x3[:, 0:255], in0=x2[:, 0:255], in1=x0[:, 0:255], op=ALU.mult), reads=['x0', 'x2'], writes=['x3'])
                        sc.op('dve', lambda e: e.scalar_tensor_tensor(out=x3[:, 0:255], in0=x3[:, 0:255], scalar=0.044715, in1=x0[:, 0:255], op0=ALU.mult, op1=ALU.add), reads=['x0', 'x3'], writes=['x3'])
                        sc.op('act', lambda e: e.activation(out=th[:, 0:255], in_=x3[:, 0:255], func=AF.Tanh, scale=0.7978845608028654), reads=['x3'], writes=['th'])
                        sc.op('dve', lambda e: e.scalar_tensor_tensor(out=th[:, 0:255], in0=th[:, 0:255], scalar=1.0, in1=x0[:, 0:255], op0=ALU.add, op1=ALU.mult), reads=['th', 'x0'], writes=['th'])
                        sc.op('dve', lambda e, hc=hc: e.tensor_scalar(out=gT[:, hc, 0:255], in0=th[:, 0:255], scalar1=0.5, scalar2=None, op0=ALU.mult), reads=['th'], writes=['gT'])
                        chk('c2')
                    if kvi == 0:
                        pk = PS[2]
                        for hc in range(2):
                            sc.op('pe', lambda e, hc=hc: e.matmul(pk[0:64, 0:256], lhsT=w2b[:, hc, :], rhs=gT[:, hc, :], start=(hc == 0), stop=(hc == 1)), reads=['w2b', 'gT'], writes=[('ps', 2)])
                        x0, x2, x3, th = xg
                        sc.op('dve', lambda e: e.tensor_copy(out=x0[0:64, :], in_=pk[0:64, 0:256]), reads=[('ps', 2)], writes=['x0'])
                        sc.op('act', lambda e: e.activation(out=x2[0:64, :], in_=x0[0:64, :], func=AF.Square), reads=['x0'], writes=['x2'])
                        sc.op('pe', lambda e: e.matmul(PS[3][0:64, 0:256], lhsT=cdn['blkf'][0:64, 0:64], rhs=x2[0:64, :], start=True, stop=True), reads=['x2', 'cdn'], writes=[('ps', 3)])
                        sc.op('dve', lambda e: e.tensor_scalar(out=x3[0:64, :], in0=PS[3][0:64, 0:256], scalar1=1.0 / 64, scalar2=EPS, op0=ALU.mult, op1=ALU.add), reads=[('ps', 3)], writes=['x3'])
                        sc.op('act', lambda e: e.activation(out=x3[0:64, :], in_=x3[0:64, :], func=AF.Sqrt), reads=['x3'], writes=['x3'])
                        sc.op('dve', lambda e: e.reciprocal(out=x3[0:64, :], in_=x3[0:64, :]), reads=['x3'], writes=['x3'])
                        sc.op('dve', lambda e, n4=n4: e.scalar_tensor_tensor(out=kcT[:, n4, :], in0=x0[0:64, :], scalar=gcol[0:64, 1:2], in1=x3[0:64, :], op0=ALU.mult, op1=ALU.mult),
                              reads=['x0', 'x3', 'gcol', 'kcT'], writes=['kcT'])
                    else:
                        for ct in range(2):
                            pv = PS[2 + ct]
                            for hc in range(2):
                                sc.op('pe', lambda e, hc=hc, ct=ct: e.matmul(pv[:, 0:64], lhsT=gT[:, hc, ct * 128:(ct + 1) * 128], rhs=w2b[:, hc, :], start=(hc == 0), stop=(hc == 1)),
                                      reads=['w2b', 'gT'], writes=[('ps', 2 + ct)])
                            sc.op('act', lambda e, ct=ct, n4=n4: e.activation(out=vaug[:, ct, n4, 0:64], in_=pv[:, 0:64], func=AF.Copy), reads=[('ps', 2 + ct), 'vaug'], writes=['vaug'])
            sc.op('dve', lambda e: e.memset(kcT[:, :, 255:256], 0.0), reads=['kcT'], writes=['kcT'])
            sc.phase()
        except StopHere:
            sc.phase()
            stopped = True
    k.st = st2
    if stopped or STOP == 'b1c':
        copy_x(k, x_d, xo_d, C)
        k.st = old
        st2.close()
        return
    B = {'Eb': [k.sb([128, 512], F32) for _ in range(2)], 'PT': [k.sb([128, 512], BF16) for _ in range(3)]}
    qTh = k.sb([64, S], BF16)
    kselT = k.sb([64, S], BF16)
    kwinT = k.sb([64, S], BF16)
    V1 = [k.sb([128, NT, 65], BF16) for _ in range(2)]
    oacc = k.sb([128, NT, 4, 64], F32)
    impacc = k.sb([128, NT, 64], F32)
    penT = k.sb([64, S], BF16)
    Bmat = k.sb([64, S], BF16)
    tabm = k.sb([128, 4096 + 128], F32)
    tslab = [k.sb([128, 512], F32) for _ in range(2)]
    smz = k.sb([128, 8], F32)
    imod = k.sb([128, 64], F32)
    iwk = k.sb([128, 64], F32)
    m8 = k.sb([128, 16], F32)
    pen = k.sb([128, 64], F32)
    obst = [k.sb([128, 256], BF16) for _ in range(2)]
    keepT = k.sb([128, 128], F32)
    addT = k.sb([128, 128], F32)
    sc.dma('sp', keepT[:, :], cdn['keepT'][:, :], writes=['keepT'], stream=2)
    sc.dma('sp', addT[:, :], cdn['addT'][:, :], writes=['keepT'], stream=2)
    bst = tabm[0:64, 0:S]
    sc.dma('sp', bst, cdn['Bmat'][:, :], writes=['tabm'], stream=2)
    sc.op('dve', lambda e: e.tensor_copy(out=Bmat[:, :], in_=bst), reads=['tabm'], writes=['Bmat'])
    for vi in range(2):
        sc.op('pool', lambda e, vi=vi: e.memset(V1[vi][:, :, 64:65], 1.0), writes=[('V1', vi)])
    for n4 in range(4):
        sc.dma('sp', kselT[:, :], kTd[2].ap()[n4, :, :], reads=['featd'], writes=['kselT'], stream=4)
        sc.dma('sp', kwinT[:, :], kTd[3].ap()[n4, :, :], reads=['featd'], writes=['kwinT'], stream=4)
        for vi in range(2):
            sc.dma('sp', V1[vi][:, :, 0:64], Vd[vi].ap().rearrange("(n p) h d -> p n h d", p=128)[:, :, n4, :], reads=['Vd'], writes=[('V1', vi)], stream=4)
        for g in range(4):
            hq = 4 * n4 + g
            sc.dma('sp', qTh[:, :], qTd.ap()[hq, :, :], reads=['featd'], writes=['qTh'], stream=5)
            build_head_tab(k, C, ohC, 6144, tabC, hq, relh, ohs, tslab, 'tabC')
            slabs = {}

            def tab_c(kt, q0, nq, hq=hq):
                key = (kt, q0)
                j = len(slabs) % 2
                slabs[key] = j
                dpp = q0 - (2048 * kt + 31)
                sc.dma('sp', tslab[j][:, 0:nq], AP(tabC.h, tabC.off + 2064 + dpp, [[WC - 16, 128], [1, nq]]), reads=['tabC'], writes=[('tslab', j)], stream=8)
                return tslab[j][:, 0:nq], ('tslab', j)

            def evac_c(qc, acc, g=g, hq=hq):
                for qb in range(4):
                    qt = qc * 4 + qb
                    a, ares = acc(qb)
                    sc.op('dve', lambda e: e.tensor_scalar(out=smz[:, 0:1], in0=a[:, 64:65], scalar1=1e-30, scalar2=None, op0=ALU.max), reads=[ares], writes=['smz'])
                    sc.op('dve', lambda e: e.reciprocal(out=smz[:, 1:2], in_=smz[:, 0:1]), reads=['smz'], writes=['smz1'])
                    sc.op('dve', lambda e: e.tensor_tensor(out=smz[:, 2:3], in0=smz[:, 1:2], in1=gates[:, qt, hq * 3:hq * 3 + 1], op=ALU.mult), reads=['smz1', 'gates'], writes=['smz2'])
                    sc.op('dve', lambda e: e.tensor_scalar(out=oacc[:, qt, g, :], in0=a[:, 0:64], scalar1=smz[:, 2:3], scalar2=(1.0 if 'c' in KBR else 0.0), op0=ALU.mult, op1=ALU.mult), reads=[ares, 'smz2'], writes=[('oacc', qt)])
                    if g == 0:
                        sc.op('dve', lambda e: e.tensor_scalar(out=impacc[:, qt, :], in0=a[:, 65:129], scalar1=smz[:, 1:2], scalar2=None, op0=ALU.mult), reads=[ares, 'smz1'], writes=[('imp', qt)])
                    else:
                        sc.op('dve', lambda e: e.scalar_tensor_tensor(out=impacc[:, qt, :], in0=a[:, 65:129], scalar=smz[:, 1:2], in1=impacc[:, qt, :], op0=ALU.mult, op1=ALU.add),
                              reads=[ares, 'smz1', ('imp', qt)], writes=[('imp', qt)])

            dense_attn(k, C, B, qTh,
                       lhsT_fn=lambda kt, n4=n4: (kcT[:, n4, kt * 128:(kt + 1) * 128], 'kcT'),
                       vaug_fn=lambda kt, n4=n4: (vaug[:, kt, n4, :], ['vaug']),
                       ncol=129, tab_fn=tab_c,
                       kts_fn=lambda qc: [0] if qc < 4 else [0, 1],
                       qb0_fn=lambda qc, kt: 0,
                       last_fn=lambda qc, qb: 0 if qc < 4 else 1,
                       evac_fn=evac_c)
        if STOP == 'b2':
            continue
        sc.op('pool', lambda e: e.memset(penT[:, 0:1024], 0.0), reads=['penT'], writes=['penT'])
        for qt in range(8, NT):
            ks = slice(64 - 2 * qt, 128 - 2 * qt)
            sc.op('dve', lambda e: e.tensor_tensor(out=imod[:, :], in0=impacc[:, qt, :], in1=keepT[:, ks], op=ALU.mult), reads=[('imp', qt), 'keepT'], writes=['imod'])
            sc.op('dve', lambda e: e.tensor_tensor(out=imod[:, :], in0=imod[:, :], in1=addT[:, ks], op=ALU.add), reads=['imod', 'keepT'], writes=['imod'])
            sc.op('dve', lambda e: e.memset(imod[:, 0:1], 3e9), reads=['imod'], writes=['imod'])
            sc.op('dve', lambda e: e.max(out=m8[:, 0:8], in_=imod[:, :]), reads=['imod'], writes=['m8'])
            sc.op('dve', lambda e: e.match_replace(out=iwk[:, :], in_to_replace=m8[:, 0:8], in_values=imod[:, :], imm_value=-3e38), reads=['imod', 'm8'], writes=['iwk'])
            sc.op('dve', lambda e: e.max(out=m8[:, 8:16], in_=iwk[:, :]), reads=['iwk', 'm8'], writes=['m8b'])
            sc.op('dve', lambda e: e.tensor_reduce(out=smz[:, 4:5], in_=m8[:, 8:16], axis=AX.X, op=ALU.min), reads=['m8b'], writes=['smz4'])
            sc.op('dve', lambda e: e.tensor_scalar(out=pen[:, :], in0=imod[:, :], scalar1=smz[:, 4:5], scalar2=None, op0=ALU.is_ge), reads=['imod', 'smz4'], writes=['pen'])
            sc.op('dve', lambda e: e.tensor_scalar(out=pen[:, :], in0=pen[:, :], scalar1=-1.0, scalar2=30000.0, op0=ALU.add, op1=ALU.mult), reads=['pen'], writes=['pen'])
            sc.op('pe', lambda e: e.matmul(PS[6][0:64, 0:128], lhsT=pen[:, :], rhs=C['identf'][:, :], start=True, stop=True), reads=['pen', 'const'], writes=[('ps', 6)])
            sc.op('act', lambda e, qt=qt: e.activation(out=penT[:, qt * 128:(qt + 1) * 128], in_=PS[6][0:64, 0:128], func=AF.Copy), reads=[('ps', 6)], writes=['penT'])
        for g in range(4):
            hq = 4 * n4 + g
            sc.dma('sp', qTh[:, :], qTd.ap()[hq, :, :], reads=['featd'], writes=['qTh'], stream=5)
            build_head_tab(k, C, ohD, 4224, tabD, hq, relh, ohs, tslab, 'tabD')
            build_head_tab(k, C, ohW, 1280, tabW, hq, relh, ohs, tslab, 'tabW')
            for br in (1, 2):
                if br == 1:
                    sc.dma('sp', tabm[:, 0:4096], AP(tabD.h, tabD.off + 127, [[WD - 1, 128], [1, 4096]]), reads=['tabD'], writes=['tabm'], stream=8)
                else:
                    sc.dma('sp', tabm[:, 0:1152], AP(tabW.h, tabW.off + 127, [[WW - 1, 128], [1, 1152]]), reads=['tabW'], writes=['tabm'], stream=8)

                def evac_sw(qc, acc, g=g, hq=hq, br=br):
                    for qb in range(4):
                        qt = qc * 4 + qb
                        a, ares = acc(qb)
                        sc.op('dve', lambda e: e.reciprocal(out=smz[:, 1:2], in_=a[:, 64:65]), reads=[ares], writes=['smz1'])
                        sc.op('dve', lambda e: e.scalar_tensor_tensor(out=smz[:, 2:3], in0=smz[:, 1:2], scalar=(1.0 if 'csw'[br] in KBR else 0.0), in1=gates[:, qt, hq * 3 + br:hq * 3 + br + 1], op0=ALU.mult, op1=ALU.mult), reads=['smz1', 'gates'], writes=['smz2'])
                        sc.op('dve', lambda e: e.scalar_tensor_tensor(out=oacc[:, qt, g, :], in0=a[:, 0:64], scalar=smz[:, 2:3], in1=oacc[:, qt, g, :], op0=ALU.mult, op1=ALU.add),
                              reads=[ares, 'smz2', ('oacc', qt)], writes=[('oacc', qt)])

                ksrc, kres, vsrc = (kselT, 'kselT', 0) if br == 1 else (kwinT, 'kwinT', 1)
                dense_attn(k, C, B, qTh,
                           lhsT_fn=lambda kt, ksrc=ksrc, kres=kres: (ksrc[:, kt * 128:(kt + 1) * 128], kres),
                           vaug_fn=lambda kt, vsrc=vsrc: (V1[vsrc][:, kt, :], [('V1', vsrc)]),
                           ncol=65,
                           tab_fn=lambda kt, q0, nq: (tabm[:, q0 - kt * 128:q0 - kt * 128 + nq], 'tabm'),
                           kts_fn=(lambda qc: list(range(0, 4 * qc + 4))) if br == 1 else (lambda qc: list(range(max(0, 4 * qc - 4), 4 * qc + 4))),
                           qb0_fn=lambda qc, kt: max(0, kt - 4 * qc),
                           last_fn=lambda qc, qb: 4 * qc + qb,
                           evac_fn=evac_sw,
                           pen_fn=(lambda kt, q0, nq: (Bmat[:, kt * 128:(kt + 1) * 128], penT[:, q0:q0 + nq], ['Bmat', 'penT'])) if br == 1 else None)
        for qt in range(NT):
            b = qt % 2
            sc.op('act', lambda e: e.activation(out=obst[b][:, :], in_=oacc[:, qt, :, :].rearrange("p g d -> p (g d)"), func=AF.Copy), reads=[('oacc', qt)], writes=[('obst', b)])
            sc.dma('pool', obd.ap()[qt * 128:(qt + 1) * 128, n4 * 256:(n4 + 1) * 256], obst[b][:, :], reads=[('obst', b)], writes=['obd'], stream=6)
        sc.phase()
    obt = [k.sb([128, D], BF16) for _ in range(2)]
    for n in range(NT):
        b = n % 2
        sc.dma('sp', obt[b][:, :], obd.ap()[n * 128:(n + 1) * 128, :], reads=['obd'], writes=[('obt', b)], stream=5)
        out_tail(k, C, obt[b][:, :], wob, oT, x_d, xo_d, n, ('obt', b))
    sc.phase()
    k.st = old
    st2.close()


def alloc_common(k):
    C = {}
    C['xt'] = [k.sb([128, 1024], F32) for _ in range(2)]
    C['sq'] = [k.sb([128, 1024], F32) for _ in range(2)]
    C['xb'] = [k.sb([128, 1024], BF16) for _ in range(2)]
    C['ss'] = [k.sb([128, 4], F32) for _ in range(2)]
    C['identb'] = k.sb([128, 128], BF16)
    C['identf'] = k.sb([128, 128], F32)
    C['gainT'] = k.sb([128, 8], F32)
    C['wst'] = [k.sb([128, 8, 128], F32) for _ in range(2)]
    C['wst2'] = [k.sb([128, 8, 128], F32) for _ in range(2)]
    C['wbf'] = [k.sb([128, 8, 128], BF16) for _ in range(2)]
    C['wbf2'] = [k.sb([128, 8, 128], BF16) for _ in range(2)]
    C['blk'] = k.sb([128, 128], BF16)
    C['onesb'] = k.sb([128, 128], BF16)
    C['relb'] = k.sb([33, 16], F32)
    return C


MIX_PARAMS = {
    0: ['norm1', 'a_w_in', 'a_q_gain', 'a_k_gain', 'a_w_out'],
    1: ['norm1', 'b_w_in', 'b_q_gain', 'b_k_gain', 'b_cmp_pos', 'b_cmp_w1', 'b_cmp_w2', 'b_w_out'],
    2: ['norm1', 'c_w_in', 'c_conv_w', 'c_a_log', 'c_dt_bias', 'c_out_gain', 'c_w_out'],
}
SHAPES = {
    'norm1': [D], 'norm2': [D], 'ffn_w_gate': [D, FF], 'ffn_w_up': [D, FF], 'ffn_w_down': [FF, D],
    'a_w_in': [D, 9216], 'a_q_gain': [3, 64], 'a_k_gain': [3, 64], 'a_w_out': [D, D],
    'b_w_in': [D, 2608], 'b_q_gain': [64], 'b_k_gain': [3, 64], 'b_cmp_pos': [2, 32, 64], 'b_cmp_w1': [2, 2048, 256],
    'b_cmp_w2': [2, 256, 64], 'b_w_out': [D, D],
    'c_w_in': [D, 4112], 'c_conv_w': [4, 3072], 'c_a_log': [8], 'c_dt_bias': [8], 'c_out_gain': [128], 'c_w_out': [D, D],
}


def consts():
    c = {'identf': np.eye(128, dtype=np.float32)}
    blk = np.zeros((128, 128), np.float32)
    blk[:64, :64] = 1
    blk[64:, 64:] = 1
    c['blkf'] = blk
    for g, (window, dil) in enumerate(A_GROUPS):
        d = np.arange(384) - 127
        c[f'ohA{g}'] = onehot_rows(d * dil, (d >= 0) & (d <= 128))
    i = np.arange(6144) - 2064
    c['ohC'] = onehot_rows(i, i >= 0)
    i = np.arange(4224) - 127
    c['ohD'] = onehot_rows(i, i >= 0)
    i = np.arange(1280) - 127
    c['ohW'] = onehot_rows(i, (i >= 0) & (i <= 511))
    cs_ = np.arange(256)[:, None] * 16
    ss_ = np.arange(64)[None, :] * 64
    c['cM'] = ((cs_ < ss_ + 64) & (cs_ + 32 > ss_) & (np.arange(256)[:, None] < 255)).astype(np.float32)
    qi = np.arange(128)[:, None]
    half = (qi >= 64).astype(np.int64)
    dl = np.arange(128)[None, :] - 64
    c['ckeepT'] = (dl < half - 1).astype(np.float32)
    c['caddT'] = np.where(dl == half, 2e9, np.where(dl == half - 1, 1e9, np.where(dl > half, -1e30, 0.0))).astype(np.float32)
    c['cBmat'] = (np.arange(4096)[None, :] // 64 == np.arange(64)[:, None]).astype(np.float32)
    p = np.arange(128)
    same = (p[:, None] // 64) == (p[None, :] // 64)
    c['cTB'] = (same & (p[:, None] <= p[None, :])).astype(np.float32)
    c['cBLK'] = same.astype(np.float32)
    c['cH0'] = np.repeat((p < 64).astype(np.float32)[:, None], 128, 1)
    c['cH1'] = np.repeat((p >= 64).astype(np.float32)[:, None], 128, 1)
    c['cONES'] = np.ones((128, 128), np.float32)
    c['cMT'] = np.where(same & (p[None, :] >= p[:, None]), 0.0, -1e30).astype(np.float32)
    c['cMBS'] = np.where(same & (p[None, :] < p[:, None]), 0.0, 1e30).astype(np.float32)
    return c


LAST_INPUT_NAMES = []


def build(layers=(0, 1, 2, 3), parts=('mix', 'ffn')):
    nc = bass.Bass("TRN2", target_bir_lowering=False)
    dr = {}
    LAST_INPUT_NAMES.clear()

    def din(name, shape):
        dr[name] = nc.dram_tensor(name, list(shape), F32, kind="ExternalInput").ap()
        LAST_INPUT_NAMES.append(name)
        return dr[name]

    x_in = din('x', [S, D])
    din('rel_bias', [32, 16])
    cs = consts()
    for name, v in cs.items():
        din(name, v.shape)
    for l in layers:
        p = f'l{l}_'
        if 'mix' in parts:
            for nm in MIX_PARAMS[l % 3]:
                din(p + nm, SHAPES[nm])
        if 'ffn' in parts:
            for nm in ('norm2', 'ffn_w_gate', 'ffn_w_up', 'ffn_w_down'):
                din(p + nm, SHAPES[nm])
    out = nc.dram_tensor('out', [S, D], F32, kind="ExternalOutput").ap()
    with ExitStack() as st:
        sc = Sched(nc, st)
        k = K(nc, st, sc)
        C = alloc_common(k)
        sc.dma('sp', C['identf'][:, :], dr['identf'][:, :], writes=['const'], stream=2)
        sc.op('dve', lambda e: e.tensor_copy(out=C['identb'][:, :], in_=C['identf'][:, :]), reads=['const'], writes=['const'])
        sc.dma('sp', C['identf'][:, :], dr['blkf'][:, :], reads=['const'], writes=['const'], stream=2)
        sc.op('dve', lambda e: e.tensor_copy(out=C['blk'][:, :], in_=C['identf'][:, :]), reads=['const'], writes=['const'])
        sc.dma('sp', C['identf'][:, :], dr['identf'][:, :], reads=['const'], writes=['const'], stream=2)
        sc.op('pool', lambda e: e.memset(C['relb'][:, :], -30000.0), writes=['relb'])
        sc.dma('sp', C['relb'][0:32, :], dr['rel_bias'][:, :], reads=['relb'], writes=['relb'], stream=2)
        cd = {}
        for nm in ('TB', 'BLK', 'H0', 'H1', 'ONES', 'MT', 'MBS'):
            cd[nm] = k.sb([128, 128], F32)
            sc.dma('sp', cd[nm][:, :], dr['c' + nm][:, :], writes=['cd'], stream=2)
        sc.op('dve', lambda e: e.tensor_copy(out=C['onesb'][:, :], in_=cd['ONES'][:, :]), reads=['cd'], writes=['const'])
        tabA = None
        kinds = set(l % 3 for l in layers) if 'mix' in parts else set()
        with ExitStack() as stt:
            k.st = stt
            C['oh'] = k.sb([33, 512], F32)
            C['relrep'] = k.sb([33, 16, 128], F32)
            C['rowst'] = [k.sb([128, 512], F32) for _ in range(2)]
            sc.op('dve', lambda e: e.tensor_copy(out=C['relrep'][:, :, :], in_=bc(C['relb'][:, :].unsqueeze(2), [33, 16, 128])), reads=['relb'], writes=['relrep'])
            if 0 in kinds:
                tabA = [k.dram([16, 128, 384]) for _ in range(3)]
                for g in range(3):
                    build_bias_rows(k, C, dr[f'ohA{g}'], 384, tabA[g])
            tabs = (dr['ohC'], dr['ohD'], dr['ohW'])
            sc.phase()
        k.st = st
        k.doff += int(os.environ.get('KDB', 0))
        k.dbase = k.doff
        cdn = {'M': dr['cM'], 'keepT': dr['ckeepT'], 'addT': dr['caddT'], 'Bmat': dr['cBmat'], 'blkf': cd['BLK']}
        cur = x_in
        if os.environ.get('KINPLACE'):
            copy_x(k, x_in, out, C)
            cur = out
        for l in layers:
            p = f'l{l}_'
            P = {nm[len(p):]: ap for nm, ap in dr.items() if nm.startswith(p)}
            if 'mix' in parts:
                if l % 3 == 0:
                    mixer_a(k, cur, out, P, C, tabA)
                elif l % 3 == 2:
                    mixer_c(k, cur, out, P, C, cd)
                else:
                    mixer_b(k, cur, out, P, C, tabs, cdn)
                cur = out
            if 'ffn' in parts and not (os.environ.get('KLASTMIX') and l == layers[-1]):
                ffn_layer(k, cur, out, P['norm2'], P['ffn_w_gate'], P['ffn_w_up'], P['ffn_w_down'], C)
                cur = out
    return nc


_NC_CACHE = {}


def kernel(**inputs):
    if 'nc' not in _NC_CACHE:
        _NC_CACHE['nc'] = build()
        _NC_CACHE['names'] = list(LAST_INPUT_NAMES)
    nc = _NC_CACHE['nc']
    cs = consts()
    x = np.ascontiguousarray(np.asarray(inputs['x'], dtype=np.float32))
    shared = {}
    for nm in _NC_CACHE['names']:
        if nm == 'x':
            continue
        if nm in cs:
            shared[nm] = cs[nm]
        else:
            shared[nm] = np.ascontiguousarray(np.asarray(inputs[nm], dtype=np.float32))
    in_maps = []
    for b in range(8):
        m = dict(shared)
        m['x'] = x[b]
        in_maps.append(m)
    res = run_bass_kernel_spmd(nc, in_maps, core_ids=list(range(8)))
    return np.stack([np.asarray(r['out'], dtype=np.float32) for r in res.results], axis=0)
```

```python
import os
import numpy as np
from contextlib import ExitStack
import concourse.bass as bass
import concourse.mybir as mybir
from concourse.ap import AP
from concourse.bass_utils import run_bass_kernel_spmd

F32, BF16 = mybir.dt.float32, mybir.dt.bfloat16
ALU, AF, AX = mybir.AluOpType, mybir.ActivationFunctionType, mybir.AxisListType

S = 4096
D = 1024
NT = S // 128
FF = 2816
NDMA = 32
EPS = 1e-6
PHASE_LIMIT = 20000


class Sched:
    CE = ('pe', 'act', 'dve', 'pool')

    def __init__(self, nc, st):
        self.nc = nc
        self.e = {'pe': nc.tensor, 'act': nc.scalar, 'dve': nc.vector, 'pool': nc.gpsimd, 'sp': nc.sync}
        self.sets = []
        for s in range(2):
            d = {k: st.enter_context(nc.semaphore(f"s{s}_{k}")) for k in self.CE}
            self.sets.append(d)
        self.dsem = {('d', i): st.enter_context(nc.semaphore(f"dq{i}")) for i in range(NDMA)}
        self.dcnt = [0] * NDMA
        self.skey = {}
        self.spsem = st.enter_context(nc.semaphore('spg'))
        self.phase_no = 0
        self.cur = 0
        self._reset()

    def _reset(self):
        self.cnt = {k: 0 for k in self.CE}
        self.lastw = {}
        self.readers = {}
        self.seen = {k: {} for k in self.CE + ('sp',)}

    def _sem(self, k):
        return self.dsem[k] if isinstance(k, tuple) else self.sets[self.cur][k]

    def _wait(self, eng, reads, writes):
        need = {}
        for r in reads:
            t = self.lastw.get(r)
            if t:
                need[t[0]] = max(need.get(t[0], 0), t[1])
        for w in writes:
            t = self.lastw.get(w)
            if t:
                need[t[0]] = max(need.get(t[0], 0), t[1])
            for k, v in self.readers.get(w, {}).items():
                need[k] = max(need.get(k, 0), v)
        for k, v in need.items():
            if k == 'pe' and eng == 'pe':
                continue
            if isinstance(k, tuple):
                v = 16 * self.dcnt[k[1]]
            if self.seen[eng].get(k, 0) < v:
                self.e[eng].wait_ge(self._sem(k), v)
                self.seen[eng][k] = v

    def _commit(self, tok, reads, writes):
        for r in reads:
            d = self.readers.setdefault(r, {})
            d[tok[0]] = max(d.get(tok[0], 0), tok[1])
        for w in writes:
            self.lastw[w] = tok
            self.readers[w] = {}

    def op(self, eng, fn, reads=(), writes=()):
        self._wait(eng, reads, writes)
        self.cnt[eng] += 1
        fn(self.e[eng]).then_inc(self.sets[self.cur][eng], 1)
        self._commit((eng, self.cnt[eng]), reads, writes)

    def dma(self, q, out, in_, reads=(), writes=(), stream=None, **kw):
        key = reads[0] if (reads and (not writes or not isinstance(writes[0], tuple)) and isinstance(reads[0], tuple)) else (writes[0] if writes else reads[0])
        if key not in self.skey:
            self.skey[key] = len(self.skey) % NDMA
        stream = self.skey[key]
        self._wait(q, reads, writes)
        self.dcnt[stream] += 1
        self.e[q].dma_start(out=out, in_=in_, **kw).then_inc(self.dsem[('d', stream)], 16)
        self._commit((('d', stream), 16 * self.dcnt[stream]), reads, writes)

    def maybe_phase(self):
        if max(self.cnt.values()) > PHASE_LIMIT:
            self.phase()

    def phase(self):
        A = self.sets[self.cur]
        B = self.sets[1 - self.cur]
        sp = self.e['sp']
        for k in self.CE:
            if self.cnt[k]:
                sp.wait_ge(A[k], self.cnt[k])
        for i, c in enumerate(self.dcnt):
            if c:
                sp.wait_ge(self.dsem[('d', i)], 16 * c)
        for k in self.CE:
            sp.sem_clear(B[k])
        self.phase_no += 1
        sp.sem_inc(self.spsem, 1)
        for k in self.CE:
            self.e[k].wait_ge(self.spsem, self.phase_no)
        self.cur = 1 - self.cur
        self._reset()


def bc(ap, shape):
    return ap.to_broadcast(list(shape))


POOL_ELEMS = 15200000


class DT:
    def __init__(self, h, off, shape, dt, nf):
        self.h, self.off, self.shape, self.dt, self.nf = h, off, shape, dt, nf

    def ap(self):
        flat = AP(self.h, self.off, [[1, self.nf]])
        if self.dt != F32:
            flat = flat.bitcast(self.dt)
        names = [f"d{i}" for i in range(len(self.shape))]
        return flat.rearrange("(" + " ".join(names) + ") -> " + " ".join(names), **{nm: sz for nm, sz in zip(names[1:], self.shape[1:])})


class K:
    def __init__(self, nc, st, sc):
        self.nc, self.st, self.sc = nc, st, sc
        self.ps = [st.enter_context(nc.psum_tensor(f"ps{i}", [128, 512], F32)) for i in range(7)]
        self.psb = st.enter_context(nc.psum_tensor("psb", [128, 1024], BF16))
        self.n_sb = 0
        self.pool = nc.dram_tensor('pool', [POOL_ELEMS], F32, kind='Internal')
        self.doff = 0

    def sb(self, shape, dt, name=None):
        self.n_sb += 1
        return self.st.enter_context(self.nc.sbuf_tensor(name or f"t{self.n_sb}", list(shape), dt))

    def dram(self, shape, dt=F32):
        n = int(np.prod(shape))
        nf = n if dt == F32 else (n + 1) // 2
        d = DT(self.pool, self.doff, list(shape), dt, nf)
        self.doff += (nf + 63) // 64 * 64
        assert self.doff <= POOL_ELEMS, self.doff
        return d


def load_w_chunk(k, W, col0, ncols, wst, wbf, gainT, rs_st, rs_bf, kc_n=8, stream=1):
    sc = k.sc
    src = W.rearrange("(kc p) n -> p kc n", p=128)[:, :, col0:col0 + ncols]
    sc.dma('sp', wst[:, 0:kc_n, 0:ncols], src, writes=[rs_st], stream=stream)
    if gainT is None:
        sc.op('dve', lambda e: e.tensor_copy(out=wbf[:, 0:kc_n, 0:ncols], in_=wst[:, 0:kc_n, 0:ncols]),
              reads=[rs_st], writes=[rs_bf])
    else:
        sc.op('dve', lambda e: e.tensor_tensor(out=wbf[:, 0:kc_n, 0:ncols], in0=wst[:, 0:kc_n, 0:ncols],
                                               in1=bc(gainT[:, 0:kc_n].unsqueeze(2), [128, kc_n, ncols]), op=ALU.mult),
              reads=[rs_st, 'gain'], writes=[rs_bf])


def load_gainT(k, g_dram, gainT):
    k.sc.dma('sp', gainT[:, :], g_dram.rearrange("(kc p) -> p kc", p=128), writes=['gain'], stream=2,
             allow_slow_non_contiguous=True)


def norm_tiles(k, x_d, hT, tiles, tok_of_tile, C):
    sc = k.sc
    for slot, n in enumerate(tiles):
        b = slot % 2
        xt, sq, ss, xb = C['xt'][b], C['sq'][b], C['ss'][b], C['xb'][b]
        sc.dma('sp', xt[:, :], x_d[n * 128:(n + 1) * 128, :], writes=[('xt', b)], stream=0)
        sc.op('act', lambda e: e.activation(out=sq[:, :], in_=xt[:, :], func=AF.Square), reads=[('xt', b)], writes=[('sq', b)])
        sc.op('dve', lambda e: e.reduce_sum(out=ss[:, 0:1], in_=sq[:, :], axis=AX.X), reads=[('sq', b)], writes=[('ss', b)])
        sc.op('dve', lambda e: e.tensor_scalar(out=ss[:, 0:1], in0=ss[:, 0:1], scalar1=1.0 / D, scalar2=EPS, op0=ALU.mult, op1=ALU.add),
              reads=[('ss', b)], writes=[('ss', b)])
        sc.op('act', lambda e: e.activation(out=ss[:, 1:2], in_=ss[:, 0:1], func=AF.Sqrt), reads=[('ss', b)], writes=[('ss2', b)])
        sc.op('dve', lambda e: e.reciprocal(out=ss[:, 2:3], in_=ss[:, 1:2]), reads=[('ss2', b)], writes=[('ss3', b)])
        sc.op('act', lambda e: e.activation(out=xb[:, :], in_=xt[:, :], func=AF.Copy, scale=ss[:, 2:3]),
              reads=[('xt', b), ('ss3', b)], writes=[('xb', b)])
        for kc in range(8):
            sc.op('pe', lambda e, kc=kc: e.transpose(out=k.psb[:, kc * 128:(kc + 1) * 128], in_=xb[:, kc * 128:(kc + 1) * 128],
                                                      identity=C['identb'][:, :]),
                  reads=[('xb', b), 'const'], writes=['psb'])
        sc.op('dve', lambda e: e.tensor_copy(out=hT[:, :, slot * 128:(slot + 1) * 128],
                                             in_=k.psb[:, :].rearrange("p (kc t) -> p kc t", kc=8)),
              reads=['psb'], writes=[('hT', slot)])


def ffn_layer(k, x_d, xo_d, norm2, wg, wu, wd, C):
    sc = k.sc
    st2 = ExitStack()
    old = k.st
    k.st = st2
    hT = k.sb([128, 8, 1024], BF16)
    actT = k.sb([128, 22, 1024], BF16)
    wdb = k.sb([128, 22, 1024], BF16)
    C['sg'] = [k.sb([128, 512], F32) for _ in range(2)]
    load_gainT(k, norm2, C['gainT'])
    for hc in range(22):
        b = hc % 2
        sc.dma('sp', C['wst'][b][:, 0:8, :].rearrange("p a b -> p (a b)"), wd[hc * 128:(hc + 1) * 128, :], writes=[('wst', b)], stream=1)
        sc.op('act', lambda e, hc=hc, b=b: e.activation(out=wdb[:, hc, :], in_=C['wst'][b][:, 0:8, :].rearrange("p a b -> p (a b)"), func=AF.Copy),
              reads=[('wst', b)], writes=['wdb'])
    for tb in range(4):
        norm_tiles(k, x_d, hT, range(tb * 8, tb * 8 + 8), None, C)
        hres = [('hT', s) for s in range(8)]
        for hc in range(22):
            b = hc % 2
            load_w_chunk(k, wg, hc * 128, 128, C['wst'][b], C['wbf'][b], C['gainT'], ('wst', b), ('wbf', b))
            load_w_chunk(k, wu, hc * 128, 128, C['wst2'][b], C['wbf2'][b], C['gainT'], ('wst2', b), ('wbf2', b))
            for sb in range(2):
                pg, pu = k.ps[sb * 2], k.ps[sb * 2 + 1]
                for kc in range(8):
                    sc.op('pe', lambda e, kc=kc: e.matmul(pg[:, :], lhsT=C['wbf'][b][:, kc, :], rhs=hT[:, kc, sb * 512:(sb + 1) * 512],
                                                         start=(kc == 0), stop=(kc == 7)),
                          reads=[('wbf', b)] + hres[sb * 4:sb * 4 + 4], writes=[('ps', sb * 2)])
                for kc in range(8):
                    sc.op('pe', lambda e, kc=kc: e.matmul(pu[:, :], lhsT=C['wbf2'][b][:, kc, :], rhs=hT[:, kc, sb * 512:(sb + 1) * 512],
                                                         start=(kc == 0), stop=(kc == 7)),
                          reads=[('wbf2', b)] + hres[sb * 4:sb * 4 + 4], writes=[('ps', sb * 2 + 1)])
                sg = C['sg'][sb]
                sc.op('act', lambda e: e.activation(out=sg[:, :], in_=pg[:, :], func=AF.Silu), reads=[('ps', sb * 2)], writes=[('sg', sb)])
                sc.op('dve', lambda e: e.tensor_tensor(out=actT[:, hc, sb * 512:(sb + 1) * 512], in0=pu[:, :], in1=sg[:, :], op=ALU.mult),
                      reads=[('ps', sb * 2 + 1), ('sg', sb)], writes=[('actT', sb)])
        for tt in range(8):
            n = tb * 8 + tt
            b = tt % 2
            xt = C['xt'][b]
            sc.dma('sp', xt[:, :], x_d[n * 128:(n + 1) * 128, :], writes=[('xt', b)], stream=0)
            for half in range(2):
                po = k.ps[4 + half]
                for hc in range(22):
                    sc.op('pe', lambda e, hc=hc: e.matmul(po[:, :], lhsT=actT[:, hc, tt * 128:(tt + 1) * 128], rhs=wdb[:, hc, half * 512:(half + 1) * 512],
                                                         start=(hc == 0), stop=(hc == 21)),
                          reads=[('actT', tt // 4), 'wdb'], writes=[('ps', 4 + half)])
                sc.op('dve', lambda e: e.tensor_tensor(out=xt[:, half * 512:(half + 1) * 512], in0=po[:, :], in1=xt[:, half * 512:(half + 1) * 512], op=ALU.add),
                      reads=[('ps', 4 + half), ('xt', b)], writes=[('xt', b)])
            sc.dma('pool', xo_d[n * 128:(n + 1) * 128, :], xt[:, :], reads=[('xt', b)], writes=[], stream=3)
        sc.maybe_phase()
    sc.phase()
    k.st = old
    st2.close()


A_GROUPS = ((128, 1), (512, 4), (2048, 16))


def rel_bucket_np(dist):
    dist = np.maximum(dist, 0)
    d_f = np.maximum(dist, 1).astype(np.float32)
    large = 16 + (np.log(d_f / np.float32(16)) / np.float32(np.log(2048 / 16)) * np.float32(16)).astype(np.int32)
    return np.where(dist < 16, dist, np.minimum(large, 31))


def onehot_rows(dists, valid):
    n = len(dists)
    oh = np.zeros((33, n), np.float32)
    b = rel_bucket_np(np.asarray(dists))
    for i in range(n):
        oh[b[i] if valid[i] else 32, i] = 1.0
    return oh


def build_bias_rows(k, C, oh_d, width, tab_d):
    sc = k.sc
    u = 0
    for c0 in range(0, width, 512):
        n = min(512, width - c0)
        sc.dma('sp', C['oh'][:, 0:n], oh_d[:, c0:c0 + n], writes=['oh'], stream=2)
        for h in range(16):
            b = u % 2
            u += 1
            sc.op('pe', lambda e, h=h, b=b: e.matmul(k.ps[b][:, 0:n], lhsT=C['relrep'][:, h, :], rhs=C['oh'][:, 0:n], start=True, stop=True),
                  reads=['oh', 'relrep'], writes=[('ps', b)])
            sc.op('act', lambda e, b=b: e.activation(out=C['rowst'][b][:, 0:n], in_=k.ps[b][:, 0:n], func=AF.Exp), reads=[('ps', b)], writes=[('rowst', b)])
            sc.dma('pool', tab_d.ap()[h, :, c0:c0 + n], C['rowst'][b][:, 0:n], reads=[('rowst', b)], writes=['tab'], stream=4)


def skew_tab(tab_d, h, W, off, pstride, n):
    return AP(tab_d.h, tab_d.off + h * 128 * W + off, [[W - pstride, 128], [1, n]])


def mixer_a(k, x_d, xo_d, P, C, tabA):
    sc = k.sc
    st2 = ExitStack()
    old = k.st
    k.st = st2
    k.doff = k.dbase
    hT = k.sb([128, 8, S], BF16)
    qT = k.sb([128, S], BF16)
    kT = k.sb([128, S], BF16)
    V1 = k.sb([128, NT, 2, 65], BF16)
    OZ = k.sb([128, NT, 2, 65], F32)
    wob = k.sb([128, 8, D], BF16)
    gcol = k.sb([128, 6], F32)
    r1 = [k.sb([128, 512], F32) for _ in range(2)]
    sqb = [k.sb([128, 512], BF16) for _ in range(2)]
    Eb = [k.sb([128, 256], F32) for _ in range(3)]
    PT = [k.sb([128, 256], BF16) for _ in range(4)]
    tab = [k.sb([128, 256], F32) for _ in range(2)]
    mrg = [k.sb([128, 16, 65], F32) for _ in range(3)]
    rz = k.sb([128, 16], F32)
    ob = k.sb([128, 16, 64], BF16)
    oT = k.sb([128, 8, 128], BF16)
    ozd = [k.dram([S, 16 * 65]) for _ in range(3)]
    load_gainT(k, P['norm1'], C['gainT'])
    for half in range(2):
        sc.dma('sp', gcol[half * 64:(half + 1) * 64, 0:3], P['a_q_gain'].rearrange("g d -> d g"), writes=['gcol'], stream=2, allow_slow_non_contiguous=True)
        sc.dma('sp', gcol[half * 64:(half + 1) * 64, 3:6], P['a_k_gain'].rearrange("g d -> d g"), writes=['gcol'], stream=2, allow_slow_non_contiguous=True)
    sc.op('pool', lambda e: e.memset(V1[:, :, :, 64:65], 1.0), writes=['V1ones'])
    for kc in range(8):
        b = kc % 2
        wflat = C['wst'][b][:, :, :].rearrange("p a b -> p (a b)")
        sc.dma('sp', wflat, P['a_w_out'][kc * 128:(kc + 1) * 128, :], writes=[('wst', b)], stream=1)
        sc.op('act', lambda e, kc=kc, wflat=wflat: e.activation(out=wob[:, kc, :], in_=wflat, func=AF.Copy), reads=[('wst', b)], writes=['wob'])
    norm_tiles(k, x_d, hT, range(NT), None, C)
    hall = [('hT', s) for s in range(NT)]
    Win = P['a_w_in']
    for g, (window, dil) in enumerate(A_GROUPS):
        L = S // dil
        nb = L // 128
        cw = min(512, L)
        hTs = [hT[:, kc, :].rearrange("p (m d) -> p d m", d=dil) for kc in range(8)]
        for hp in range(8):
            cq = (g * 3 + 0) * 1024 + hp * 128
            load_w_chunk(k, Win, cq, 128, C['wst'][0], C['wbf'][0], C['gainT'], ('wst', 0), ('wbf', 0))
            load_w_chunk(k, Win, cq + 1024, 128, C['wst'][1], C['wbf'][1], C['gainT'], ('wst', 1), ('wbf', 1))
            load_w_chunk(k, Win, cq + 2048, 128, C['wst2'][0], C['wbf2'][0], C['gainT'], ('wst2', 0), ('wbf2', 0))
            for j in range(2):
                sc.dma('sp', tab[j][:, :], skew_tab(tabA[g], hp * 2 + j, 384, 127, 1, 256), writes=[('tab', j)], stream=5)
            for which, (wb, wres, dst, gc) in enumerate(((C['wbf'][0], ('wbf', 0), qT, g), (C['wbf'][1], ('wbf', 1), kT, 3 + g))):
                dname = 'qT' if which == 0 else 'kT'
                for c in range(S // cw):
                    r, m0 = (c * cw) // L, (c * cw) % L
                    b = c % 2
                    pqi, pssi = (4, 5) if b == 0 else (6, 3)
                    pq, pss = k.ps[pqi], k.ps[pssi]
                    for kc in range(8):
                        sc.op('pe', lambda e, kc=kc: e.matmul(pq[:, 0:cw], lhsT=wb[:, kc, :], rhs=hTs[kc][:, r, m0:m0 + cw], start=(kc == 0), stop=(kc == 7)),
                              reads=[wres] + hall, writes=[('ps', pqi)])
                    sc.op('act', lambda e: e.activation(out=sqb[b][:, 0:cw], in_=pq[:, 0:cw], func=AF.Square), reads=[('ps', pqi)], writes=[('sqb', b)])
                    sc.op('pe', lambda e: e.matmul(pss[:, 0:cw], lhsT=C['blk'][:, :], rhs=sqb[b][:, 0:cw], start=True, stop=True),
                          reads=[('sqb', b), 'const'], writes=[('ps', pssi)])
                    sc.op('dve', lambda e: e.tensor_scalar(out=r1[b][:, 0:cw], in0=pss[:, 0:cw], scalar1=1.0 / 64, scalar2=EPS, op0=ALU.mult, op1=ALU.add),
                          reads=[('ps', pssi)], writes=[('r1', b)])
                    sc.op('act', lambda e: e.activation(out=r1[b][:, 0:cw], in_=r1[b][:, 0:cw], func=AF.Sqrt), reads=[('r1', b)], writes=[('r1', b)])
                    sc.op('dve', lambda e: e.reciprocal(out=r1[b][:, 0:cw], in_=r1[b][:, 0:cw]), reads=[('r1', b)], writes=[('r1', b)])
                    tl = [(dname, t) for t in range(c * cw // 128, (c + 1) * cw // 128)]
                    sc.op('dve', lambda e: e.scalar_tensor_tensor(out=dst[:, c * cw:(c + 1) * cw], in0=pq[:, 0:cw], scalar=gcol[:, gc:gc + 1], in1=r1[b][:, 0:cw],
                                                                   op0=ALU.mult, op1=ALU.mult),
                          reads=[('ps', pqi), ('r1', b), 'gcol'], writes=tl)
            for n in range(NT):
                r, m0 = (n * 128) // L, (n * 128) % L
                pvi = 6 if n % 2 == 0 else 5
                pv = k.ps[pvi]
                for kc in range(8):
                    sc.op('pe', lambda e, kc=kc: e.matmul(pv[:, 0:128], lhsT=hTs[kc][:, r, m0:m0 + 128], rhs=C['wbf2'][0][:, kc, :], start=(kc == 0), stop=(kc == 7)),
                          reads=[('wbf2', 0)] + hall, writes=[('ps', pvi)])
                sc.op('act', lambda e: e.activation(out=V1[:, n, :, 0:64], in_=pv[:, 0:128].rearrange("p (h d) -> p h d", h=2), func=AF.Copy),
                      reads=[('ps', pvi)], writes=[('V1', n)])
            units = [(h2, r, j) for h2 in range(2) for r in range(dil) for j in range(nb)]
            SB = (0, 1, 6)

            def emit_s(u):
                h2, r, j = units[u]
                ps_ = slice(64 * h2, 64 * h2 + 64)
                n = r * nb + j
                nq = 256 if j < nb - 1 else 128
                pS = k.ps[SB[u % 3]]
                sc.op('pe', lambda e: e.matmul(pS[:, 0:nq], lhsT=kT[ps_, n * 128:(n + 1) * 128], rhs=qT[ps_, n * 128:n * 128 + nq], start=True, stop=True),
                      reads=[('kT', n), ('qT', n)] + ([('qT', n + 1)] if nq == 256 else []), writes=[('ps', SB[u % 3])])

            emit_s(0)
            emit_s(1)
            for u, (h2, r, j) in enumerate(units):
                if u + 2 < len(units):
                    emit_s(u + 2)
                n = r * nb + j
                nq = 256 if j < nb - 1 else 128
                pS, pO = k.ps[SB[u % 3]], k.ps[2 + u % 2]
                eb, pt, ptp = Eb[u % 3], PT[u % 4], PT[(u - 1) % 4]
                sc.op('act', lambda e: e.activation(out=eb[:, 0:nq], in_=pS[:, 0:nq], func=AF.Exp, scale=0.125), reads=[('ps', SB[u % 3])], writes=[('Eb', u % 3)])
                sc.op('dve', lambda e: e.tensor_tensor(out=pt[:, 0:nq], in0=eb[:, 0:nq], in1=tab[h2][:, 0:nq], op=ALU.mult),
                      reads=[('Eb', u % 3), ('tab', h2)], writes=[('PT', u % 4)])
                if j > 0:
                    sc.op('pe', lambda e: e.matmul(pO[:, 0:65], lhsT=ptp[:, 128:256], rhs=V1[:, n - 1, h2, :], start=True, stop=False),
                          reads=[('PT', (u - 1) % 4), ('V1', n - 1), 'V1ones'], writes=[('ps', 2 + u % 2)])
                sc.op('pe', lambda e: e.matmul(pO[:, 0:65], lhsT=pt[:, 0:128], rhs=V1[:, n, h2, :], start=(j == 0), stop=True),
                      reads=[('PT', u % 4), ('V1', n), 'V1ones'], writes=[('ps', 2 + u % 2)])
                sc.op('dve', lambda e: e.tensor_copy(out=OZ[:, n, h2, :], in_=pO[:, 0:65]), reads=[('ps', 2 + u % 2)], writes=[('OZ', n)])
            ozv = ozd[g].ap().rearrange("(m d) c -> d m c", d=dil)
            for n in range(NT):
                r, j = n // nb, n % nb
                sc.dma('pool', ozv[r, j * 128:(j + 1) * 128, hp * 130:(hp + 1) * 130], OZ[:, n, :, :].rearrange("p h c -> p (h c)"),
                       reads=[('OZ', n)], writes=['ozd'], stream=6)
            sc.maybe_phase()
    sc.phase()
    for n in range(NT):
        b = n % 2
        xt = C['xt'][b]
        sc.dma('sp', xt[:, :], x_d[n * 128:(n + 1) * 128, :], writes=[('xt', b)], stream=0)
        for g in range(3):
            sc.dma('sp', mrg[g][:, :, :].rearrange("p h c -> p (h c)"), ozd[g].ap()[n * 128:(n + 1) * 128, :], writes=[('mrg', g)], stream=7)
        sc.op('dve', lambda e: e.tensor_tensor(out=mrg[0][:, :, :], in0=mrg[0][:, :, :], in1=mrg[1][:, :, :], op=ALU.add), reads=[('mrg', 0), ('mrg', 1)], writes=[('mrg', 0)])
        sc.op('dve', lambda e: e.tensor_tensor(out=mrg[0][:, :, :], in0=mrg[0][:, :, :], in1=mrg[2][:, :, :], op=ALU.add), reads=[('mrg', 0), ('mrg', 2)], writes=[('mrg', 0)])
        sc.op('dve', lambda e: e.reciprocal(out=rz[:, :], in_=mrg[0][:, :, 64]), reads=[('mrg', 0)], writes=['rz'])
        sc.op('dve', lambda e: e.tensor_tensor(out=ob[:, :, :], in0=mrg[0][:, :, 0:64], in1=bc(rz[:, :].unsqueeze(2), [128, 16, 64]), op=ALU.mult),
              reads=[('mrg', 0), 'rz'], writes=['ob'])
        obf = ob[:, :, :].rearrange("p h d -> p (h d)")
        for kc in range(8):
            sc.op('pe', lambda e, kc=kc: e.transpose(out=k.psb[:, kc * 128:(kc + 1) * 128], in_=obf[:, kc * 128:(kc + 1) * 128], identity=C['identb'][:, :]),
                  reads=['ob', 'const'], writes=['psb'])
        sc.op('act', lambda e: e.activation(out=oT[:, :, :], in_=k.psb[:, :].rearrange("p (kc t) -> p kc t", kc=8), func=AF.Copy), reads=['psb'], writes=['oT'])
        for half in range(2):
            po = k.ps[4 + half]
            for kc in range(8):
                sc.op('pe', lambda e, kc=kc: e.matmul(po[:, :], lhsT=oT[:, kc, :], rhs=wob[:, kc, half * 512:(half + 1) * 512], start=(kc == 0), stop=(kc == 7)),
                      reads=['oT', 'wob'], writes=[('ps', 4 + half)])
            sc.op('dve', lambda e: e.tensor_tensor(out=xt[:, half * 512:(half + 1) * 512], in0=po[:, :], in1=xt[:, half * 512:(half + 1) * 512], op=ALU.add),
                  reads=[('ps', 4 + half), ('xt', b)], writes=[('xt', b)])
        sc.dma('pool', xo_d[n * 128:(n + 1) * 128, :], xt[:, :], reads=[('xt', b)], stream=3)
    sc.phase()
    k.st = old
    st2.close()


def out_tail(k, C, obf, wob, oT, x_d, xo_d, n, obres):
    sc = k.sc
    b = n % 2
    xt = C['xt'][b]
    sc.dma('sp', xt[:, :], x_d[n * 128:(n + 1) * 128, :], writes=[('xt', b)], stream=0)
    for kc in range(8):
        sc.op('pe', lambda e, kc=kc: e.transpose(out=k.psb[:, kc * 128:(kc + 1) * 128], in_=obf[:, kc * 128:(kc + 1) * 128], identity=C['identb'][:, :]),
              reads=[obres, 'const'], writes=['psb'])
    sc.op('act', lambda e: e.activation(out=oT[:, :, :], in_=k.psb[:, :].rearrange("p (kc t) -> p kc t", kc=8), func=AF.Copy), reads=['psb'], writes=['oT'])
    sc.op('act', lambda e: e.activation(out=C['ss'][0][:, 3:4], in_=C['ss'][1][:, 3:4], func=AF.Copy), reads=['oT'], writes=['oT'])
    for half in range(2):
        pb_ = int(os.environ.get('KTAILB', 4)) + half
        po = k.ps[pb_]
        for kc in range(8):
            sc.op('pe', lambda e, kc=kc: e.matmul(po[:, :], lhsT=oT[:, kc, :], rhs=wob[:, kc, half * 512:(half + 1) * 512], start=(kc == 0), stop=(kc == 7)),
                  reads=['oT', 'wob'], writes=[('ps', pb_)])
        sc.op('dve', lambda e: e.tensor_tensor(out=xt[:, half * 512:(half + 1) * 512], in0=po[:, :], in1=xt[:, half * 512:(half + 1) * 512], op=ALU.add),
              reads=[('ps', pb_), ('xt', b)], writes=[('xt', b)])
    sc.dma('pool', xo_d[n * 128:(n + 1) * 128, :], xt[:, :], reads=[('xt', b)], stream=3)


def load_wob(k, C, w_out, wob):
    sc = k.sc
    for kc in range(8):
        b = kc % 2
        wflat = C['wst'][b][:, :, :].rearrange("p a b -> p (a b)")
        sc.dma('sp', wflat, w_out[kc * 128:(kc + 1) * 128, :], writes=[('wst', b)], stream=1)
        sc.op('act', lambda e, kc=kc, wflat=wflat: e.activation(out=wob[:, kc, :], in_=wflat, func=AF.Copy), reads=[('wst', b)], writes=['wob'])


def copy_x(k, x_d, xo_d, C):
    for n in range(NT):
        b = n % 2
        k.sc.dma('sp', C['xt'][b][:, :], x_d[n * 128:(n + 1) * 128, :], writes=[('xt', b)], stream=0)
        k.sc.dma('pool', xo_d[n * 128:(n + 1) * 128, :], C['xt'][b][:, :], reads=[('xt', b)], stream=3)
    k.sc.phase()


STOP = os.environ.get('KSTOP', '')
KBR = os.environ.get('KBR', 'csw')


def mixer_c(k, x_d, xo_d, P, C, cd):
    sc = k.sc
    nc = k.nc
    st2 = ExitStack()
    old = k.st
    k.st = st2
    k.doff = k.dbase
    qTd, kTd = k.dram([8, 128, S], BF16), k.dram([8, 128, S], BF16)
    Kd, Vd = k.dram([S, 8, 128], BF16), k.dram([S, 8, 128], BF16)
    zd = k.dram([S, D], BF16)
    wob = k.sb([128, 8, D], BF16)
    oT = k.sb([128, 8, 128], BF16)
    beta = k.sb([128, NT, 8], F32)
    gg = k.sb([128, NT, 8], F32)
    ba = k.sb([128, NT, 16], F32)
    vec8 = k.sb([128, 4, 8], F32)
    og = k.sb([128, 128], F32)
    load_gainT(k, P['norm1'], C['gainT'])
    load_wob(k, C, P['c_w_out'], wob)
    sc.dma('sp', vec8[:, 0, :], P['c_a_log'].partition_broadcast(128), writes=['vec8'], stream=2)
    sc.dma('sp', vec8[:, 1, :], P['c_dt_bias'].partition_broadcast(128), writes=['vec8'], stream=2)
    sc.dma('sp', og[:, :], P['c_out_gain'].partition_broadcast(128), writes=['og'], stream=2)
    Win = P['c_w_in']
    with ExitStack() as st3:
        k.st = st3
        hT = k.sb([128, 8, S], BF16)
        raw = k.sb([128, S + 3], F32)
        acc = k.sb([128, S], F32)
        so = acc
        outb = k.sb([128, S], BF16)
        sqb = [k.sb([128, 512], BF16) for _ in range(2)]
        r1 = [k.sb([128, 512], F32) for _ in range(2)]
        cw = k.sb([128, 4], F32)
        tstage = [k.sb([128, 8, 128], BF16) for _ in range(2)]
        wz = k.sb([128, 8, D], BF16)
        zs = [k.sb([128, D], BF16) for _ in range(2)]
        norm_tiles(k, x_d, hT, range(NT), None, C)
        hall = [('hT', s_) for s_ in range(NT)]
        sc.op('pool', lambda e: e.memset(raw[:, 0:3], 0.0), writes=['raw0'])
        for j in range(3):
            for h in range(8):
                col = j * 1024 + h * 128
                load_w_chunk(k, Win, col, 128, C['wst'][0], C['wbf'][0], C['gainT'], ('wst', 0), ('wbf', 0))
                sc.dma('sp', cw[:, :], P['c_conv_w'][:, col:col + 128].rearrange("i c -> c i"), writes=['cw'], stream=2, allow_slow_non_contiguous=True)
                for c in range(8):
                    pq = k.ps[c % 2]
                    for kc in range(8):
                        sc.op('pe', lambda e, kc=kc: e.matmul(pq[:, :], lhsT=C['wbf'][0][:, kc, :], rhs=hT[:, kc, c * 512:(c + 1) * 512], start=(kc == 0), stop=(kc == 7)),
                              reads=[('wbf', 0)] + hall, writes=[('ps', c % 2)])
                    sc.op('act', lambda e: e.activation(out=raw[:, 3 + c * 512:3 + (c + 1) * 512], in_=pq[:, :], func=AF.Copy), reads=[('ps', c % 2)], writes=[('raw', c)])
                rall = [('raw', c) for c in range(8)] + ['raw0']
                for hf in range(2):
                    sl = slice(hf * 2048, (hf + 1) * 2048)
                    sc.op('dve', lambda e: e.tensor_scalar(out=acc[:, sl], in0=raw[:, 3 + hf * 2048:3 + (hf + 1) * 2048], scalar1=cw[:, 3:4], scalar2=None, op0=ALU.mult),
                          reads=rall + ['cw'], writes=[('acc', hf), ('so', hf)])
                    for i in range(3):
                        sc.op('dve', lambda e, i=i: e.scalar_tensor_tensor(out=acc[:, sl], in0=raw[:, i + hf * 2048:i + (hf + 1) * 2048], scalar=cw[:, i:i + 1], in1=acc[:, sl],
                                                                          op0=ALU.mult, op1=ALU.add),
                              reads=rall + ['cw', ('acc', hf)], writes=[('acc', hf)])
                    if j == 2:
                        sc.op('act', lambda e: e.activation(out=outb[:, sl], in_=acc[:, sl], func=AF.Silu), reads=[('acc', hf)], writes=[('outb', hf)])
                    else:
                        sc.op('act', lambda e: e.activation(out=so[:, sl], in_=acc[:, sl], func=AF.Silu), reads=[('acc', hf)], writes=[('so', hf), ('acc', hf)])
                if j < 2:
                    for c in range(8):
                        b = c % 2
                        cs = slice(c * 512, (c + 1) * 512)
                        pss = k.ps[2 + b]
                        sc.op('act', lambda e: e.activation(out=sqb[b][:, :], in_=so[:, cs], func=AF.Square), reads=[('so', c // 4)], writes=[('sqb', b)])
                        sc.op('pe', lambda e: e.matmul(pss[:, :], lhsT=C['onesb'][:, :], rhs=sqb[b][:, :], start=True, stop=True), reads=[('sqb', b), 'const'], writes=[('ps', 2 + b)])
                        sc.op('dve', lambda e: e.tensor_scalar(out=r1[b][:, :], in0=pss[:, :], scalar1=EPS, scalar2=None, op0=ALU.add), reads=[('ps', 2 + b)], writes=[('r1', b)])
                        sc.op('act', lambda e: e.activation(out=r1[b][:, :], in_=r1[b][:, :], func=AF.Sqrt), reads=[('r1', b)], writes=[('r1', b)])
                        sc.op('dve', lambda e: e.reciprocal(out=r1[b][:, :], in_=r1[b][:, :]), reads=[('r1', b)], writes=[('r1', b)])
                        sc.op('dve', lambda e: e.scalar_tensor_tensor(out=outb[:, cs], in0=so[:, cs], scalar=(128.0 ** -0.5 if j == 0 else 1.0), in1=r1[b][:, :], op0=ALU.mult, op1=ALU.mult),
                              reads=[('so', c // 4), ('r1', b)], writes=[('outb', c // 4)])
                    sc.dma('pool', (qTd if j == 0 else kTd).ap()[h, :, :], outb[:, :], reads=[('outb', 0), ('outb', 1)], writes=['qkTd'], stream=6)
                if j >= 1:
                    dst = Kd if j == 1 else Vd
                    dv_ = dst.ap().rearrange("(n p) h d -> p n h d", p=128)
                    for n8 in range(4):
                        b = n8 % 2
                        for t in range(8):
                            n = n8 * 8 + t
                            sc.op('pe', lambda e, t=t, n=n: e.transpose(out=k.psb[:, t * 128:(t + 1) * 128], in_=outb[:, n * 128:(n + 1) * 128], identity=C['identb'][:, :]),
                                  reads=[('outb', n // 16), 'const'], writes=['psb'])
                        sc.op('act', lambda e: e.activation(out=tstage[b][:, :, :], in_=k.psb[:, :].rearrange("p (a t) -> p a t", a=8), func=AF.Copy), reads=['psb'], writes=[('tst', b)])
                        sc.dma('pool', dv_[:, n8 * 8:(n8 + 1) * 8, h, :], tstage[b][:, :, :], reads=[('tst', b)], writes=['KVd'], stream=7)
                sc.maybe_phase()
        for pc in range(8):
            load_w_chunk(k, Win, 3072 + pc * 128, 128, C['wst'][pc % 2], C['wbf'][pc % 2], C['gainT'], ('wst', pc % 2), ('wbf', pc % 2))
            sc.op('act', lambda e, pc=pc: e.activation(out=wz[:, :, pc * 128:(pc + 1) * 128], in_=C['wbf'][pc % 2][:, :, :], func=AF.Copy), reads=[('wbf', pc % 2)], writes=['wz'])
        load_w_chunk(k, Win, 4096, 16, C['wst2'][0], C['wbf2'][0], C['gainT'], ('wst2', 0), ('wbf2', 0))
        for n in range(NT):
            b = n % 2
            for half in range(2):
                pz = k.ps[half]
                for kc in range(8):
                    sc.op('pe', lambda e, kc=kc: e.matmul(pz[:, :], lhsT=hT[:, kc, n * 128:(n + 1) * 128], rhs=wz[:, kc, half * 512:(half + 1) * 512], start=(kc == 0), stop=(kc == 7)),
                          reads=['wz', ('hT', n)], writes=[('ps', half)])
                sc.op('act', lambda e: e.activation(out=zs[b][:, half * 512:(half + 1) * 512], in_=pz[:, :], func=AF.Silu), reads=[('ps', half)], writes=[('zs', b)])
            sc.dma('pool', zd.ap()[n * 128:(n + 1) * 128, :], zs[b][:, :], reads=[('zs', b)], writes=['zd'], stream=6)
            pb = k.ps[2]
            for kc in range(8):
                sc.op('pe', lambda e, kc=kc: e.matmul(pb[:, 0:16], lhsT=hT[:, kc, n * 128:(n + 1) * 128], rhs=C['wbf2'][0][:, kc, 0:16], start=(kc == 0), stop=(kc == 7)),
                      reads=[('wbf2', 0), ('hT', n)], writes=[('ps', 2)])
            sc.op('dve', lambda e: e.tensor_copy(out=ba[:, n, :], in_=pb[:, 0:16]), reads=[('ps', 2)], writes=['ba'])
        tmp = raw[:, 0:NT * 8].rearrange("p (n h) -> p n h", h=8)
        tmp2 = acc[:, 0:NT * 8].rearrange("p (n h) -> p n h", h=8)
        tmp3 = acc[:, 2048:2048 + NT * 8].rearrange("p (n h) -> p n h", h=8)
        sc.op('act', lambda e: e.activation(out=beta[:, :, :], in_=ba[:, :, 0:8], func=AF.Sigmoid), reads=['ba'], writes=['beta'])
        sc.op('act', lambda e: e.activation(out=vec8[:, 2, :], in_=vec8[:, 0, :], func=AF.Exp), reads=['vec8'], writes=['vec8b'])
        sc.op('dve', lambda e: e.tensor_tensor(out=tmp, in0=ba[:, :, 8:16], in1=bc(vec8[:, 1:2, :], [128, NT, 8]), op=ALU.add),
              reads=['ba', 'vec8', 'raw0'] + [('raw', c) for c in range(8)], writes=['tmpx'])
        sc.op('act', lambda e: e.activation(out=tmp2, in_=tmp, func=AF.Abs), reads=['tmpx', ('acc', 0), ('so', 0)], writes=['tmp2'])
        sc.op('act', lambda e: e.activation(out=tmp2, in_=tmp2, func=AF.Exp, scale=-1.0), reads=['tmp2'], writes=['tmp2'])
        sc.op('dve', lambda e: e.tensor_scalar(out=tmp2, in0=tmp2, scalar1=1.0, scalar2=None, op0=ALU.add), reads=['tmp2'], writes=['tmp2'])
        sc.op('act', lambda e: e.activation(out=tmp2, in_=tmp2, func=AF.Ln), reads=['tmp2'], writes=['tmp2'])
        sc.op('dve', lambda e: e.tensor_scalar(out=tmp3, in0=tmp, scalar1=0.0, scalar2=None, op0=ALU.max), reads=['tmpx', ('so', 0), ('so', 1), ('acc', 1)], writes=['tmp3'])
        sc.op('dve', lambda e: e.tensor_tensor(out=tmp3, in0=tmp3, in1=tmp2, op=ALU.add), reads=['tmp3', 'tmp2'], writes=['tmp3'])
        sc.op('dve', lambda e: e.tensor_tensor(out=tmp3, in0=tmp3, in1=bc(vec8[:, 2:3, :], [128, NT, 8]), op=ALU.mult), reads=['tmp3', 'vec8b'], writes=['tmp3'])
        sc.op('dve', lambda e: e.tensor_scalar(out=gg[:, :, :], in0=tmp3, scalar1=-1.0, scalar2=None, op0=ALU.mult), reads=['tmp3'], writes=['gg'])
        sc.phase()
    k.st = st2
    if STOP == 'c1':
        copy_x(k, x_d, xo_d, C)
        k.st = old
        st2.close()
        return
    def t4(nm, dt=F32, n=4):
        return k.sb([128, n, 128], dt)
    qTt = [k.sb([128, 8, 128], BF16) for _ in range(2)]
    kTt = [k.sb([128, 8, 128], BF16) for _ in range(2)]
    Kt = [k.sb([128, 8, 128], BF16) for _ in range(2)]
    Vt = [k.sb([128, 8, 128], BF16) for _ in range(2)]
    zt = [k.sb([128, D], BF16) for _ in range(2)]
    Rm = k.sb([128, 8, 128], F32)
    sm = k.sb([128, 12, 8], F32)
    t0 = t4('t0'); tA = t4('tA'); DmT = t4('DmT'); Dms = t4('Dms')
    Xa, Xb, Ya, Yb = t4('Xa'), t4('Xb'), t4('Ya'), t4('Yb')
    Qm = t4('Qm'); Qb = t4('Qb', BF16)
    attnT = k.sb([128, 8, 128], BF16)
    Vb = k.sb([128, 8, 128], BF16); Kh = k.sb([128, 8, 128], BF16); Kti = k.sb([128, 8, 128], BF16)
    u = k.sb([128, 8, 128], F32); wTb = k.sb([128, 8, 128], BF16)
    vn = k.sb([128, 8, 128], BF16)
    Sf = k.sb([128, 8, 128], F32); Sb = k.sb([128, 8, 128], BF16)
    osb = k.sb([128, 8, 128], F32); tq = t4('tq')
    sqo = k.sb([128, 8, 128], F32)
    ob = k.sb([128, 8, 128], BF16)
    sc.op('pool', lambda e: e.memset(Sf[:, :, :], 0.0), writes=['Sf'])
    sc.op('pool', lambda e: e.memset(Sb[:, :, :], 0.0), writes=['Sb'])
    qv = qTd.ap().rearrange("h p t -> p h t")
    kv = kTd.ap().rearrange("h p t -> p h t")
    PS = k.ps
    for n in range(int(os.environ.get('KTILES', NT))):
        b = n % 2
        ts_ = slice(n * 128, (n + 1) * 128)
        sc.dma('sp', qTt[b][:, :, :], qv[:, :, ts_], writes=[('qTt', b)], stream=1)
        sc.dma('sp', kTt[b][:, :, :], kv[:, :, ts_], writes=[('kTt', b)], stream=1)
        sc.dma('sp', Kt[b][:, :, :], Kd.ap()[ts_, :, :], writes=[('Kt', b)], stream=4)
        sc.dma('sp', Vt[b][:, :, :], Vd.ap()[ts_, :, :], writes=[('Vt', b)], stream=4)
        sc.dma('sp', zt[b][:, :], zd.ap()[ts_, :], writes=[('zt', b)], stream=5)
        gt = gg[:, n, :]
        for idx, lh in ((0, cd['TB']), (1, cd['BLK']), (2, cd['H0']), (3, cd['H1'])):
            sc.op('pe', lambda e, lh=lh, idx=idx: e.matmul(PS[6][:, idx * 8:idx * 8 + 8], lhsT=lh[:, :], rhs=gt, start=True, stop=True), reads=['gg', 'cd'], writes=[('ps', 6)])
        sc.op('dve', lambda e: e.tensor_copy(out=sm[:, 0:4, :], in_=PS[6][:, 0:32].rearrange("p (a h) -> p a h", a=4)), reads=[('ps', 6)], writes=['sm'])
        sc.op('act', lambda e: e.activation(out=sm[:, 4, :], in_=sm[:, 0, :], func=AF.Exp), reads=['sm'], writes=['sm4'])
        sc.op('act', lambda e: e.activation(out=sm[:, 5:7, :], in_=sm[:, 2:4, :], func=AF.Exp), reads=['sm'], writes=['sm5'])
        sc.op('dve', lambda e: e.tensor_tensor(out=sm[:, 7, :], in0=sm[:, 1, :], in1=sm[:, 0, :], op=ALU.subtract), reads=['sm'], writes=['sm7'])
        sc.op('act', lambda e: e.activation(out=sm[:, 7, :], in_=sm[:, 7, :], func=AF.Exp), reads=['sm7'], writes=['sm7'])
        sc.op('dve', lambda e: e.tensor_tensor(out=sm[:, 8, :], in0=sm[:, 4, :], in1=beta[:, n, :], op=ALU.mult), reads=['sm4', 'beta'], writes=['sm8'])
        sc.op('dve', lambda e: e.tensor_scalar(out=sm[:, 9, :], in0=beta[:, n, :], scalar1=-1.0, scalar2=None, op0=ALU.mult), reads=['beta'], writes=['sm9'])
        sc.op('dve', lambda e: e.tensor_tensor(out=Rm[:, :, :], in0=bc(cd['TB'][:, :].unsqueeze(1), [128, 8, 128]), in1=bc(gt.unsqueeze(2), [128, 8, 128]), op=ALU.mult),
              reads=['gg', 'cd'], writes=['Rm'])
        sc.op('dve', lambda e: e.tensor_tensor(out=Vb[:, :, :], in0=Vt[b][:, :, :], in1=bc(beta[:, n, :].unsqueeze(2), [128, 8, 128]), op=ALU.mult), reads=[('Vt', b), 'beta'], writes=['Vb'])
        sc.op('dve', lambda e: e.tensor_tensor(out=Kh[:, :, :], in0=Kt[b][:, :, :], in1=bc(sm[:, 8, :].unsqueeze(2), [128, 8, 128]), op=ALU.mult), reads=[('Kt', b), 'sm8'], writes=['Kh'])
        sc.op('dve', lambda e: e.tensor_tensor(out=Kti[:, :, :], in0=Kt[b][:, :, :], in1=bc(sm[:, 7, :].unsqueeze(2), [128, 8, 128]), op=ALU.mult), reads=[('Kt', b), 'sm7'], writes=['Kti'])
        if STOP == 'small':
            continue
        for hg in range(2):
            hs = slice(4 * hg, 4 * hg + 4)
            H = range(4 * hg, 4 * hg + 4)
            def v4(p):
                return p[:, :].rearrange("p (a t) -> p a t", a=4)
            sc.op('pe', lambda e: e.matmul(PS[0][:, :], lhsT=cd['ONES'][:, :], rhs=Rm[:, hs, :].rearrange("p a t -> p (a t)"), start=True, stop=True), reads=['Rm', 'cd'], writes=[('ps', 0)])
            for i, h in enumerate(H):
                sc.op('pe', lambda e, i=i, h=h: e.matmul(PS[1][:, i * 128:(i + 1) * 128], lhsT=kTt[b][:, h, :], rhs=kTt[b][:, h, :], start=True, stop=True), reads=[('kTt', b)], writes=[('ps', 1)])
            for i, h in enumerate(H):
                sc.op('pe', lambda e, i=i, h=h: e.matmul(PS[2][:, i * 128:(i + 1) * 128], lhsT=kTt[b][:, h, :], rhs=qTt[b][:, h, :], start=True, stop=True), reads=[('kTt', b), ('qTt', b)], writes=[('ps', 2)])
            sc.op('dve', lambda e: e.tensor_tensor(out=t0[:, :, :], in0=v4(PS[0]), in1=bc(sm[:, 0, hs].unsqueeze(2), [128, 4, 128]), op=ALU.subtract), reads=[('ps', 0), 'sm'], writes=['t0'])
            if STOP == 'g1':
                continue
            sc.op('dve', lambda e: e.tensor_tensor(out=tA[:, :, :], in0=t0[:, :, :], in1=bc(cd['MT'][:, :].unsqueeze(1), [128, 4, 128]), op=ALU.add), reads=['t0', 'cd'], writes=['tA'])
            sc.op('act', lambda e: e.activation(out=DmT[:, :, :], in_=tA[:, :, :], func=AF.Exp), reads=['tA'], writes=['DmT'])
            sc.op('dve', lambda e: e.tensor_tensor(out=tA[:, :, :], in0=t0[:, :, :], in1=bc(cd['MBS'][:, :].unsqueeze(1), [128, 4, 128]), op=ALU.add), reads=['t0', 'cd', 'tA'], writes=['tA'])
            sc.op('act', lambda e: e.activation(out=Dms[:, :, :], in_=tA[:, :, :], func=AF.Exp, scale=-1.0), reads=['tA'], writes=['Dms'])
            if STOP == 'g2':
                continue
            sc.op('dve', lambda e: e.tensor_tensor(out=attnT[:, hs, :], in0=v4(PS[2]), in1=DmT[:, :, :], op=ALU.mult), reads=[('ps', 2), 'DmT'], writes=[('attnT', hg)])
            sc.op('dve', lambda e: e.tensor_tensor(out=Xa[:, :, :], in0=v4(PS[1]), in1=Dms[:, :, :], op=ALU.mult), reads=[('ps', 1), 'Dms'], writes=['Xa'])
            sc.op('dve', lambda e: e.tensor_tensor(out=Xa[:, :, :], in0=Xa[:, :, :], in1=bc(sm[:, 9, hs].unsqueeze(2), [128, 4, 128]), op=ALU.mult), reads=['Xa', 'sm9'], writes=['Xa'])
            if STOP == 'g3':
                continue
            for i in range(4):
                sc.op('pe', lambda e, i=i: e.matmul(PS[3][:, i * 128:(i + 1) * 128], lhsT=Xa[:, i, :], rhs=C['identf'][:, :], start=True, stop=True), reads=['Xa', 'const'], writes=[('ps', 3)])
            if STOP == 'g4':
                continue
            sc.op('act', lambda e: e.activation(out=Ya[:, :, :], in_=v4(PS[3]), func=AF.Copy), reads=[('ps', 3)], writes=['Ya'])
            if STOP == 'g5':
                continue
            sc.op('dve', lambda e: e.tensor_tensor(out=Qm[:, :, :], in0=Ya[:, :, :], in1=bc(C['identf'][:, :].unsqueeze(1), [128, 4, 128]), op=ALU.add), reads=['Ya', 'const'], writes=['Qm'])
            if STOP == 'pre':
                continue
            Xc, Yc, Xn, Yn, xc, yc, xn, yn = Xa, Ya, Xb, Yb, 'Xa', 'Ya', 'Xb', 'Yb'
            for step in range(5):
                for i in range(4):
                    sc.op('pe', lambda e, i=i, Xc=Xc, Yc=Yc: e.matmul(PS[0][:, i * 128:(i + 1) * 128], lhsT=Yc[:, i, :], rhs=Xc[:, i, :], start=True, stop=True), reads=[xc, yc], writes=[('ps', 0)])
                sc.op('act', lambda e, Xn=Xn: e.activation(out=Xn[:, :, :], in_=v4(PS[0]), func=AF.Copy), reads=[('ps', 0)], writes=[xn])
                if step < 4:
                    for i in range(4):
                        sc.op('pe', lambda e, i=i, Xc=Xc, Yc=Yc: e.matmul(PS[1][:, i * 128:(i + 1) * 128], lhsT=Xc[:, i, :], rhs=Yc[:, i, :], start=True, stop=True), reads=[xc, yc], writes=[('ps', 1)])
                    sc.op('act', lambda e, Yn=Yn: e.activation(out=Yn[:, :, :], in_=v4(PS[1]), func=AF.Copy), reads=[('ps', 1)], writes=[yn])
                for i in range(4):
                    sc.op('pe', lambda e, i=i, Xn=Xn: e.matmul(PS[2][:, i * 128:(i + 1) * 128], lhsT=Xn[:, i, :], rhs=Qm[:, i, :], start=True, stop=True), reads=[xn, 'Qm'], writes=[('ps', 2)])
                sc.op('dve', lambda e: e.tensor_tensor(out=Qm[:, :, :], in0=v4(PS[2]), in1=Qm[:, :, :], op=ALU.add), reads=[('ps', 2), 'Qm'], writes=['Qm'])
                Xc, Yc, Xn, Yn, xc, yc, xn, yn = Xn, Yn, Xc, Yc, xn, yn, xc, yc
            if STOP == 'inv':
                continue
            sc.op('act', lambda e: e.activation(out=Qb[:, :, :], in_=Qm[:, :, :], func=AF.Copy), reads=['Qm'], writes=['Qb'])
            for i, h in enumerate(H):
                sc.op('pe', lambda e, i=i, h=h: e.matmul(PS[3][:, i * 128:(i + 1) * 128], lhsT=Qb[:, i, :], rhs=Vb[:, h, :], start=True, stop=True), reads=['Qb', 'Vb'], writes=[('ps', 3)])
            sc.op('act', lambda e: e.activation(out=u[:, hs, :], in_=v4(PS[3]), func=AF.Copy), reads=[('ps', 3)], writes=[('u', hg)])
            for i, h in enumerate(H):
                sc.op('pe', lambda e, i=i, h=h: e.matmul(PS[4][:, i * 128:(i + 1) * 128], lhsT=Kh[:, h, :], rhs=Qb[:, i, :], start=True, stop=True), reads=['Qb', 'Kh'], writes=[('ps', 4)])
            sc.op('act', lambda e: e.activation(out=wTb[:, hs, :], in_=v4(PS[4]), func=AF.Copy), reads=[('ps', 4)], writes=[('wTb', hg)])
        if STOP in ('setup', 'pre', 'inv', 'g1', 'g2', 'g3', 'g4', 'g5'):
            continue
        for hf in range(2):
            Pp = slice(64 * hf, 64 * hf + 64)
            for hg in range(2):
                hs = slice(4 * hg, 4 * hg + 4)
                H = range(4 * hg, 4 * hg + 4)
                sres = ('Sb', hg)
                for i, h in enumerate(H):
                    sc.op('pe', lambda e, i=i, h=h: e.matmul(PS[0][:, i * 128:(i + 1) * 128], lhsT=wTb[:, h, :], rhs=Sb[:, h, :], start=True, stop=True), reads=[('wTb', hg), sres, 'Sb'], writes=[('ps', 0)])
                for i, h in enumerate(H):
                    sc.op('pe', lambda e, i=i, h=h: e.matmul(PS[1][:, i * 128:(i + 1) * 128], lhsT=qTt[b][:, h, :], rhs=Sb[:, h, :], start=True, stop=True), reads=[('qTt', b), sres, 'Sb'], writes=[('ps', 1)])
                sc.op('dve', lambda e: e.tensor_tensor(out=vn[Pp, hs, :], in0=u[Pp, hs, :], in1=v4(PS[0])[Pp, :, :], op=ALU.subtract), reads=[('u', hg), ('ps', 0)], writes=[('vn', hg)])
                sc.op('dve', lambda e: e.tensor_tensor(out=tq[Pp, :, :], in0=v4(PS[1])[Pp, :, :], in1=bc(sm[Pp, 4, hs].unsqueeze(2), [64, 4, 128]), op=ALU.mult), reads=[('ps', 1), 'sm4'], writes=['tq'])
                for i, h in enumerate(H):
                    sc.op('pe', lambda e, i=i, h=h: e.matmul(PS[2][:, i * 128:(i + 1) * 128], lhsT=Kti[Pp, h, :], rhs=vn[Pp, h, :], start=True, stop=True), reads=['Kti', ('vn', hg)], writes=[('ps', 2)])
                for i, h in enumerate(H):
                    sc.op('pe', lambda e, i=i, h=h: e.matmul(PS[3][:, i * 128:(i + 1) * 128], lhsT=attnT[Pp, h, :], rhs=vn[Pp, h, :], start=True, stop=True), reads=[('attnT', hg), ('vn', hg)], writes=[('ps', 3)])
                sc.op('dve', lambda e: e.tensor_tensor(out=Sf[:, hs, :], in0=Sf[:, hs, :], in1=bc(sm[:, 5 + hf, hs].unsqueeze(2), [128, 4, 128]), op=ALU.mult), reads=['Sf', 'sm5', ('Sf', hg)], writes=[('Sf', hg)])
                sc.op('dve', lambda e: e.tensor_tensor(out=Sf[:, hs, :], in0=Sf[:, hs, :], in1=v4(PS[2]), op=ALU.add), reads=[('Sf', hg), ('ps', 2)], writes=[('Sf', hg)])
                sc.op('act', lambda e: e.activation(out=Sb[:, hs, :], in_=Sf[:, hs, :], func=AF.Copy), reads=[('Sf', hg)], writes=[sres])
                sc.op('dve', lambda e: e.tensor_tensor(out=osb[Pp, hs, :], in0=tq[Pp, :, :], in1=v4(PS[3])[Pp, :, :], op=ALU.add), reads=['tq', ('ps', 3)], writes=[('osb', hg)])
        if STOP == 'seq':
            continue
        sc.op('act', lambda e: e.activation(out=sqo[:, :, :], in_=osb[:, :, :], func=AF.Square), reads=[('osb', 0), ('osb', 1)], writes=['sqo'])
        sc.op('dve', lambda e: e.tensor_reduce(out=sm[:, 10, :], in_=sqo[:, :, :], axis=AX.X, op=ALU.add), reads=['sqo'], writes=['sm10'])
        sc.op('dve', lambda e: e.tensor_scalar(out=sm[:, 10, :], in0=sm[:, 10, :], scalar1=1.0 / 128, scalar2=EPS, op0=ALU.mult, op1=ALU.add), reads=['sm10'], writes=['sm10'])
        sc.op('act', lambda e: e.activation(out=sm[:, 10, :], in_=sm[:, 10, :], func=AF.Sqrt), reads=['sm10'], writes=['sm10'])
        sc.op('dve', lambda e: e.reciprocal(out=sm[:, 11, :], in_=sm[:, 10, :]), reads=['sm10'], writes=['sm11'])
        sc.op('dve', lambda e: e.tensor_tensor(out=sqo[:, :, :], in0=osb[:, :, :], in1=bc(sm[:, 11, :].unsqueeze(2), [128, 8, 128]), op=ALU.mult), reads=[('osb', 0), ('osb', 1), 'sm11', 'sqo'], writes=['sqo'])
        sc.op('dve', lambda e: e.tensor_tensor(out=sqo[:, :, :], in0=sqo[:, :, :], in1=bc(og[:, :].unsqueeze(1), [128, 8, 128]), op=ALU.mult), reads=['sqo', 'og'], writes=['sqo'])
        sc.op('dve', lambda e: e.tensor_tensor(out=ob[:, :, :], in0=sqo[:, :, :], in1=zt[b][:, :].rearrange("p (h d) -> p h d", h=8), op=ALU.mult), reads=['sqo', ('zt', b)], writes=['ob'])
        out_tail(k, C, ob[:, :, :].rearrange("p h d -> p (h d)"), wob, oT, x_d, xo_d, n, 'ob')
        sc.maybe_phase()
    sc.phase()
    k.st = old
    st2.close()


def dense_attn(k, C, B, qTh, lhsT_fn, vaug_fn, ncol, tab_fn, kts_fn, qb0_fn, last_fn, evac_fn, pen_fn=None, qcs=range(8), tag=''):
    sc = k.sc
    PS = k.ps
    Eb, PT = B['Eb'], B['PT']

    def acc(qb):
        return PS[2 + qb][:, 0:ncol], ('ps', 2 + qb)

    u = 0
    for qc in (list(qcs)[::-1] if os.environ.get('KREV') else qcs):
        kts = kts_fn(qc)
        units = []
        for kt in kts:
            qb0 = qb0_fn(qc, kt)
            units.append((kt, qb0, qc * 512 + qb0 * 128, 512 - qb0 * 128))

        def emit_s(ui, un):
            kt, qb0, q0, nq = un
            pS = PS[ui % 2]
            lh, lres = lhsT_fn(kt)
            sc.op('pe', lambda e: e.matmul(pS[:, 0:nq], lhsT=lh, rhs=qTh[0:64, q0:q0 + nq], start=True, stop=(pen_fn is None)),
                  reads=[lres, 'qTh'], writes=[('ps', ui % 2)])
            if pen_fn is not None:
                pl, pr, pres = pen_fn(kt, q0, nq)
                sc.op('pe', lambda e: e.matmul(pS[:, 0:nq], lhsT=pl, rhs=pr, start=False, stop=True), reads=pres, writes=[('ps', ui % 2)])

        if units:
            emit_s(u, units[0])
        started = set()
        for i, un in enumerate(units):
            kt, qb0, q0, nq = un
            ui = u + i
            if i + 1 < len(units):
                emit_s(ui + 1, units[i + 1])
            pS, eb, pt = PS[ui % 2], Eb[ui % 2], PT[ui % 3]
            sc.op('act', lambda e: e.activation(out=eb[:, 0:nq], in_=pS[:, 0:nq], func=AF.Exp, scale=0.125), reads=[('ps', ui % 2)], writes=[('Eb', ui % 2)])
            tb, tres = tab_fn(kt, q0, nq)
            sc.op('dve', lambda e: e.tensor_tensor(out=pt[:, 0:nq], in0=eb[:, 0:nq], in1=tb, op=ALU.mult), reads=[('Eb', ui % 2), tres], writes=[('PT', ui % 3)])
            va, vres = vaug_fn(kt)
            for qb in range(qb0, 4):
                if kt > last_fn(qc, qb):
                    continue
                a, ares = acc(qb)
                sc.op('pe', lambda e, qb=qb, a=a: e.matmul(a, lhsT=pt[:, (qb - qb0) * 128:(qb - qb0 + 1) * 128], rhs=va, start=(qb not in started), stop=(kt == last_fn(qc, qb))),
                      reads=[('PT', ui % 3)] + vres, writes=[ares])
                started.add(qb)
        u += len(units)
        evac_fn(qc, acc)
        sc.maybe_phase()


class StopHere(Exception):
    pass


def chk(tag):
    if STOP == tag:
        raise StopHere()


def build_head_tab(k, C, oh_d, width, tab, hq, relh, ohs, stg, res):
    sc = k.sc
    sc.op('dve', lambda e: e.tensor_copy(out=relh[:, :], in_=bc(C['relb'][:, hq:hq + 1], [33, 128])), reads=['relb'], writes=['relh'])
    for ci, c0 in enumerate(range(0, width, 512)):
        n = min(512, width - c0)
        j = ci % 2
        sc.dma('sp', ohs[:, 0:n], oh_d[:, c0:c0 + n], writes=['ohs'], stream=2)
        sc.op('pe', lambda e: e.matmul(k.ps[6][:, 0:n], lhsT=relh[:, :], rhs=ohs[:, 0:n], start=True, stop=True), reads=['ohs', 'relh'], writes=[('ps', 6)])
        sc.op('act', lambda e: e.activation(out=stg[j][:, 0:n], in_=k.ps[6][:, 0:n], func=AF.Exp), reads=[('ps', 6)], writes=[('tslab', j)])
        sc.dma('pool', tab.ap()[:, c0:c0 + n], stg[j][:, 0:n], reads=[('tslab', j)], writes=[res], stream=9)


def mixer_b(k, x_d, xo_d, P, C, tabs, cdn):
    sc = k.sc
    st2 = ExitStack()
    old = k.st
    k.st = st2
    PS = k.ps
    k.doff = k.dbase
    ohC, ohD, ohW = tabs
    tabC, tabD, tabW = k.dram([128, 6144]), k.dram([128, 4224]), k.dram([128, 1280])
    relh = k.sb([33, 128], F32)
    ohs = k.sb([33, 512], F32)
    if STOP == 'b0':
        copy_x(k, x_d, xo_d, C)
        k.st = old
        st2.close()
        return
    WC, WD, WW = 6144, 4224, 1280
    qTd = k.dram([16, 64, S], BF16)
    kTd = [k.dram([4, 64, S], BF16) for _ in range(4)]
    Vd = [k.dram([S, 4, 64], BF16) for _ in range(2)]
    obd = k.dram([S, D], BF16)
    wob = k.sb([128, 8, D], BF16)
    oT = k.sb([128, 8, 128], BF16)
    gates = k.sb([128, NT, 48], F32)
    gcol = k.sb([128, 4], F32)
    load_gainT(k, P['norm1'], C['gainT'])
    load_wob(k, C, P['b_w_out'], wob)
    for half in range(2):
        sc.dma('sp', gcol[half * 64:(half + 1) * 64, 0:1], P['b_q_gain'].rearrange("(d o) -> d o", o=1), writes=['gcol'], stream=2, allow_slow_non_contiguous=True)
        sc.dma('sp', gcol[half * 64:(half + 1) * 64, 1:4], P['b_k_gain'].rearrange("g d -> d g"), writes=['gcol'], stream=2, allow_slow_non_contiguous=True)
    Win = P['b_w_in']
    with ExitStack() as st3:
        k.st = st3
        hT = k.sb([128, 8, S], BF16)
        outb = k.sb([128, S], BF16)
        sqb = [k.sb([128, 512], BF16) for _ in range(2)]
        r1 = [k.sb([128, 512], F32) for _ in range(2)]
        wv = k.sb([128, 8, 256], BF16)
        vst = [k.sb([128, 256], BF16) for _ in range(2)]
        norm_tiles(k, x_d, hT, range(NT), None, C)
        hall = [('hT', s_) for s_ in range(NT)]
        chunks = [(c * 128, qTd, 2 * c, 0) for c in range(8)]
        for br, kvi, dst, gc in ((0, 0, kTd[0], None), (0, 1, kTd[1], None), (1, 0, kTd[2], 2), (2, 0, kTd[3], 3)):
            for c2 in range(2):
                chunks.append((1024 + (br * 2 + kvi) * 256 + c2 * 128, dst, 2 * c2, gc))
        for (col, dst, h0, gc) in chunks[int(os.environ.get('KCH0', 0)):int(os.environ.get('KCH1', 99))]:
            load_w_chunk(k, Win, col, 128, C['wst'][0], C['wbf'][0], C['gainT'], ('wst', 0), ('wbf', 0))
            for c in range(8):
                b = c % 2
                cs = slice(c * 512, (c + 1) * 512)
                pq, pss = PS[4], PS[5]
                for kc in range(8):
                    sc.op('pe', lambda e, kc=kc: e.matmul(pq[:, :], lhsT=C['wbf'][0][:, kc, :], rhs=hT[:, kc, cs], start=(kc == 0), stop=(kc == 7)),
                          reads=[('wbf', 0)] + hall, writes=[('ps', 4)])
                if gc is None:
                    sc.op('act', lambda e: e.activation(out=outb[:, cs], in_=pq[:, :], func=AF.Copy), reads=[('ps', 4)], writes=[('outb', c)])
                    continue
                sc.op('act', lambda e: e.activation(out=sqb[b][:, :], in_=pq[:, :], func=AF.Square), reads=[('ps', 4)], writes=[('sqb', b)])
                sc.op('pe', lambda e: e.matmul(pss[:, :], lhsT=C['blk'][:, :], rhs=sqb[b][:, :], start=True, stop=True), reads=[('sqb', b), 'const'], writes=[('ps', 5)])
                sc.op('dve', lambda e: e.tensor_scalar(out=r1[b][:, :], in0=pss[:, :], scalar1=1.0 / 64, scalar2=EPS, op0=ALU.mult, op1=ALU.add), reads=[('ps', 5)], writes=[('r1', b)])
                sc.op('act', lambda e: e.activation(out=r1[b][:, :], in_=r1[b][:, :], func=AF.Sqrt), reads=[('r1', b)], writes=[('r1', b)])
                sc.op('dve', lambda e: e.reciprocal(out=r1[b][:, :], in_=r1[b][:, :]), reads=[('r1', b)], writes=[('r1', b)])
                sc.op('dve', lambda e: e.scalar_tensor_tensor(out=outb[:, cs], in0=pq[:, :], scalar=gcol[:, gc:gc + 1], in1=r1[b][:, :], op0=ALU.mult, op1=ALU.mult),
                      reads=[('ps', 4), ('r1', b), 'gcol'], writes=[('outb', c)])
            oall = [('outb', c) for c in range(8)]
            sc.dma('pool', dst.ap()[h0, :, :], outb[0:64, :], reads=oall, writes=['featd'], stream=6)
            sc.dma('pool', dst.ap()[h0 + 1, :, :], outb[64:128, :], reads=oall, writes=['featd'], stream=6)
            sc.maybe_phase()
        for vi, col in enumerate((1024 + 3 * 256, 1024 + 5 * 256)):
            if STOP == 'b1a':
                continue
            for pc in range(2):
                load_w_chunk(k, Win, col + pc * 128, 128, C['wst'][pc], C['wbf'][pc], C['gainT'], ('wst', pc), ('wbf', pc))
                sc.op('act', lambda e, pc=pc: e.activation(out=wv[:, :, pc * 128:(pc + 1) * 128], in_=C['wbf'][pc][:, :, :], func=AF.Copy), reads=[('wbf', pc)], writes=['wv'])
            for n in range(NT):
                b = n % 2
                for kc in range(8):
                    sc.op('pe', lambda e, kc=kc: e.matmul(PS[b][:, 0:256], lhsT=hT[:, kc, n * 128:(n + 1) * 128], rhs=wv[:, kc, :], start=(kc == 0), stop=(kc == 7)),
                          reads=['wv', ('hT', n)], writes=[('ps', b)])
                sc.op('act', lambda e: e.activation(out=vst[b][:, :], in_=PS[b][:, 0:256], func=AF.Copy), reads=[('ps', b)], writes=[('vst', b)])
                sc.dma('pool', Vd[vi].ap()[n * 128:(n + 1) * 128, :, :].rearrange("p h d -> p (h d)"), vst[b][:, :], reads=[('vst', b)], writes=['Vd'], stream=7)
        load_w_chunk(k, Win, 2560, 48, C['wst2'][0], C['wbf2'][0], C['gainT'], ('wst2', 0), ('wbf2', 0))
        for n in range(NT if STOP not in ('b1a', 'b1b') else 0):
            b = n % 2
            for kc in range(8):
                sc.op('pe', lambda e, kc=kc: e.matmul(PS[b][:, 0:48], lhsT=hT[:, kc, n * 128:(n + 1) * 128], rhs=C['wbf2'][0][:, kc, 0:48], start=(kc == 0), stop=(kc == 7)),
                      reads=[('wbf2', 0), ('hT', n)], writes=[('ps', b)])
            sc.op('act', lambda e: e.activation(out=gates[:, n, :], in_=PS[b][:, 0:48], func=AF.Sigmoid), reads=[('ps', b)], writes=['gates'])
        sc.phase()
    k.st = st2
    if STOP in ('b1', 'b1a', 'b1b'):
        for _ in range(int(os.environ.get('KXP2', 0))):
            sc.phase()
        copy_x(k, x_d, xo_d, C)
        k.st = old
        st2.close()
        return
    stopped = False
    kcT = k.sb([64, 4, 256], BF16)
    vaug = k.sb([128, 2, 4, 129], BF16)
    if not os.environ.get('KNOMS'):
        sc.op('dve', lambda e: e.memset(kcT[:, :, :], 0.0), writes=['kcT'])
    if not os.environ.get('KNOMS2'):
        sc.op('dve', lambda e: e.memset(vaug[:, :, :, :], 0.0), writes=['vaug'])
    with ExitStack() as st3:
        k.st = st3
        try:
            w1b = k.sb([64, 32, 256], BF16)
            w2b = k.sb([128, 2, 64], BF16)
            posT = k.sb([64, 32], BF16)
            posn = k.sb([32, 64], F32)
            w2f = k.sb([128, 2, 64], F32)
            tT = k.sb([64, S], BF16)
            biasc = k.sb([128, 2], F32)
            xg = [k.sb([128, 256], F32) for _ in range(4)]
            gT = k.sb([128, 2, 256], BF16)
            mstage = k.sb([128, 2, 64], F32)
            chk('c000')
            sc.op('dve', lambda e: e.memset(gT[:, :, :], 0.0), writes=['gT'])
            chk('c001')
            for ct in range(2):
                sc.dma('sp', mstage[:, ct, :], cdn['M'][ct * 128:(ct + 1) * 128, :], writes=['mstage'], stream=2)
            chk('c002')
            for n4 in range(4):
                sc.op('act', lambda e, n4=n4: e.activation(out=vaug[:, :, n4, 65:129], in_=mstage[:, :, :], func=AF.Copy), reads=['mstage', 'vaug'], writes=['vaug'])
            sc.op('dve', lambda e: e.memset(vaug[:, :, :, 64:65], 1.0), reads=['vaug'], writes=['vaug'])
            chk('c00')
            for kvi in range(2):
                w1v = P['b_cmp_w1'][kvi].rearrange("(l d) h -> d l h", d=64)
                for lq in range(8):
                    wstv = C['wst'][lq % 2][0:64, :, :].rearrange("p a b -> p (a b)").rearrange("p (l h) -> p l h", h=256)
                    sc.dma('sp', wstv, w1v[:, lq * 4:(lq + 1) * 4, :], writes=[('wst', lq % 2)], stream=1)
                    sc.op('act', lambda e, lq=lq, wstv=wstv: e.activation(out=w1b[:, lq * 4:(lq + 1) * 4, :], in_=wstv, func=AF.Copy), reads=[('wst', lq % 2)], writes=['w1b'])
                chk('c01')
                sc.dma('sp', w2f[:, :, :], P['b_cmp_w2'][kvi].rearrange("(c p) d -> p c d", p=128), writes=['w2f'], stream=2)
                sc.op('dve', lambda e: e.tensor_copy(out=w2b[:, :, :], in_=w2f[:, :, :]), reads=['w2f'], writes=['w2b'])
                chk('c02')
                sc.dma('sp', posf[0:32, 0:64].rearrange("p d -> p d") if False else posn[:, :], P['b_cmp_pos'][kvi], writes=['posn'], stream=2)
                sc.op('pe', lambda e: e.matmul(PS[6][0:64, 0:32], lhsT=posn[:, :], rhs=C['identf'][0:32, 0:32], start=True, stop=True), reads=['posn', 'const'], writes=[('ps', 6)])
                sc.op('dve', lambda e: e.tensor_copy(out=posT[:, :], in_=PS[6][0:64, 0:32]), reads=[('ps', 6)], writes=['posT'])
                chk('c0')
                for hc in range(2):
                    for l in range(32):
                        sc.op('pe', lambda e, l=l, hc=hc: e.matmul(PS[6][:, hc:hc + 1], lhsT=w1b[:, l, hc * 128:(hc + 1) * 128], rhs=posT[:, l:l + 1], start=(l == 0), stop=(l == 31)),
                              reads=['w1b', 'posT'], writes=[('ps', 6)])
                sc.op('dve', lambda e: e.tensor_copy(out=biasc[:, :], in_=PS[6][:, 0:2]), reads=[('ps', 6)], writes=['biasc'])
                chk('c1')
                for n4 in range(4):
                    sc.dma('sp', tT[:, :], kTd[kvi].ap()[n4, :, :], reads=['featd'], writes=['tT'], stream=4)
                    tv = tT[:, :].rearrange("p (c r) -> p c r", r=16)
                    for hc in range(2):
                        ph = PS[hc]
                        for l in range(32):
                            sc.op('pe', lambda e, l=l, hc=hc: e.matmul(ph[:, 0:255], lhsT=w1b[:, l, hc * 128:(hc + 1) * 128], rhs=tv[:, l // 16:l // 16 + 255, l % 16], start=(l == 0), stop=(l == 31)),
                                  reads=['w1b', 'tT'], writes=[('ps', hc)])
                        x0, x2, x3, th = xg
                        sc.op('dve', lambda e: e.tensor_scalar(out=x0[:, 0:255], in0=ph[:, 0:255], scalar1=biasc[:, hc:hc + 1], scalar2=None, op0=ALU.add), reads=[('ps', hc), 'biasc'], writes=['x0'])
                        sc.op('dve', lambda e: e.tensor_tensor(out=x2[:, 0:255], in0=x0[:, 0:255], in1=x0[:, 0:255], op=ALU.mult), reads=['x0'], writes=['x2'])
                        sc.op('dve', lambda e: e.tensor_tensor(out=x3[:, 0:255], in0=x2[:, 0:255], in1=x0[:, 0:255], op=ALU.mult), reads=['x0', 'x2'], writes=['x3'])
                        sc.op('dve', lambda e: e.scalar_tensor_tensor(out=x3[:, 0:255], in0=x3[:, 0:255], scalar=0.044715, in1=x0[:, 0:255], op0=ALU.mult, op1=ALU.add), reads=['x0', 'x3'], writes=['x3'])
                        sc.op('act', lambda e: e.activation(out=th[:, 0:255], in_=x3[:, 0:255], func=AF.Tanh, scale=0.7978845608028654), reads=['x3'], writes=['th'])
                        sc.op('dve', lambda e: e.scalar_tensor_tensor(out=th[:, 0:255], in0=th[:, 0:255], scalar=1.0, in1=x0[:, 0:255], op0=ALU.add, op1=ALU.mult), reads=['th', 'x0'], writes=['th'])
                        sc.op('dve', lambda e, hc=hc: e.tensor_scalar(out=gT[:, hc, 0:255], in0=th[:, 0:255], scalar1=0.5, scalar2=None, op0=ALU.mult), reads=['th'], writes=['gT'])
                        chk('c2')
                    if kvi == 0:
                        pk = PS[2]
                        for hc in range(2):
                            sc.op('pe', lambda e, hc=hc: e.matmul(pk[0:64, 0:256], lhsT=w2b[:, hc, :], rhs=gT[:, hc, :], start=(hc == 0), stop=(hc == 1)), reads=['w2b', 'gT'], writes=[('ps', 2)])
                        x0, x2, x3, th = xg
                        sc.op('dve', lambda e: e.tensor_copy(out=x0[0:64, :], in_=pk[0:64, 0:256]), reads=[('ps', 2)], writes=['x0'])
                        sc.op('act', lambda e: e.activation(out=x2[0:64, :], in_=x0[0:64, :], func=AF.Square), reads=['x0'], writes=['x2'])
                        sc.op('pe', lambda e: e.matmul(PS[3][0:64, 0:256], lhsT=cdn['blkf'][0:64, 0:64], rhs=x2[0:64, :], start=True, stop=True), reads=['x2', 'cdn'], writes=[('ps', 3)])
                        sc.op('dve', lambda e: e.tensor_scalar(out=x3[0:64, :], in0=PS[3][0:64, 0:256], scalar1=1.0 / 64, scalar2=EPS, op0=ALU.mult, op1=ALU.add), reads=[('ps', 3)], writes=['x3'])
                        sc.op('act', lambda e: e.activation(out=x3[0:64, :], in_=x3[0:64, :], func=AF.Sqrt), reads=['x3'], writes=['x3'])
                        sc.op('dve', lambda e: e.reciprocal(out=x3[0:64, :], in_=x3[0:64, :]), reads=['x3'], writes=['x3'])
                        sc.op('dve', lambda e, n4=n4: e.scalar_tensor_tensor(out=kcT[:, n4, :], in0=x0[0:64, :], scalar=gcol[0:64, 1:2], in1=x3[0:64, :], op0=ALU.mult, op1=ALU.mult),
                              reads=['x0', 'x3', 'gcol', 'kcT'], writes=['kcT'])
                    else:
                        for ct in range(2):
                            pv = PS[2 + ct]
                            for hc in range(2):
                                sc.op('pe', lambda e, hc=hc, ct=ct: e.matmul(pv[:, 0:64], lhsT=gT[:, hc, ct * 128:(ct + 1) * 128], rhs=w2b[:, hc, :], start=(hc == 0), stop=(hc == 1)),
                                      reads=['w2b', 'gT'], writes=[('ps', 2 + ct)])
                            sc.op('act', lambda e, ct=ct, n4=n4: e.activation(out=vaug[:, ct, n4, 0:64], in_=pv[:, 0:64], func=AF.Copy), reads=[('ps', 2 + ct), 'vaug'], writes=['vaug'])
            sc.op('dve', lambda e: e.memset(kcT[:, :, 255:256], 0.0), reads=['kcT'], writes=['kcT'])
            sc.phase()
        except StopHere:
            sc.phase()
            stopped = True
    k.st = st2
    if stopped or STOP == 'b1c':
        copy_x(k, x_d, xo_d, C)
        k.st = old
        st2.close()
        return
    B = {'Eb': [k.sb([128, 512], F32) for _ in range(2)], 'PT': [k.sb([128, 512], BF16) for _ in range(3)]}
    qTh = k.sb([64, S], BF16)
    kselT = k.sb([64, S], BF16)
    kwinT = k.sb([64, S], BF16)
    V1 = [k.sb([128, NT, 65], BF16) for _ in range(2)]
    oacc = k.sb([128, NT, 4, 64], F32)
    impacc = k.sb([128, NT, 64], F32)
    penT = k.sb([64, S], BF16)
    Bmat = k.sb([64, S], BF16)
    tabm = k.sb([128, 4096 + 128], F32)
    tslab = [k.sb([128, 512], F32) for _ in range(2)]
    smz = k.sb([128, 8], F32)
    imod = k.sb([128, 64], F32)
    iwk = k.sb([128, 64], F32)
    m8 = k.sb([128, 16], F32)
    pen = k.sb([128, 64], F32)
    obst = [k.sb([128, 256], BF16) for _ in range(2)]
    keepT = k.sb([128, 128], F32)
    addT = k.sb([128, 128], F32)
    sc.dma('sp', keepT[:, :], cdn['keepT'][:, :], writes=['keepT'], stream=2)
    sc.dma('sp', addT[:, :], cdn['addT'][:, :], writes=['keepT'], stream=2)
    bst = tabm[0:64, 0:S]
    sc.dma('sp', bst, cdn['Bmat'][:, :], writes=['tabm'], stream=2)
    sc.op('dve', lambda e: e.tensor_copy(out=Bmat[:, :], in_=bst), reads=['tabm'], writes=['Bmat'])
    for vi in range(2):
        sc.op('pool', lambda e, vi=vi: e.memset(V1[vi][:, :, 64:65], 1.0), writes=[('V1', vi)])
    for n4 in range(4):
        sc.dma('sp', kselT[:, :], kTd[2].ap()[n4, :, :], reads=['featd'], writes=['kselT'], stream=4)
        sc.dma('sp', kwinT[:, :], kTd[3].ap()[n4, :, :], reads=['featd'], writes=['kwinT'], stream=4)
        for vi in range(2):
            sc.dma('sp', V1[vi][:, :, 0:64], Vd[vi].ap().rearrange("(n p) h d -> p n h d", p=128)[:, :, n4, :], reads=['Vd'], writes=[('V1', vi)], stream=4)
        for g in range(4):
            hq = 4 * n4 + g
            sc.dma('sp', qTh[:, :], qTd.ap()[hq, :, :], reads=['featd'], writes=['qTh'], stream=5)
            build_head_tab(k, C, ohC, 6144, tabC, hq, relh, ohs, tslab, 'tabC')
            slabs = {}

            def tab_c(kt, q0, nq, hq=hq):
                key = (kt, q0)
                j = len(slabs) % 2
                slabs[key] = j
                dpp = q0 - (2048 * kt + 31)
                sc.dma('sp', tslab[j][:, 0:nq], AP(tabC.h, tabC.off + 2064 + dpp, [[WC - 16, 128], [1, nq]]), reads=['tabC'], writes=[('tslab', j)], stream=8)
                return tslab[j][:, 0:nq], ('tslab', j)

            def evac_c(qc, acc, g=g, hq=hq):
                for qb in range(4):
                    qt = qc * 4 + qb
                    a, ares = acc(qb)
                    sc.op('dve', lambda e: e.tensor_scalar(out=smz[:, 0:1], in0=a[:, 64:65], scalar1=1e-30, scalar2=None, op0=ALU.max), reads=[ares], writes=['smz'])
                    sc.op('dve', lambda e: e.reciprocal(out=smz[:, 1:2], in_=smz[:, 0:1]), reads=['smz'], writes=['smz1'])
                    sc.op('dve', lambda e: e.tensor_tensor(out=smz[:, 2:3], in0=smz[:, 1:2], in1=gates[:, qt, hq * 3:hq * 3 + 1], op=ALU.mult), reads=['smz1', 'gates'], writes=['smz2'])
                    sc.op('dve', lambda e: e.tensor_scalar(out=oacc[:, qt, g, :], in0=a[:, 0:64], scalar1=smz[:, 2:3], scalar2=(1.0 if 'c' in KBR else 0.0), op0=ALU.mult, op1=ALU.mult), reads=[ares, 'smz2'], writes=[('oacc', qt)])
                    if g == 0:
                        sc.op('dve', lambda e: e.tensor_scalar(out=impacc[:, qt, :], in0=a[:, 65:129], scalar1=smz[:, 1:2], scalar2=None, op0=ALU.mult), reads=[ares, 'smz1'], writes=[('imp', qt)])
                    else:
                        sc.op('dve', lambda e: e.scalar_tensor_tensor(out=impacc[:, qt, :], in0=a[:, 65:129], scalar=smz[:, 1:2], in1=impacc[:, qt, :], op0=ALU.mult, op1=ALU.add),
                              reads=[ares, 'smz1', ('imp', qt)], writes=[('imp', qt)])

            dense_attn(k, C, B, qTh,
                       lhsT_fn=lambda kt, n4=n4: (kcT[:, n4, kt * 128:(kt + 1) * 128], 'kcT'),
                       vaug_fn=lambda kt, n4=n4: (vaug[:, kt, n4, :], ['vaug']),
                       ncol=129, tab_fn=tab_c,
                       kts_fn=lambda qc: [0] if qc < 4 else [0, 1],
                       qb0_fn=lambda qc, kt: 0,
                       last_fn=lambda qc, qb: 0 if qc < 4 else 1,
                       evac_fn=evac_c)
        if STOP == 'b2':
            continue
        sc.op('pool', lambda e: e.memset(penT[:, 0:1024], 0.0), reads=['penT'], writes=['penT'])
        for qt in range(8, NT):
            ks = slice(64 - 2 * qt, 128 - 2 * qt)
            sc.op('dve', lambda e: e.tensor_tensor(out=imod[:, :], in0=impacc[:, qt, :], in1=keepT[:, ks], op=ALU.mult), reads=[('imp', qt), 'keepT'], writes=['imod'])
            sc.op('dve', lambda e: e.tensor_tensor(out=imod[:, :], in0=imod[:, :], in1=addT[:, ks], op=ALU.add), reads=['imod', 'keepT'], writes=['imod'])
            sc.op('dve', lambda e: e.memset(imod[:, 0:1], 3e9), reads=['imod'], writes=['imod'])
            sc.op('dve', lambda e: e.max(out=m8[:, 0:8], in_=imod[:, :]), reads=['imod'], writes=['m8'])
            sc.op('dve', lambda e: e.match_replace(out=iwk[:, :], in_to_replace=m8[:, 0:8], in_values=imod[:, :], imm_value=-3e38), reads=['imod', 'm8'], writes=['iwk'])
            sc.op('dve', lambda e: e.max(out=m8[:, 8:16], in_=iwk[:, :]), reads=['iwk', 'm8'], writes=['m8b'])
            sc.op('dve', lambda e: e.tensor_reduce(out=smz[:, 4:5], in_=m8[:, 8:16], axis=AX.X, op=ALU.min), reads=['m8b'], writes=['smz4'])
            sc.op('dve', lambda e: e.tensor_scalar(out=pen[:, :], in0=imod[:, :], scalar1=smz[:, 4:5], scalar2=None, op0=ALU.is_ge), reads=['imod', 'smz4'], writes=['pen'])
            sc.op('dve', lambda e: e.tensor_scalar(out=pen[:, :], in0=pen[:, :], scalar1=-1.0, scalar2=30000.0, op0=ALU.add, op1=ALU.mult), reads=['pen'], writes=['pen'])
            sc.op('pe', lambda e: e.matmul(PS[6][0:64, 0:128], lhsT=pen[:, :], rhs=C['identf'][:, :], start=True, stop=True), reads=['pen', 'const'], writes=[('ps', 6)])
            sc.op('act', lambda e, qt=qt: e.activation(out=penT[:, qt * 128:(qt + 1) * 128], in_=PS[6][0:64, 0:128], func=AF.Copy), reads=[('ps', 6)], writes=['penT'])
        for g in range(4):
            hq = 4 * n4 + g
            sc.dma('sp', qTh[:, :], qTd.ap()[hq, :, :], reads=['featd'], writes=['qTh'], stream=5)
            build_head_tab(k, C, ohD, 4224, tabD, hq, relh, ohs, tslab, 'tabD')
            build_head_tab(k, C, ohW, 1280, tabW, hq, relh, ohs, tslab, 'tabW')
            for br in (1, 2):
                if br == 1:
                    sc.dma('sp', tabm[:, 0:4096], AP(tabD.h, tabD.off + 127, [[WD - 1, 128], [1, 4096]]), reads=['tabD'], writes=['tabm'], stream=8)
                else:
                    sc.dma('sp', tabm[:, 0:1152], AP(tabW.h, tabW.off + 127, [[WW - 1, 128], [1, 1152]]), reads=['tabW'], writes=['tabm'], stream=8)

                def evac_sw(qc, acc, g=g, hq=hq, br=br):
                    for qb in range(4):
                        qt = qc * 4 + qb
                        a, ares = acc(qb)
                        sc.op('dve', lambda e: e.reciprocal(out=smz[:, 1:2], in_=a[:, 64:65]), reads=[ares], writes=['smz1'])
                        sc.op('dve', lambda e: e.scalar_tensor_tensor(out=smz[:, 2:3], in0=smz[:, 1:2], scalar=(1.0 if 'csw'[br] in KBR else 0.0), in1=gates[:, qt, hq * 3 + br:hq * 3 + br + 1], op0=ALU.mult, op1=ALU.mult), reads=['smz1', 'gates'], writes=['smz2'])
                        sc.op('dve', lambda e: e.scalar_tensor_tensor(out=oacc[:, qt, g, :], in0=a[:, 0:64], scalar=smz[:, 2:3], in1=oacc[:, qt, g, :], op0=ALU.mult, op1=ALU.add),
                              reads=[ares, 'smz2', ('oacc', qt)], writes=[('oacc', qt)])

                ksrc, kres, vsrc = (kselT, 'kselT', 0) if br == 1 else (kwinT, 'kwinT', 1)
                dense_attn(k, C, B, qTh,
                           lhsT_fn=lambda kt, ksrc=ksrc, kres=kres: (ksrc[:, kt * 128:(kt + 1) * 128], kres),
                           vaug_fn=lambda kt, vsrc=vsrc: (V1[vsrc][:, kt, :], [('V1', vsrc)]),
                           ncol=65,
                           tab_fn=lambda kt, q0, nq: (tabm[:, q0 - kt * 128:q0 - kt * 128 + nq], 'tabm'),
                           kts_fn=(lambda qc: list(range(0, 4 * qc + 4))) if br == 1 else (lambda qc: list(range(max(0, 4 * qc - 4), 4 * qc + 4))),
                           qb0_fn=lambda qc, kt: max(0, kt - 4 * qc),
                           last_fn=lambda qc, qb: 4 * qc + qb,
                           evac_fn=evac_sw,
                           pen_fn=(lambda kt, q0, nq: (Bmat[:, kt * 128:(kt + 1) * 128], penT[:, q0:q0 + nq], ['Bmat', 'penT'])) if br == 1 else None)
        for qt in range(NT):
            b = qt % 2
            sc.op('act', lambda e: e.activation(out=obst[b][:, :], in_=oacc[:, qt, :, :].rearrange("p g d -> p (g d)"), func=AF.Copy), reads=[('oacc', qt)], writes=[('obst', b)])
            sc.dma('pool', obd.ap()[qt * 128:(qt + 1) * 128, n4 * 256:(n4 + 1) * 256], obst[b][:, :], reads=[('obst', b)], writes=['obd'], stream=6)
        sc.phase()
    obt = [k.sb([128, D], BF16) for _ in range(2)]
    for n in range(NT):
        b = n % 2
        sc.dma('sp', obt[b][:, :], obd.ap()[n * 128:(n + 1) * 128, :], reads=['obd'], writes=[('obt', b)], stream=5)
        out_tail(k, C, obt[b][:, :], wob, oT, x_d, xo_d, n, ('obt', b))
    sc.phase()
    k.st = old
    st2.close()


def alloc_common(k):
    C = {}
    C['xt'] = [k.sb([128, 1024], F32) for _ in range(2)]
    C['sq'] = [k.sb([128, 1024], F32) for _ in range(2)]
    C['xb'] = [k.sb([128, 1024], BF16) for _ in range(2)]
    C['ss'] = [k.sb([128, 4], F32) for _ in range(2)]
    C['identb'] = k.sb([128, 128], BF16)
    C['identf'] = k.sb([128, 128], F32)
    C['gainT'] = k.sb([128, 8], F32)
    C['wst'] = [k.sb([128, 8, 128], F32) for _ in range(2)]
    C['wst2'] = [k.sb([128, 8, 128], F32) for _ in range(2)]
    C['wbf'] = [k.sb([128, 8, 128], BF16) for _ in range(2)]
    C['wbf2'] = [k.sb([128, 8, 128], BF16) for _ in range(2)]
    C['blk'] = k.sb([128, 128], BF16)
    C['onesb'] = k.sb([128, 128], BF16)
    C['relb'] = k.sb([33, 16], F32)
    return C


MIX_PARAMS = {
    0: ['norm1', 'a_w_in', 'a_q_gain', 'a_k_gain', 'a_w_out'],
    1: ['norm1', 'b_w_in', 'b_q_gain', 'b_k_gain', 'b_cmp_pos', 'b_cmp_w1', 'b_cmp_w2', 'b_w_out'],
    2: ['norm1', 'c_w_in', 'c_conv_w', 'c_a_log', 'c_dt_bias', 'c_out_gain', 'c_w_out'],
}
SHAPES = {
    'norm1': [D], 'norm2': [D], 'ffn_w_gate': [D, FF], 'ffn_w_up': [D, FF], 'ffn_w_down': [FF, D],
    'a_w_in': [D, 9216], 'a_q_gain': [3, 64], 'a_k_gain': [3, 64], 'a_w_out': [D, D],
    'b_w_in': [D, 2608], 'b_q_gain': [64], 'b_k_gain': [3, 64], 'b_cmp_pos': [2, 32, 64], 'b_cmp_w1': [2, 2048, 256],
    'b_cmp_w2': [2, 256, 64], 'b_w_out': [D, D],
    'c_w_in': [D, 4112], 'c_conv_w': [4, 3072], 'c_a_log': [8], 'c_dt_bias': [8], 'c_out_gain': [128], 'c_w_out': [D, D],
}


def consts():
    c = {'identf': np.eye(128, dtype=np.float32)}
    blk = np.zeros((128, 128), np.float32)
    blk[:64, :64] = 1
    blk[64:, 64:] = 1
    c['blkf'] = blk
    for g, (window, dil) in enumerate(A_GROUPS):
        d = np.arange(384) - 127
        c[f'ohA{g}'] = onehot_rows(d * dil, (d >= 0) & (d <= 128))
    i = np.arange(6144) - 2064
    c['ohC'] = onehot_rows(i, i >= 0)
    i = np.arange(4224) - 127
    c['ohD'] = onehot_rows(i, i >= 0)
    i = np.arange(1280) - 127
    c['ohW'] = onehot_rows(i, (i >= 0) & (i <= 511))
    cs_ = np.arange(256)[:, None] * 16
    ss_ = np.arange(64)[None, :] * 64
    c['cM'] = ((cs_ < ss_ + 64) & (cs_ + 32 > ss_) & (np.arange(256)[:, None] < 255)).astype(np.float32)
    qi = np.arange(128)[:, None]
    half = (qi >= 64).astype(np.int64)
    dl = np.arange(128)[None, :] - 64
    c['ckeepT'] = (dl < half - 1).astype(np.float32)
    c['caddT'] = np.where(dl == half, 2e9, np.where(dl == half - 1, 1e9, np.where(dl > half, -1e30, 0.0))).astype(np.float32)
    c['cBmat'] = (np.arange(4096)[None, :] // 64 == np.arange(64)[:, None]).astype(np.float32)
    p = np.arange(128)
    same = (p[:, None] // 64) == (p[None, :] // 64)
    c['cTB'] = (same & (p[:, None] <= p[None, :])).astype(np.float32)
    c['cBLK'] = same.astype(np.float32)
    c['cH0'] = np.repeat((p < 64).astype(np.float32)[:, None], 128, 1)
    c['cH1'] = np.repeat((p >= 64).astype(np.float32)[:, None], 128, 1)
    c['cONES'] = np.ones((128, 128), np.float32)
    c['cMT'] = np.where(same & (p[None, :] >= p[:, None]), 0.0, -1e30).astype(np.float32)
    c['cMBS'] = np.where(same & (p[None, :] < p[:, None]), 0.0, 1e30).astype(np.float32)
    return c


LAST_INPUT_NAMES = []


def build(layers=(0, 1, 2, 3), parts=('mix', 'ffn')):
    nc = bass.Bass("TRN2", target_bir_lowering=False)
    dr = {}
    LAST_INPUT_NAMES.clear()

    def din(name, shape):
        dr[name] = nc.dram_tensor(name, list(shape), F32, kind="ExternalInput").ap()
        LAST_INPUT_NAMES.append(name)
        return dr[name]

    x_in = din('x', [S, D])
    din('rel_bias', [32, 16])
    cs = consts()
    for name, v in cs.items():
        din(name, v.shape)
    for l in layers:
        p = f'l{l}_'
        if 'mix' in parts:
            for nm in MIX_PARAMS[l % 3]:
                din(p + nm, SHAPES[nm])
        if 'ffn' in parts:
            for nm in ('norm2', 'ffn_w_gate', 'ffn_w_up', 'ffn_w_down'):
                din(p + nm, SHAPES[nm])
    out = nc.dram_tensor('out', [S, D], F32, kind="ExternalOutput").ap()
    with ExitStack() as st:
        sc = Sched(nc, st)
        k = K(nc, st, sc)
        C = alloc_common(k)
        sc.dma('sp', C['identf'][:, :], dr['identf'][:, :], writes=['const'], stream=2)
        sc.op('dve', lambda e: e.tensor_copy(out=C['identb'][:, :], in_=C['identf'][:, :]), reads=['const'], writes=['const'])
        sc.dma('sp', C['identf'][:, :], dr['blkf'][:, :], reads=['const'], writes=['const'], stream=2)
        sc.op('dve', lambda e: e.tensor_copy(out=C['blk'][:, :], in_=C['identf'][:, :]), reads=['const'], writes=['const'])
        sc.dma('sp', C['identf'][:, :], dr['identf'][:, :], reads=['const'], writes=['const'], stream=2)
        sc.op('pool', lambda e: e.memset(C['relb'][:, :], -30000.0), writes=['relb'])
        sc.dma('sp', C['relb'][0:32, :], dr['rel_bias'][:, :], reads=['relb'], writes=['relb'], stream=2)
        cd = {}
        for nm in ('TB', 'BLK', 'H0', 'H1', 'ONES', 'MT', 'MBS'):
            cd[nm] = k.sb([128, 128], F32)
            sc.dma('sp', cd[nm][:, :], dr['c' + nm][:, :], writes=['cd'], stream=2)
        sc.op('dve', lambda e: e.tensor_copy(out=C['onesb'][:, :], in_=cd['ONES'][:, :]), reads=['cd'], writes=['const'])
        tabA = None
        kinds = set(l % 3 for l in layers) if 'mix' in parts else set()
        with ExitStack() as stt:
            k.st = stt
            C['oh'] = k.sb([33, 512], F32)
            C['relrep'] = k.sb([33, 16, 128], F32)
            C['rowst'] = [k.sb([128, 512], F32) for _ in range(2)]
            sc.op('dve', lambda e: e.tensor_copy(out=C['relrep'][:, :, :], in_=bc(C['relb'][:, :].unsqueeze(2), [33, 16, 128])), reads=['relb'], writes=['relrep'])
            if 0 in kinds:
                tabA = [k.dram([16, 128, 384]) for _ in range(3)]
                for g in range(3):
                    build_bias_rows(k, C, dr[f'ohA{g}'], 384, tabA[g])
            tabs = (dr['ohC'], dr['ohD'], dr['ohW'])
            sc.phase()
        k.st = st
        k.doff += int(os.environ.get('KDB', 0))
        k.dbase = k.doff
        cdn = {'M': dr['cM'], 'keepT': dr['ckeepT'], 'addT': dr['caddT'], 'Bmat': dr['cBmat'], 'blkf': cd['BLK']}
        cur = x_in
        if os.environ.get('KINPLACE'):
            copy_x(k, x_in, out, C)
            cur = out
        for l in layers:
            p = f'l{l}_'
            P = {nm[len(p):]: ap for nm, ap in dr.items() if nm.startswith(p)}
            if 'mix' in parts:
                if l % 3 == 0:
                    mixer_a(k, cur, out, P, C, tabA)
                elif l % 3 == 2:
                    mixer_c(k, cur, out, P, C, cd)
                else:
                    mixer_b(k, cur, out, P, C, tabs, cdn)
                cur = out
            if 'ffn' in parts and not (os.environ.get('KLASTMIX') and l == layers[-1]):
                ffn_layer(k, cur, out, P['norm2'], P['ffn_w_gate'], P['ffn_w_up'], P['ffn_w_down'], C)
                cur = out
    return nc


_NC_CACHE = {}


def kernel(**inputs):
    if 'nc' not in _NC_CACHE:
        _NC_CACHE['nc'] = build()
        _NC_CACHE['names'] = list(LAST_INPUT_NAMES)
    nc = _NC_CACHE['nc']
    cs = consts()
    x = np.ascontiguousarray(np.asarray(inputs['x'], dtype=np.float32))
    shared = {}
    for nm in _NC_CACHE['names']:
        if nm == 'x':
            continue
        if nm in cs:
            shared[nm] = cs[nm]
        else:
            shared[nm] = np.ascontiguousarray(np.asarray(inputs[nm], dtype=np.float32))
    in_maps = []
    for b in range(8):
        m = dict(shared)
        m['x'] = x[b]
        in_maps.append(m)
    res = run_bass_kernel_spmd(nc, in_maps, core_ids=list(range(8)))
    return np.stack([np.asarray(r['out'], dtype=np.float32) for r in res.results], axis=0)
```

```python
import os
import numpy as np
from contextlib import ExitStack
import concourse.bass as bass
import concourse.mybir as mybir
from concourse.ap import AP
from concourse.bass_utils import run_bass_kernel_spmd

F32, BF16 = mybir.dt.float32, mybir.dt.bfloat16
ALU, AF, AX = mybir.AluOpType, mybir.ActivationFunctionType, mybir.AxisListType

S = 4096
D = 1024
NT = S // 128
FF = 2816
NDMA = 32
EPS = 1e-6
PHASE_LIMIT = 20000


class Sched:
    CE = ('pe', 'act', 'dve', 'pool')

    def __init__(self, nc, st):
        self.nc = nc
        self.e = {'pe': nc.tensor, 'act': nc.scalar, 'dve': nc.vector, 'pool': nc.gpsimd, 'sp': nc.sync}
        self.sets = []
        for s in range(2):
            d = {k: st.enter_context(nc.semaphore(f"s{s}_{k}")) for k in self.CE}
            self.sets.append(d)
        self.dsem = {('d', i): st.enter_context(nc.semaphore(f"dq{i}")) for i in range(NDMA)}
        self.dcnt = [0] * NDMA
        self.skey = {}
        self.spsem = st.enter_context(nc.semaphore('spg'))
        self.phase_no = 0
        self.cur = 0
        self._reset()

    def _reset(self):
        self.cnt = {k: 0 for k in self.CE}
        self.lastw = {}
        self.readers = {}
        self.seen = {k: {} for k in self.CE + ('sp',)}

    def _sem(self, k):
        return self.dsem[k] if isinstance(k, tuple) else self.sets[self.cur][k]

    def _wait(self, eng, reads, writes):
        need = {}
        for r in reads:
            t = self.lastw.get(r)
            if t:
                need[t[0]] = max(need.get(t[0], 0), t[1])
        for w in writes:
            t = self.lastw.get(w)
            if t:
                need[t[0]] = max(need.get(t[0], 0), t[1])
            for k, v in self.readers.get(w, {}).items():
                need[k] = max(need.get(k, 0), v)
        for k, v in need.items():
            if k == 'pe' and eng == 'pe':
                continue
            if isinstance(k, tuple):
                v = 16 * self.dcnt[k[1]]
            if self.seen[eng].get(k, 0) < v:
                self.e[eng].wait_ge(self._sem(k), v)
                self.seen[eng][k] = v

    def _commit(self, tok, reads, writes):
        for r in reads:
            d = self.readers.setdefault(r, {})
            d[tok[0]] = max(d.get(tok[0], 0), tok[1])
        for w in writes:
            self.lastw[w] = tok
            self.readers[w] = {}

    def op(self, eng, fn, reads=(), writes=()):
        self._wait(eng, reads, writes)
        self.cnt[eng] += 1
        fn(self.e[eng]).then_inc(self.sets[self.cur][eng], 1)
        self._commit((eng, self.cnt[eng]), reads, writes)

    def dma(self, q, out, in_, reads=(), writes=(), stream=None, **kw):
        key = reads[0] if (reads and (not writes or not isinstance(writes[0], tuple)) and isinstance(reads[0], tuple)) else (writes[0] if writes else reads[0])
        half = NDMA // 2
        qk = (q, key)
        if qk not in self.skey:
            nq_ = sum(1 for kk in self.skey if kk[0] == q)
            self.skey[qk] = (0 if q == 'sp' else half) + nq_ % half
        stream = self.skey[qk]
        self._wait(q, reads, writes)
        self.dcnt[stream] += 1
        self.e[q].dma_start(out=out, in_=in_, **kw).then_inc(self.dsem[('d', stream)], 16)
        self._commit((('d', stream), 16 * self.dcnt[stream]), reads, writes)

    def maybe_phase(self):
        if max(self.cnt.values()) > PHASE_LIMIT:
            self.phase()

    def phase(self):
        A = self.sets[self.cur]
        B = self.sets[1 - self.cur]
        sp = self.e['sp']
        for k in self.CE:
            if self.cnt[k]:
                sp.wait_ge(A[k], self.cnt[k])
        for i, c in enumerate(self.dcnt):
            if c:
                sp.wait_ge(self.dsem[('d', i)], 16 * c)
        for k in self.CE:
            sp.sem_clear(B[k])
        self.phase_no += 1
        sp.sem_inc(self.spsem, 1)
        for k in self.CE:
            self.e[k].wait_ge(self.spsem, self.phase_no)
        self.cur = 1 - self.cur
        self._reset()


def bc(ap, shape):
    return ap.to_broadcast(list(shape))


POOL_ELEMS = 15200000


class DT:
    def __init__(self, h, off, shape, dt, nf):
        self.h, self.off, self.shape, self.dt, self.nf = h, off, shape, dt, nf

    def ap(self):
        flat = AP(self.h, self.off, [[1, self.nf]])
        if self.dt != F32:
            flat = flat.bitcast(self.dt)
        names = [f"d{i}" for i in range(len(self.shape))]
        return flat.rearrange("(" + " ".join(names) + ") -> " + " ".join(names), **{nm: sz for nm, sz in zip(names[1:], self.shape[1:])})


class K:
    def __init__(self, nc, st, sc):
        self.nc, self.st, self.sc = nc, st, sc
        self.ps = [st.enter_context(nc.psum_tensor(f"ps{i}", [128, 512], F32)) for i in range(7)]
        self.psb = st.enter_context(nc.psum_tensor("psb", [128, 1024], BF16))
        self.n_sb = 0
        self.pool = nc.dram_tensor('pool', [POOL_ELEMS], F32, kind='Internal')
        self.doff = 0

    def sb(self, shape, dt, name=None):
        self.n_sb += 1
        return self.st.enter_context(self.nc.sbuf_tensor(name or f"t{self.n_sb}", list(shape), dt))

    def dram(self, shape, dt=F32):
        n = int(np.prod(shape))
        nf = n if dt == F32 else (n + 1) // 2
        d = DT(self.pool, self.doff, list(shape), dt, nf)
        self.doff += (nf + 63) // 64 * 64
        assert self.doff <= POOL_ELEMS, self.doff
        return d


def load_w_chunk(k, W, col0, ncols, wst, wbf, gainT, rs_st, rs_bf, kc_n=8, stream=1):
    sc = k.sc
    src = W.rearrange("(kc p) n -> p kc n", p=128)[:, :, col0:col0 + ncols]
    sc.dma('sp', wst[:, 0:kc_n, 0:ncols], src, writes=[rs_st], stream=stream)
    if gainT is None:
        sc.op('dve', lambda e: e.tensor_copy(out=wbf[:, 0:kc_n, 0:ncols], in_=wst[:, 0:kc_n, 0:ncols]),
              reads=[rs_st], writes=[rs_bf])
    else:
        sc.op('dve', lambda e: e.tensor_tensor(out=wbf[:, 0:kc_n, 0:ncols], in0=wst[:, 0:kc_n, 0:ncols],
                                               in1=bc(gainT[:, 0:kc_n].unsqueeze(2), [128, kc_n, ncols]), op=ALU.mult),
              reads=[rs_st, 'gain'], writes=[rs_bf])


def load_gainT(k, g_dram, gainT):
    k.sc.dma('sp', gainT[:, :], g_dram.rearrange("(kc p) -> p kc", p=128), writes=['gain'], stream=2,
             allow_slow_non_contiguous=True)


def norm_tiles(k, x_d, hT, tiles, tok_of_tile, C):
    sc = k.sc
    for slot, n in enumerate(tiles):
        b = slot % 2
        xt, sq, ss, xb = C['xt'][b], C['sq'][b], C['ss'][b], C['xb'][b]
        sc.dma('sp', xt[:, :], x_d[n * 128:(n + 1) * 128, :], writes=[('xt', b)], stream=0)
        sc.op('act', lambda e: e.activation(out=sq[:, :], in_=xt[:, :], func=AF.Square), reads=[('xt', b)], writes=[('sq', b)])
        sc.op('dve', lambda e: e.reduce_sum(out=ss[:, 0:1], in_=sq[:, :], axis=AX.X), reads=[('sq', b)], writes=[('ss', b)])
        sc.op('dve', lambda e: e.tensor_scalar(out=ss[:, 0:1], in0=ss[:, 0:1], scalar1=1.0 / D, scalar2=EPS, op0=ALU.mult, op1=ALU.add),
              reads=[('ss', b)], writes=[('ss', b)])
        sc.op('act', lambda e: e.activation(out=ss[:, 1:2], in_=ss[:, 0:1], func=AF.Sqrt), reads=[('ss', b)], writes=[('ss2', b)])
        sc.op('dve', lambda e: e.reciprocal(out=ss[:, 2:3], in_=ss[:, 1:2]), reads=[('ss2', b)], writes=[('ss3', b)])
        sc.op('act', lambda e: e.activation(out=xb[:, :], in_=xt[:, :], func=AF.Copy, scale=ss[:, 2:3]),
              reads=[('xt', b), ('ss3', b)], writes=[('xb', b)])
        for kc in range(8):
            sc.op('pe', lambda e, kc=kc: e.transpose(out=k.psb[:, kc * 128:(kc + 1) * 128], in_=xb[:, kc * 128:(kc + 1) * 128],
                                                      identity=C['identb'][:, :]),
                  reads=[('xb', b), 'const'], writes=['psb'])
        sc.op('dve', lambda e: e.tensor_copy(out=hT[:, :, slot * 128:(slot + 1) * 128],
                                             in_=k.psb[:, :].rearrange("p (kc t) -> p kc t", kc=8)),
              reads=['psb'], writes=[('hT', slot)])


def ffn_layer(k, x_d, xo_d, norm2, wg, wu, wd, C):
    sc = k.sc
    st2 = ExitStack()
    old = k.st
    k.st = st2
    hT = k.sb([128, 8, 1024], BF16)
    actT = k.sb([128, 22, 1024], BF16)
    wdb = k.sb([128, 22, 1024], BF16)
    C['sg'] = [k.sb([128, 512], F32) for _ in range(2)]
    load_gainT(k, norm2, C['gainT'])
    for hc in range(22):
        b = hc % 2
        sc.dma('sp', C['wst'][b][:, 0:8, :].rearrange("p a b -> p (a b)"), wd[hc * 128:(hc + 1) * 128, :], writes=[('wst', b)], stream=1)
        sc.op('act', lambda e, hc=hc, b=b: e.activation(out=wdb[:, hc, :], in_=C['wst'][b][:, 0:8, :].rearrange("p a b -> p (a b)"), func=AF.Copy),
              reads=[('wst', b)], writes=['wdb'])
    for tb in range(4):
        norm_tiles(k, x_d, hT, range(tb * 8, tb * 8 + 8), None, C)
        hres = [('hT', s) for s in range(8)]
        for hc in range(22):
            b = hc % 2
            load_w_chunk(k, wg, hc * 128, 128, C['wst'][b], C['wbf'][b], C['gainT'], ('wst', b), ('wbf', b))
            load_w_chunk(k, wu, hc * 128, 128, C['wst2'][b], C['wbf2'][b], C['gainT'], ('wst2', b), ('wbf2', b))
            for sb in range(2):
                pg, pu = k.ps[sb * 2], k.ps[sb * 2 + 1]
                for kc in range(8):
                    sc.op('pe', lambda e, kc=kc: e.matmul(pg[:, :], lhsT=C['wbf'][b][:, kc, :], rhs=hT[:, kc, sb * 512:(sb + 1) * 512],
                                                         start=(kc == 0), stop=(kc == 7)),
                          reads=[('wbf', b)] + hres[sb * 4:sb * 4 + 4], writes=[('ps', sb * 2)])
                for kc in range(8):
                    sc.op('pe', lambda e, kc=kc: e.matmul(pu[:, :], lhsT=C['wbf2'][b][:, kc, :], rhs=hT[:, kc, sb * 512:(sb + 1) * 512],
                                                         start=(kc == 0), stop=(kc == 7)),
                          reads=[('wbf2', b)] + hres[sb * 4:sb * 4 + 4], writes=[('ps', sb * 2 + 1)])
                sg = C['sg'][sb]
                sc.op('act', lambda e: e.activation(out=sg[:, :], in_=pg[:, :], func=AF.Silu), reads=[('ps', sb * 2)], writes=[('sg', sb)])
                sc.op('dve', lambda e: e.tensor_tensor(out=actT[:, hc, sb * 512:(sb + 1) * 512], in0=pu[:, :], in1=sg[:, :], op=ALU.mult),
                      reads=[('ps', sb * 2 + 1), ('sg', sb)], writes=[('actT', sb)])
        for tt in range(8):
            n = tb * 8 + tt
            b = tt % 2
            xt = C['xt'][b]
            sc.dma('sp', xt[:, :], x_d[n * 128:(n + 1) * 128, :], writes=[('xt', b)], stream=0)
            for half in range(2):
                po = k.ps[4 + half]
                for hc in range(22):
                    sc.op('pe', lambda e, hc=hc: e.matmul(po[:, :], lhsT=actT[:, hc, tt * 128:(tt + 1) * 128], rhs=wdb[:, hc, half * 512:(half + 1) * 512],
                                                         start=(hc == 0), stop=(hc == 21)),
                          reads=[('actT', tt // 4), 'wdb'], writes=[('ps', 4 + half)])
                sc.op('dve', lambda e: e.tensor_tensor(out=xt[:, half * 512:(half + 1) * 512], in0=po[:, :], in1=xt[:, half * 512:(half + 1) * 512], op=ALU.add),
                      reads=[('ps', 4 + half), ('xt', b)], writes=[('xt', b)])
            sc.dma('pool', xo_d[n * 128:(n + 1) * 128, :], xt[:, :], reads=[('xt', b)], writes=[], stream=3)
        sc.maybe_phase()
    sc.phase()
    k.st = old
    st2.close()


A_GROUPS = ((128, 1), (512, 4), (2048, 16))


def rel_bucket_np(dist):
    dist = np.maximum(dist, 0)
    d_f = np.maximum(dist, 1).astype(np.float32)
    large = 16 + (np.log(d_f / np.float32(16)) / np.float32(np.log(2048 / 16)) * np.float32(16)).astype(np.int32)
    return np.where(dist < 16, dist, np.minimum(large, 31))


def onehot_rows(dists, valid):
    n = len(dists)
    oh = np.zeros((33, n), np.float32)
    b = rel_bucket_np(np.asarray(dists))
    for i in range(n):
        oh[b[i] if valid[i] else 32, i] = 1.0
    return oh


def build_bias_rows(k, C, oh_d, width, tab_d):
    sc = k.sc
    u = 0
    for c0 in range(0, width, 512):
        n = min(512, width - c0)
        sc.dma('sp', C['oh'][:, 0:n], oh_d[:, c0:c0 + n], writes=['oh'], stream=2)
        for h in range(16):
            b = u % 2
            u += 1
            sc.op('pe', lambda e, h=h, b=b: e.matmul(k.ps[b][:, 0:n], lhsT=C['relrep'][:, h, :], rhs=C['oh'][:, 0:n], start=True, stop=True),
                  reads=['oh', 'relrep'], writes=[('ps', b)])
            sc.op('act', lambda e, b=b: e.activation(out=C['rowst'][b][:, 0:n], in_=k.ps[b][:, 0:n], func=AF.Exp), reads=[('ps', b)], writes=[('rowst', b)])
            sc.dma('pool', tab_d.ap()[h, :, c0:c0 + n], C['rowst'][b][:, 0:n], reads=[('rowst', b)], writes=['tab'], stream=4)


def skew_tab(tab_d, h, W, off, pstride, n):
    return AP(tab_d.h, tab_d.off + h * 128 * W + off, [[W - pstride, 128], [1, n]])


def mixer_a(k, x_d, xo_d, P, C, tabA):
    sc = k.sc
    st2 = ExitStack()
    old = k.st
    k.st = st2
    k.doff = k.dbase
    hT = k.sb([128, 8, S], BF16)
    qT = k.sb([128, S], BF16)
    kT = k.sb([128, S], BF16)
    V1 = k.sb([128, NT, 2, 65], BF16)
    OZ = k.sb([128, NT, 2, 65], F32)
    wob = k.sb([128, 8, D], BF16)
    gcol = k.sb([128, 6], F32)
    r1 = [k.sb([128, 512], F32) for _ in range(2)]
    sqb = [k.sb([128, 512], BF16) for _ in range(2)]
    Eb = [k.sb([128, 256], F32) for _ in range(3)]
    PT = [k.sb([128, 256], BF16) for _ in range(4)]
    tab = [k.sb([128, 256], F32) for _ in range(2)]
    mrg = [k.sb([128, 16, 65], F32) for _ in range(3)]
    rz = k.sb([128, 16], F32)
    ob = k.sb([128, 16, 64], BF16)
    oT = k.sb([128, 8, 128], BF16)
    ozd = [k.dram([S, 16 * 65]) for _ in range(3)]
    load_gainT(k, P['norm1'], C['gainT'])
    for half in range(2):
        sc.dma('sp', gcol[half * 64:(half + 1) * 64, 0:3], P['a_q_gain'].rearrange("g d -> d g"), writes=['gcol'], stream=2, allow_slow_non_contiguous=True)
        sc.dma('sp', gcol[half * 64:(half + 1) * 64, 3:6], P['a_k_gain'].rearrange("g d -> d g"), writes=['gcol'], stream=2, allow_slow_non_contiguous=True)
    sc.op('pool', lambda e: e.memset(V1[:, :, :, 64:65], 1.0), writes=['V1ones'])
    for kc in range(8):
        b = kc % 2
        wflat = C['wst'][b][:, :, :].rearrange("p a b -> p (a b)")
        sc.dma('sp', wflat, P['a_w_out'][kc * 128:(kc + 1) * 128, :], writes=[('wst', b)], stream=1)
        sc.op('act', lambda e, kc=kc, wflat=wflat: e.activation(out=wob[:, kc, :], in_=wflat, func=AF.Copy), reads=[('wst', b)], writes=['wob'])
    norm_tiles(k, x_d, hT, range(NT), None, C)
    hall = [('hT', s) for s in range(NT)]
    Win = P['a_w_in']
    for g, (window, dil) in enumerate(A_GROUPS):
        L = S // dil
        nb = L // 128
        cw = min(512, L)
        hTs = [hT[:, kc, :].rearrange("p (m d) -> p d m", d=dil) for kc in range(8)]
        for hp in range(8):
            cq = (g * 3 + 0) * 1024 + hp * 128
            load_w_chunk(k, Win, cq, 128, C['wst'][0], C['wbf'][0], C['gainT'], ('wst', 0), ('wbf', 0))
            load_w_chunk(k, Win, cq + 1024, 128, C['wst'][1], C['wbf'][1], C['gainT'], ('wst', 1), ('wbf', 1))
            load_w_chunk(k, Win, cq + 2048, 128, C['wst2'][0], C['wbf2'][0], C['gainT'], ('wst2', 0), ('wbf2', 0))
            for j in range(2):
                sc.dma('sp', tab[j][:, :], skew_tab(tabA[g], hp * 2 + j, 384, 127, 1, 256), writes=[('tab', j)], stream=5)
            for which, (wb, wres, dst, gc) in enumerate(((C['wbf'][0], ('wbf', 0), qT, g), (C['wbf'][1], ('wbf', 1), kT, 3 + g))):
                dname = 'qT' if which == 0 else 'kT'
                for c in range(S // cw):
                    r, m0 = (c * cw) // L, (c * cw) % L
                    b = c % 2
                    pqi, pssi = (4, 5) if b == 0 else (6, 3)
                    pq, pss = k.ps[pqi], k.ps[pssi]
                    for kc in range(8):
                        sc.op('pe', lambda e, kc=kc: e.matmul(pq[:, 0:cw], lhsT=wb[:, kc, :], rhs=hTs[kc][:, r, m0:m0 + cw], start=(kc == 0), stop=(kc == 7)),
                              reads=[wres] + hall, writes=[('ps', pqi)])
                    sc.op('act', lambda e: e.activation(out=sqb[b][:, 0:cw], in_=pq[:, 0:cw], func=AF.Square), reads=[('ps', pqi)], writes=[('sqb', b)])
                    sc.op('pe', lambda e: e.matmul(pss[:, 0:cw], lhsT=C['blk'][:, :], rhs=sqb[b][:, 0:cw], start=True, stop=True),
                          reads=[('sqb', b), 'const'], writes=[('ps', pssi)])
                    sc.op('dve', lambda e: e.tensor_scalar(out=r1[b][:, 0:cw], in0=pss[:, 0:cw], scalar1=1.0 / 64, scalar2=EPS, op0=ALU.mult, op1=ALU.add),
                          reads=[('ps', pssi)], writes=[('r1', b)])
                    sc.op('act', lambda e: e.activation(out=r1[b][:, 0:cw], in_=r1[b][:, 0:cw], func=AF.Sqrt), reads=[('r1', b)], writes=[('r1', b)])
                    sc.op('dve', lambda e: e.reciprocal(out=r1[b][:, 0:cw], in_=r1[b][:, 0:cw]), reads=[('r1', b)], writes=[('r1', b)])
                    tl = [(dname, t) for t in range(c * cw // 128, (c + 1) * cw // 128)]
                    sc.op('dve', lambda e: e.scalar_tensor_tensor(out=dst[:, c * cw:(c + 1) * cw], in0=pq[:, 0:cw], scalar=gcol[:, gc:gc + 1], in1=r1[b][:, 0:cw],
                                                                   op0=ALU.mult, op1=ALU.mult),
                          reads=[('ps', pqi), ('r1', b), 'gcol'], writes=tl)
            for n in range(NT):
                r, m0 = (n * 128) // L, (n * 128) % L
                pvi = 6 if n % 2 == 0 else 5
                pv = k.ps[pvi]
                for kc in range(8):
                    sc.op('pe', lambda e, kc=kc: e.matmul(pv[:, 0:128], lhsT=hTs[kc][:, r, m0:m0 + 128], rhs=C['wbf2'][0][:, kc, :], start=(kc == 0), stop=(kc == 7)),
                          reads=[('wbf2', 0)] + hall, writes=[('ps', pvi)])
                sc.op('act', lambda e: e.activation(out=V1[:, n, :, 0:64], in_=pv[:, 0:128].rearrange("p (h d) -> p h d", h=2), func=AF.Copy),
                      reads=[('ps', pvi)], writes=[('V1', n)])
            units = [(h2, r, j) for h2 in range(2) for r in range(dil) for j in range(nb)]
            SB = (0, 1, 6)

            def emit_s(u):
                h2, r, j = units[u]
                ps_ = slice(64 * h2, 64 * h2 + 64)
                n = r * nb + j
                nq = 256 if j < nb - 1 else 128
                pS = k.ps[SB[u % 3]]
                sc.op('pe', lambda e: e.matmul(pS[:, 0:nq], lhsT=kT[ps_, n * 128:(n + 1) * 128], rhs=qT[ps_, n * 128:n * 128 + nq], start=True, stop=True),
                      reads=[('kT', n), ('qT', n)] + ([('qT', n + 1)] if nq == 256 else []), writes=[('ps', SB[u % 3])])

            pend = []

            def evac(u_, n_, h2_):
                pO_ = k.ps[2 + u_ % 4]
                sc.op('act', lambda e: e.activation(out=OZ[:, n_, h2_, :], in_=pO_[:, 0:65], func=AF.Copy), reads=[('ps', 2 + u_ % 4)], writes=[('OZ', n_)])

            emit_s(0)
            emit_s(1)
            for u, (h2, r, j) in enumerate(units):
                if u + 2 < len(units):
                    emit_s(u + 2)
                n = r * nb + j
                nq = 256 if j < nb - 1 else 128
                pS, pO = k.ps[SB[u % 3]], k.ps[2 + u % 4]
                eb, pt, ptp = Eb[u % 3], PT[u % 4], PT[(u - 1) % 4]
                sc.op('act', lambda e: e.activation(out=eb[:, 0:nq], in_=pS[:, 0:nq], func=AF.Exp, scale=0.125), reads=[('ps', SB[u % 3])], writes=[('Eb', u % 3)])
                sc.op('dve', lambda e: e.tensor_tensor(out=pt[:, 0:nq], in0=eb[:, 0:nq], in1=tab[h2][:, 0:nq], op=ALU.mult),
                      reads=[('Eb', u % 3), ('tab', h2)], writes=[('PT', u % 4)])
                if j > 0:
                    sc.op('pe', lambda e: e.matmul(pO[:, 0:65], lhsT=ptp[:, 128:256], rhs=V1[:, n - 1, h2, :], start=True, stop=False),
                          reads=[('PT', (u - 1) % 4), ('V1', n - 1), 'V1ones'], writes=[('ps', 2 + u % 4)])
                sc.op('pe', lambda e: e.matmul(pO[:, 0:65], lhsT=pt[:, 0:128], rhs=V1[:, n, h2, :], start=(j == 0), stop=True),
                      reads=[('PT', u % 4), ('V1', n), 'V1ones'], writes=[('ps', 2 + u % 4)])
                pend.append((u, n, h2))
                if len(pend) > 2:
                    evac(*pend.pop(0))
            while pend:
                evac(*pend.pop(0))
            ozv = ozd[g].ap().rearrange("(m d) c -> d m c", d=dil)
            for n in range(NT):
                r, j = n // nb, n % nb
                sc.dma('pool', ozv[r, j * 128:(j + 1) * 128, hp * 130:(hp + 1) * 130], OZ[:, n, :, :].rearrange("p h c -> p (h c)"),
                       reads=[('OZ', n)], writes=['ozd'], stream=6)
            sc.maybe_phase()
    sc.phase()
    for n in range(NT):
        b = n % 2
        xt = C['xt'][b]
        sc.dma('sp', xt[:, :], x_d[n * 128:(n + 1) * 128, :], writes=[('xt', b)], stream=0)
        for g in range(3):
            sc.dma('sp', mrg[g][:, :, :].rearrange("p h c -> p (h c)"), ozd[g].ap()[n * 128:(n + 1) * 128, :], writes=[('mrg', g)], stream=7)
        sc.op('dve', lambda e: e.tensor_tensor(out=mrg[0][:, :, :], in0=mrg[0][:, :, :], in1=mrg[1][:, :, :], op=ALU.add), reads=[('mrg', 0), ('mrg', 1)], writes=[('mrg', 0)])
        sc.op('dve', lambda e: e.tensor_tensor(out=mrg[0][:, :, :], in0=mrg[0][:, :, :], in1=mrg[2][:, :, :], op=ALU.add), reads=[('mrg', 0), ('mrg', 2)], writes=[('mrg', 0)])
        sc.op('dve', lambda e: e.reciprocal(out=rz[:, :], in_=mrg[0][:, :, 64]), reads=[('mrg', 0)], writes=['rz'])
        sc.op('dve', lambda e: e.tensor_tensor(out=ob[:, :, :], in0=mrg[0][:, :, 0:64], in1=bc(rz[:, :].unsqueeze(2), [128, 16, 64]), op=ALU.mult),
              reads=[('mrg', 0), 'rz'], writes=['ob'])
        obf = ob[:, :, :].rearrange("p h d -> p (h d)")
        for kc in range(8):
            sc.op('pe', lambda e, kc=kc: e.transpose(out=k.psb[:, kc * 128:(kc + 1) * 128], in_=obf[:, kc * 128:(kc + 1) * 128], identity=C['identb'][:, :]),
                  reads=['ob', 'const'], writes=['psb'])
        sc.op('act', lambda e: e.activation(out=oT[:, :, :], in_=k.psb[:, :].rearrange("p (kc t) -> p kc t", kc=8), func=AF.Copy), reads=['psb'], writes=['oT'])
        for half in range(2):
            po = k.ps[4 + half]
            for kc in range(8):
                sc.op('pe', lambda e, kc=kc: e.matmul(po[:, :], lhsT=oT[:, kc, :], rhs=wob[:, kc, half * 512:(half + 1) * 512], start=(kc == 0), stop=(kc == 7)),
                      reads=['oT', 'wob'], writes=[('ps', 4 + half)])
            sc.op('dve', lambda e: e.tensor_tensor(out=xt[:, half * 512:(half + 1) * 512], in0=po[:, :], in1=xt[:, half * 512:(half + 1) * 512], op=ALU.add),
                  reads=[('ps', 4 + half), ('xt', b)], writes=[('xt', b)])
        sc.dma('pool', xo_d[n * 128:(n + 1) * 128, :], xt[:, :], reads=[('xt', b)], stream=3)
    sc.phase()
    k.st = old
    st2.close()


def out_tail(k, C, obf, wob, oT, x_d, xo_d, n, obres):
    sc = k.sc
    b = n % 2
    xt = C['xt'][b]
    sc.dma('sp', xt[:, :], x_d[n * 128:(n + 1) * 128, :], writes=[('xt', b)], stream=0)
    for kc in range(8):
        sc.op('pe', lambda e, kc=kc: e.transpose(out=k.psb[:, kc * 128:(kc + 1) * 128], in_=obf[:, kc * 128:(kc + 1) * 128], identity=C['identb'][:, :]),
              reads=[obres, 'const'], writes=['psb'])
    sc.op('act', lambda e: e.activation(out=oT[:, :, :], in_=k.psb[:, :].rearrange("p (kc t) -> p kc t", kc=8), func=AF.Copy), reads=['psb'], writes=['oT'])
    sc.op('act', lambda e: e.activation(out=C['ss'][0][:, 3:4], in_=C['ss'][1][:, 3:4], func=AF.Copy), reads=['oT'], writes=['oT'])
    for half in range(2):
        pb_ = int(os.environ.get('KTAILB', 4)) + half
        po = k.ps[pb_]
        for kc in range(8):
            sc.op('pe', lambda e, kc=kc: e.matmul(po[:, :], lhsT=oT[:, kc, :], rhs=wob[:, kc, half * 512:(half + 1) * 512], start=(kc == 0), stop=(kc == 7)),
                  reads=['oT', 'wob'], writes=[('ps', pb_)])
        sc.op('dve', lambda e: e.tensor_tensor(out=xt[:, half * 512:(half + 1) * 512], in0=po[:, :], in1=xt[:, half * 512:(half + 1) * 512], op=ALU.add),
              reads=[('ps', pb_), ('xt', b)], writes=[('xt', b)])
    sc.dma('pool', xo_d[n * 128:(n + 1) * 128, :], xt[:, :], reads=[('xt', b)], stream=3)


def load_wob(k, C, w_out, wob):
    sc = k.sc
    for kc in range(8):
        b = kc % 2
        wflat = C['wst'][b][:, :, :].rearrange("p a b -> p (a b)")
        sc.dma('sp', wflat, w_out[kc * 128:(kc + 1) * 128, :], writes=[('wst', b)], stream=1)
        sc.op('act', lambda e, kc=kc, wflat=wflat: e.activation(out=wob[:, kc, :], in_=wflat, func=AF.Copy), reads=[('wst', b)], writes=['wob'])


def copy_x(k, x_d, xo_d, C):
    for n in range(NT):
        b = n % 2
        k.sc.dma('sp', C['xt'][b][:, :], x_d[n * 128:(n + 1) * 128, :], writes=[('xt', b)], stream=0)
        k.sc.dma('pool', xo_d[n * 128:(n + 1) * 128, :], C['xt'][b][:, :], reads=[('xt', b)], stream=3)
    k.sc.phase()


STOP = os.environ.get('KSTOP', '')
KBR = os.environ.get('KBR', 'csw')


def mixer_c(k, x_d, xo_d, P, C, cd):
    sc = k.sc
    nc = k.nc
    st2 = ExitStack()
    old = k.st
    k.st = st2
    k.doff = k.dbase
    qTd, kTd = k.dram([8, 128, S], BF16), k.dram([8, 128, S], BF16)
    Kd, Vd = k.dram([S, 8, 128], BF16), k.dram([S, 8, 128], BF16)
    zd = k.dram([S, D], BF16)
    wob = k.sb([128, 8, D], BF16)
    oT = k.sb([128, 8, 128], BF16)
    beta = k.sb([128, NT, 8], F32)
    gg = k.sb([128, NT, 8], F32)
    ba = k.sb([128, NT, 16], F32)
    vec8 = k.sb([128, 4, 8], F32)
    og = k.sb([128, 128], F32)
    load_gainT(k, P['norm1'], C['gainT'])
    load_wob(k, C, P['c_w_out'], wob)
    sc.dma('sp', vec8[:, 0, :], P['c_a_log'].partition_broadcast(128), writes=['vec8'], stream=2)
    sc.dma('sp', vec8[:, 1, :], P['c_dt_bias'].partition_broadcast(128), writes=['vec8'], stream=2)
    sc.dma('sp', og[:, :], P['c_out_gain'].partition_broadcast(128), writes=['og'], stream=2)
    Win = P['c_w_in']
    with ExitStack() as st3:
        k.st = st3
        hT = k.sb([128, 8, S], BF16)
        raw = k.sb([128, S + 3], F32)
        acc = k.sb([128, S], F32)
        so = acc
        outb = k.sb([128, S], BF16)
        sqb = [k.sb([128, 512], BF16) for _ in range(2)]
        r1 = [k.sb([128, 512], F32) for _ in range(2)]
        cw = k.sb([128, 4], F32)
        tstage = [k.sb([128, 8, 128], BF16) for _ in range(2)]
        wz = k.sb([128, 8, D], BF16)
        zs = [k.sb([128, D], BF16) for _ in range(2)]
        norm_tiles(k, x_d, hT, range(NT), None, C)
        hall = [('hT', s_) for s_ in range(NT)]
        sc.op('pool', lambda e: e.memset(raw[:, 0:3], 0.0), writes=['raw0'])
        for j in range(3):
            for h in range(8):
                col = j * 1024 + h * 128
                load_w_chunk(k, Win, col, 128, C['wst'][0], C['wbf'][0], C['gainT'], ('wst', 0), ('wbf', 0))
                sc.dma('sp', cw[:, :], P['c_conv_w'][:, col:col + 128].rearrange("i c -> c i"), writes=['cw'], stream=2, allow_slow_non_contiguous=True)
                for c in range(8):
                    pq = k.ps[c % 2]
                    for kc in range(8):
                        sc.op('pe', lambda e, kc=kc: e.matmul(pq[:, :], lhsT=C['wbf'][0][:, kc, :], rhs=hT[:, kc, c * 512:(c + 1) * 512], start=(kc == 0), stop=(kc == 7)),
                              reads=[('wbf', 0)] + hall, writes=[('ps', c % 2)])
                    sc.op('act', lambda e: e.activation(out=raw[:, 3 + c * 512:3 + (c + 1) * 512], in_=pq[:, :], func=AF.Copy), reads=[('ps', c % 2)], writes=[('raw', c)])
                rall = [('raw', c) for c in range(8)] + ['raw0']
                for hf in range(2):
                    sl = slice(hf * 2048, (hf + 1) * 2048)
                    sc.op('dve', lambda e: e.tensor_scalar(out=acc[:, sl], in0=raw[:, 3 + hf * 2048:3 + (hf + 1) * 2048], scalar1=cw[:, 3:4], scalar2=None, op0=ALU.mult),
                          reads=rall + ['cw'], writes=[('acc', hf), ('so', hf)])
                    for i in range(3):
                        sc.op('dve', lambda e, i=i: e.scalar_tensor_tensor(out=acc[:, sl], in0=raw[:, i + hf * 2048:i + (hf + 1) * 2048], scalar=cw[:, i:i + 1], in1=acc[:, sl],
                                                                          op0=ALU.mult, op1=ALU.add),
                              reads=rall + ['cw', ('acc', hf)], writes=[('acc', hf)])
                    if j == 2:
                        sc.op('act', lambda e: e.activation(out=outb[:, sl], in_=acc[:, sl], func=AF.Silu), reads=[('acc', hf)], writes=[('outb', hf)])
                    else:
                        sc.op('act', lambda e: e.activation(out=so[:, sl], in_=acc[:, sl], func=AF.Silu), reads=[('acc', hf)], writes=[('so', hf), ('acc', hf)])
                if j < 2:
                    for c in range(8):
                        b = c % 2
                        cs = slice(c * 512, (c + 1) * 512)
                        pss = k.ps[2 + b]
                        sc.op('act', lambda e: e.activation(out=sqb[b][:, :], in_=so[:, cs], func=AF.Square), reads=[('so', c // 4)], writes=[('sqb', b)])
                        sc.op('pe', lambda e: e.matmul(pss[:, :], lhsT=C['onesb'][:, :], rhs=sqb[b][:, :], start=True, stop=True), reads=[('sqb', b), 'const'], writes=[('ps', 2 + b)])
                        sc.op('dve', lambda e: e.tensor_scalar(out=r1[b][:, :], in0=pss[:, :], scalar1=EPS, scalar2=None, op0=ALU.add), reads=[('ps', 2 + b)], writes=[('r1', b)])
                        sc.op('act', lambda e: e.activation(out=r1[b][:, :], in_=r1[b][:, :], func=AF.Sqrt), reads=[('r1', b)], writes=[('r1', b)])
                        sc.op('dve', lambda e: e.reciprocal(out=r1[b][:, :], in_=r1[b][:, :]), reads=[('r1', b)], writes=[('r1', b)])
                        sc.op('dve', lambda e: e.scalar_tensor_tensor(out=outb[:, cs], in0=so[:, cs], scalar=(128.0 ** -0.5 if j == 0 else 1.0), in1=r1[b][:, :], op0=ALU.mult, op1=ALU.mult),
                              reads=[('so', c // 4), ('r1', b)], writes=[('outb', c // 4)])
                    sc.dma('pool', (qTd if j == 0 else kTd).ap()[h, :, :], outb[:, :], reads=[('outb', 0), ('outb', 1)], writes=['qkTd'], stream=6)
                if j >= 1:
                    dst = Kd if j == 1 else Vd
                    dv_ = dst.ap().rearrange("(n p) h d -> p n h d", p=128)
                    for n8 in range(4):
                        b = n8 % 2
                        for t in range(8):
                            n = n8 * 8 + t
                            sc.op('pe', lambda e, t=t, n=n: e.transpose(out=k.psb[:, t * 128:(t + 1) * 128], in_=outb[:, n * 128:(n + 1) * 128], identity=C['identb'][:, :]),
                                  reads=[('outb', n // 16), 'const'], writes=['psb'])
                        sc.op('act', lambda e: e.activation(out=tstage[b][:, :, :], in_=k.psb[:, :].rearrange("p (a t) -> p a t", a=8), func=AF.Copy), reads=['psb'], writes=[('tst', b)])
                        sc.dma('pool', dv_[:, n8 * 8:(n8 + 1) * 8, h, :], tstage[b][:, :, :], reads=[('tst', b)], writes=['KVd'], stream=7)
                sc.maybe_phase()
        for pc in range(8):
            load_w_chunk(k, Win, 3072 + pc * 128, 128, C['wst'][pc % 2], C['wbf'][pc % 2], C['gainT'], ('wst', pc % 2), ('wbf', pc % 2))
            sc.op('act', lambda e, pc=pc: e.activation(out=wz[:, :, pc * 128:(pc + 1) * 128], in_=C['wbf'][pc % 2][:, :, :], func=AF.Copy), reads=[('wbf', pc % 2)], writes=['wz'])
        load_w_chunk(k, Win, 4096, 16, C['wst2'][0], C['wbf2'][0], C['gainT'], ('wst2', 0), ('wbf2', 0))
        for n in range(NT):
            b = n % 2
            for half in range(2):
                pz = k.ps[half]
                for kc in range(8):
                    sc.op('pe', lambda e, kc=kc: e.matmul(pz[:, :], lhsT=hT[:, kc, n * 128:(n + 1) * 128], rhs=wz[:, kc, half * 512:(half + 1) * 512], start=(kc == 0), stop=(kc == 7)),
                          reads=['wz', ('hT', n)], writes=[('ps', half)])
                sc.op('act', lambda e: e.activation(out=zs[b][:, half * 512:(half + 1) * 512], in_=pz[:, :], func=AF.Silu), reads=[('ps', half)], writes=[('zs', b)])
            sc.dma('pool', zd.ap()[n * 128:(n + 1) * 128, :], zs[b][:, :], reads=[('zs', b)], writes=['zd'], stream=6)
            pb = k.ps[2]
            for kc in range(8):
                sc.op('pe', lambda e, kc=kc: e.matmul(pb[:, 0:16], lhsT=hT[:, kc, n * 128:(n + 1) * 128], rhs=C['wbf2'][0][:, kc, 0:16], start=(kc == 0), stop=(kc == 7)),
                      reads=[('wbf2', 0), ('hT', n)], writes=[('ps', 2)])
            sc.op('dve', lambda e: e.tensor_copy(out=ba[:, n, :], in_=pb[:, 0:16]), reads=[('ps', 2)], writes=['ba'])
        tmp = raw[:, 0:NT * 8].rearrange("p (n h) -> p n h", h=8)
        tmp2 = acc[:, 0:NT * 8].rearrange("p (n h) -> p n h", h=8)
        tmp3 = acc[:, 2048:2048 + NT * 8].rearrange("p (n h) -> p n h", h=8)
        sc.op('act', lambda e: e.activation(out=beta[:, :, :], in_=ba[:, :, 0:8], func=AF.Sigmoid), reads=['ba'], writes=['beta'])
        sc.op('act', lambda e: e.activation(out=vec8[:, 2, :], in_=vec8[:, 0, :], func=AF.Exp), reads=['vec8'], writes=['vec8b'])
        sc.op('dve', lambda e: e.tensor_tensor(out=tmp, in0=ba[:, :, 8:16], in1=bc(vec8[:, 1:2, :], [128, NT, 8]), op=ALU.add),
              reads=['ba', 'vec8', 'raw0'] + [('raw', c) for c in range(8)], writes=['tmpx'])
        sc.op('act', lambda e: e.activation(out=tmp2, in_=tmp, func=AF.Abs), reads=['tmpx', ('acc', 0), ('so', 0)], writes=['tmp2'])
        sc.op('act', lambda e: e.activation(out=tmp2, in_=tmp2, func=AF.Exp, scale=-1.0), reads=['tmp2'], writes=['tmp2'])
        sc.op('dve', lambda e: e.tensor_scalar(out=tmp2, in0=tmp2, scalar1=1.0, scalar2=None, op0=ALU.add), reads=['tmp2'], writes=['tmp2'])
        sc.op('act', lambda e: e.activation(out=tmp2, in_=tmp2, func=AF.Ln), reads=['tmp2'], writes=['tmp2'])
        sc.op('dve', lambda e: e.tensor_scalar(out=tmp3, in0=tmp, scalar1=0.0, scalar2=None, op0=ALU.max), reads=['tmpx', ('so', 0), ('so', 1), ('acc', 1)], writes=['tmp3'])
        sc.op('dve', lambda e: e.tensor_tensor(out=tmp3, in0=tmp3, in1=tmp2, op=ALU.add), reads=['tmp3', 'tmp2'], writes=['tmp3'])
        sc.op('dve', lambda e: e.tensor_tensor(out=tmp3, in0=tmp3, in1=bc(vec8[:, 2:3, :], [128, NT, 8]), op=ALU.mult), reads=['tmp3', 'vec8b'], writes=['tmp3'])
        sc.op('dve', lambda e: e.tensor_scalar(out=gg[:, :, :], in0=tmp3, scalar1=-1.0, scalar2=None, op0=ALU.mult), reads=['tmp3'], writes=['gg'])
        sc.phase()
    k.st = st2
    if STOP == 'c1':
        copy_x(k, x_d, xo_d, C)
        k.st = old
        st2.close()
        return
    def t4(nm, dt=F32, n=4):
        return k.sb([128, n, 128], dt)
    qTt = [k.sb([128, 8, 128], BF16) for _ in range(2)]
    kTt = [k.sb([128, 8, 128], BF16) for _ in range(2)]
    Kt = [k.sb([128, 8, 128], BF16) for _ in range(2)]
    Vt = [k.sb([128, 8, 128], BF16) for _ in range(2)]
    zt = [k.sb([128, D], BF16) for _ in range(2)]
    Rm = k.sb([128, 8, 128], F32)
    sm = k.sb([128, 12, 8], F32)
    t0 = t4('t0'); tA = t4('tA'); DmT = t4('DmT'); Dms = t4('Dms')
    Xa, Xb, Ya, Yb = t4('Xa'), t4('Xb'), t4('Ya'), t4('Yb')
    Qm = t4('Qm'); Qb = t4('Qb', BF16)
    attnT = k.sb([128, 8, 128], BF16)
    Vb = k.sb([128, 8, 128], BF16); Kh = k.sb([128, 8, 128], BF16); Kti = k.sb([128, 8, 128], BF16)
    u = k.sb([128, 8, 128], F32); wTb = k.sb([128, 8, 128], BF16)
    vn = k.sb([128, 8, 128], BF16)
    Sf = k.sb([128, 8, 128], F32); Sb = k.sb([128, 8, 128], BF16)
    osb = k.sb([128, 8, 128], F32); tq = t4('tq')
    sqo = k.sb([128, 8, 128], F32)
    ob = k.sb([128, 8, 128], BF16)
    sc.op('pool', lambda e: e.memset(Sf[:, :, :], 0.0), writes=['Sf'])
    sc.op('pool', lambda e: e.memset(Sb[:, :, :], 0.0), writes=['Sb'])
    qv = qTd.ap().rearrange("h p t -> p h t")
    kv = kTd.ap().rearrange("h p t -> p h t")
    PS = k.ps
    for n in range(int(os.environ.get('KTILES', NT))):
        b = n % 2
        ts_ = slice(n * 128, (n + 1) * 128)
        sc.dma('sp', qTt[b][:, :, :], qv[:, :, ts_], writes=[('qTt', b)], stream=1)
        sc.dma('sp', kTt[b][:, :, :], kv[:, :, ts_], writes=[('kTt', b)], stream=1)
        sc.dma('sp', Kt[b][:, :, :], Kd.ap()[ts_, :, :], writes=[('Kt', b)], stream=4)
        sc.dma('sp', Vt[b][:, :, :], Vd.ap()[ts_, :, :], writes=[('Vt', b)], stream=4)
        sc.dma('sp', zt[b][:, :], zd.ap()[ts_, :], writes=[('zt', b)], stream=5)
        gt = gg[:, n, :]
        for idx, lh in ((0, cd['TB']), (1, cd['BLK']), (2, cd['H0']), (3, cd['H1'])):
            sc.op('pe', lambda e, lh=lh, idx=idx: e.matmul(PS[6][:, idx * 8:idx * 8 + 8], lhsT=lh[:, :], rhs=gt, start=True, stop=True), reads=['gg', 'cd'], writes=[('ps', 6)])
        sc.op('dve', lambda e: e.tensor_copy(out=sm[:, 0:4, :], in_=PS[6][:, 0:32].rearrange("p (a h) -> p a h", a=4)), reads=[('ps', 6)], writes=['sm'])
        sc.op('act', lambda e: e.activation(out=sm[:, 4, :], in_=sm[:, 0, :], func=AF.Exp), reads=['sm'], writes=['sm4'])
        sc.op('act', lambda e: e.activation(out=sm[:, 5:7, :], in_=sm[:, 2:4, :], func=AF.Exp), reads=['sm'], writes=['sm5'])
        sc.op('dve', lambda e: e.tensor_tensor(out=sm[:, 7, :], in0=sm[:, 1, :], in1=sm[:, 0, :], op=ALU.subtract), reads=['sm'], writes=['sm7'])
        sc.op('act', lambda e: e.activation(out=sm[:, 7, :], in_=sm[:, 7, :], func=AF.Exp), reads=['sm7'], writes=['sm7'])
        sc.op('dve', lambda e: e.tensor_tensor(out=sm[:, 8, :], in0=sm[:, 4, :], in1=beta[:, n, :], op=ALU.mult), reads=['sm4', 'beta'], writes=['sm8'])
        sc.op('dve', lambda e: e.tensor_scalar(out=sm[:, 9, :], in0=beta[:, n, :], scalar1=-1.0, scalar2=None, op0=ALU.mult), reads=['beta'], writes=['sm9'])
        sc.op('dve', lambda e: e.tensor_tensor(out=Rm[:, :, :], in0=bc(cd['TB'][:, :].unsqueeze(1), [128, 8, 128]), in1=bc(gt.unsqueeze(2), [128, 8, 128]), op=ALU.mult),
              reads=['gg', 'cd'], writes=['Rm'])
        sc.op('dve', lambda e: e.tensor_tensor(out=Vb[:, :, :], in0=Vt[b][:, :, :], in1=bc(beta[:, n, :].unsqueeze(2), [128, 8, 128]), op=ALU.mult), reads=[('Vt', b), 'beta'], writes=['Vb'])
        sc.op('dve', lambda e: e.tensor_tensor(out=Kh[:, :, :], in0=Kt[b][:, :, :], in1=bc(sm[:, 8, :].unsqueeze(2), [128, 8, 128]), op=ALU.mult), reads=[('Kt', b), 'sm8'], writes=['Kh'])
        sc.op('dve', lambda e: e.tensor_tensor(out=Kti[:, :, :], in0=Kt[b][:, :, :], in1=bc(sm[:, 7, :].unsqueeze(2), [128, 8, 128]), op=ALU.mult), reads=[('Kt', b), 'sm7'], writes=['Kti'])
        if STOP == 'small':
            continue
        for hg in range(2):
            hs = slice(4 * hg, 4 * hg + 4)
            H = range(4 * hg, 4 * hg + 4)
            def v4(p):
                return p[:, :].rearrange("p (a t) -> p a t", a=4)
            sc.op('pe', lambda e: e.matmul(PS[0][:, :], lhsT=cd['ONES'][:, :], rhs=Rm[:, hs, :].rearrange("p a t -> p (a t)"), start=True, stop=True), reads=['Rm', 'cd'], writes=[('ps', 0)])
            for i, h in enumerate(H):
                sc.op('pe', lambda e, i=i, h=h: e.matmul(PS[1][:, i * 128:(i + 1) * 128], lhsT=kTt[b][:, h, :], rhs=kTt[b][:, h, :], start=True, stop=True), reads=[('kTt', b)], writes=[('ps', 1)])
            for i, h in enumerate(H):
                sc.op('pe', lambda e, i=i, h=h: e.matmul(PS[2][:, i * 128:(i + 1) * 128], lhsT=kTt[b][:, h, :], rhs=qTt[b][:, h, :], start=True, stop=True), reads=[('kTt', b), ('qTt', b)], writes=[('ps', 2)])
            sc.op('dve', lambda e: e.tensor_tensor(out=t0[:, :, :], in0=v4(PS[0]), in1=bc(sm[:, 0, hs].unsqueeze(2), [128, 4, 128]), op=ALU.subtract), reads=[('ps', 0), 'sm'], writes=['t0'])
            if STOP == 'g1':
                continue
            sc.op('dve', lambda e: e.tensor_tensor(out=tA[:, :, :], in0=t0[:, :, :], in1=bc(cd['MT'][:, :].unsqueeze(1), [128, 4, 128]), op=ALU.add), reads=['t0', 'cd'], writes=['tA'])
            sc.op('act', lambda e: e.activation(out=DmT[:, :, :], in_=tA[:, :, :], func=AF.Exp), reads=['tA'], writes=['DmT'])
            sc.op('dve', lambda e: e.tensor_tensor(out=tA[:, :, :], in0=t0[:, :, :], in1=bc(cd['MBS'][:, :].unsqueeze(1), [128, 4, 128]), op=ALU.add), reads=['t0', 'cd', 'tA'], writes=['tA'])
            sc.op('act', lambda e: e.activation(out=Dms[:, :, :], in_=tA[:, :, :], func=AF.Exp, scale=-1.0), reads=['tA'], writes=['Dms'])
            if STOP == 'g2':
                continue
            sc.op('dve', lambda e: e.tensor_tensor(out=attnT[:, hs, :], in0=v4(PS[2]), in1=DmT[:, :, :], op=ALU.mult), reads=[('ps', 2), 'DmT'], writes=[('attnT', hg)])
            sc.op('dve', lambda e: e.tensor_tensor(out=Xa[:, :, :], in0=v4(PS[1]), in1=Dms[:, :, :], op=ALU.mult), reads=[('ps', 1), 'Dms'], writes=['Xa'])
            sc.op('dve', lambda e: e.tensor_tensor(out=Xa[:, :, :], in0=Xa[:, :, :], in1=bc(sm[:, 9, hs].unsqueeze(2), [128, 4, 128]), op=ALU.mult), reads=['Xa', 'sm9'], writes=['Xa'])
            if STOP == 'g3':
                continue
            for i in range(4):
                sc.op('pe', lambda e, i=i: e.matmul(PS[3][:, i * 128:(i + 1) * 128], lhsT=Xa[:, i, :], rhs=C['identf'][:, :], start=True, stop=True), reads=['Xa', 'const'], writes=[('ps', 3)])
            if STOP == 'g4':
                continue
            sc.op('act', lambda e: e.activation(out=Ya[:, :, :], in_=v4(PS[3]), func=AF.Copy), reads=[('ps', 3)], writes=['Ya'])
            if STOP == 'g5':
                continue
            sc.op('dve', lambda e: e.tensor_tensor(out=Qm[:, :, :], in0=Ya[:, :, :], in1=bc(C['identf'][:, :].unsqueeze(1), [128, 4, 128]), op=ALU.add), reads=['Ya', 'const'], writes=['Qm'])
            if STOP == 'pre':
                continue
            Xc, Yc, Xn, Yn, xc, yc, xn, yn = Xa, Ya, Xb, Yb, 'Xa', 'Ya', 'Xb', 'Yb'
            for step in range(5):
                for i in range(4):
                    sc.op('pe', lambda e, i=i, Xc=Xc, Yc=Yc: e.matmul(PS[0][:, i * 128:(i + 1) * 128], lhsT=Yc[:, i, :], rhs=Xc[:, i, :], start=True, stop=True), reads=[xc, yc], writes=[('ps', 0)])
                sc.op('act', lambda e, Xn=Xn: e.activation(out=Xn[:, :, :], in_=v4(PS[0]), func=AF.Copy), reads=[('ps', 0)], writes=[xn])
                if step < 4:
                    for i in range(4):
                        sc.op('pe', lambda e, i=i, Xc=Xc, Yc=Yc: e.matmul(PS[1][:, i * 128:(i + 1) * 128], lhsT=Xc[:, i, :], rhs=Yc[:, i, :], start=True, stop=True), reads=[xc, yc], writes=[('ps', 1)])
                    sc.op('act', lambda e, Yn=Yn: e.activation(out=Yn[:, :, :], in_=v4(PS[1]), func=AF.Copy), reads=[('ps', 1)], writes=[yn])
                for i in range(4):
                    sc.op('pe', lambda e, i=i, Xn=Xn: e.matmul(PS[2][:, i * 128:(i + 1) * 128], lhsT=Xn[:, i, :], rhs=Qm[:, i, :], start=True, stop=True), reads=[xn, 'Qm'], writes=[('ps', 2)])
                sc.op('dve', lambda e: e.tensor_tensor(out=Qm[:, :, :], in0=v4(PS[2]), in1=Qm[:, :, :], op=ALU.add), reads=[('ps', 2), 'Qm'], writes=['Qm'])
                Xc, Yc, Xn, Yn, xc, yc, xn, yn = Xn, Yn, Xc, Yc, xn, yn, xc, yc
            if STOP == 'inv':
                continue
            sc.op('act', lambda e: e.activation(out=Qb[:, :, :], in_=Qm[:, :, :], func=AF.Copy), reads=['Qm'], writes=['Qb'])
            for i, h in enumerate(H):
                sc.op('pe', lambda e, i=i, h=h: e.matmul(PS[3][:, i * 128:(i + 1) * 128], lhsT=Qb[:, i, :], rhs=Vb[:, h, :], start=True, stop=True), reads=['Qb', 'Vb'], writes=[('ps', 3)])
            sc.op('act', lambda e: e.activation(out=u[:, hs, :], in_=v4(PS[3]), func=AF.Copy), reads=[('ps', 3)], writes=[('u', hg)])
            for i, h in enumerate(H):
                sc.op('pe', lambda e, i=i, h=h: e.matmul(PS[4][:, i * 128:(i + 1) * 128], lhsT=Kh[:, h, :], rhs=Qb[:, i, :], start=True, stop=True), reads=['Qb', 'Kh'], writes=[('ps', 4)])
            sc.op('act', lambda e: e.activation(out=wTb[:, hs, :], in_=v4(PS[4]), func=AF.Copy), reads=[('ps', 4)], writes=[('wTb', hg)])
        if STOP in ('setup', 'pre', 'inv', 'g1', 'g2', 'g3', 'g4', 'g5'):
            continue
        for hf in range(2):
            Pp = slice(64 * hf, 64 * hf + 64)
            for hg in range(2):
                hs = slice(4 * hg, 4 * hg + 4)
                H = range(4 * hg, 4 * hg + 4)
                sres = ('Sb', hg)
                for i, h in enumerate(H):
                    sc.op('pe', lambda e, i=i, h=h: e.matmul(PS[0][:, i * 128:(i + 1) * 128], lhsT=wTb[:, h, :], rhs=Sb[:, h, :], start=True, stop=True), reads=[('wTb', hg), sres, 'Sb'], writes=[('ps', 0)])
                for i, h in enumerate(H):
                    sc.op('pe', lambda e, i=i, h=h: e.matmul(PS[1][:, i * 128:(i + 1) * 128], lhsT=qTt[b][:, h, :], rhs=Sb[:, h, :], start=True, stop=True), reads=[('qTt', b), sres, 'Sb'], writes=[('ps', 1)])
                sc.op('dve', lambda e: e.tensor_tensor(out=vn[Pp, hs, :], in0=u[Pp, hs, :], in1=v4(PS[0])[Pp, :, :], op=ALU.subtract), reads=[('u', hg), ('ps', 0)], writes=[('vn', hg)])
                sc.op('dve', lambda e: e.tensor_tensor(out=tq[Pp, :, :], in0=v4(PS[1])[Pp, :, :], in1=bc(sm[Pp, 4, hs].unsqueeze(2), [64, 4, 128]), op=ALU.mult), reads=[('ps', 1), 'sm4'], writes=['tq'])
                for i, h in enumerate(H):
                    sc.op('pe', lambda e, i=i, h=h: e.matmul(PS[2][:, i * 128:(i + 1) * 128], lhsT=Kti[Pp, h, :], rhs=vn[Pp, h, :], start=True, stop=True), reads=['Kti', ('vn', hg)], writes=[('ps', 2)])
                for i, h in enumerate(H):
                    sc.op('pe', lambda e, i=i, h=h: e.matmul(PS[3][:, i * 128:(i + 1) * 128], lhsT=attnT[Pp, h, :], rhs=vn[Pp, h, :], start=True, stop=True), reads=[('attnT', hg), ('vn', hg)], writes=[('ps', 3)])
                sc.op('dve', lambda e: e.tensor_tensor(out=Sf[:, hs, :], in0=Sf[:, hs, :], in1=bc(sm[:, 5 + hf, hs].unsqueeze(2), [128, 4, 128]), op=ALU.mult), reads=['Sf', 'sm5', ('Sf', hg)], writes=[('Sf', hg)])
                sc.op('dve', lambda e: e.tensor_tensor(out=Sf[:, hs, :], in0=Sf[:, hs, :], in1=v4(PS[2]), op=ALU.add), reads=[('Sf', hg), ('ps', 2)], writes=[('Sf', hg)])
                sc.op('act', lambda e: e.activation(out=Sb[:, hs, :], in_=Sf[:, hs, :], func=AF.Copy), reads=[('Sf', hg)], writes=[sres])
                sc.op('dve', lambda e: e.tensor_tensor(out=osb[Pp, hs, :], in0=tq[Pp, :, :], in1=v4(PS[3])[Pp, :, :], op=ALU.add), reads=['tq', ('ps', 3)], writes=[('osb', hg)])
        if STOP == 'seq':
            continue
        sc.op('act', lambda e: e.activation(out=sqo[:, :, :], in_=osb[:, :, :], func=AF.Square), reads=[('osb', 0), ('osb', 1)], writes=['sqo'])
        sc.op('dve', lambda e: e.tensor_reduce(out=sm[:, 10, :], in_=sqo[:, :, :], axis=AX.X, op=ALU.add), reads=['sqo'], writes=['sm10'])
        sc.op('dve', lambda e: e.tensor_scalar(out=sm[:, 10, :], in0=sm[:, 10, :], scalar1=1.0 / 128, scalar2=EPS, op0=ALU.mult, op1=ALU.add), reads=['sm10'], writes=['sm10'])
        sc.op('act', lambda e: e.activation(out=sm[:, 10, :], in_=sm[:, 10, :], func=AF.Sqrt), reads=['sm10'], writes=['sm10'])
        sc.op('dve', lambda e: e.reciprocal(out=sm[:, 11, :], in_=sm[:, 10, :]), reads=['sm10'], writes=['sm11'])
        sc.op('dve', lambda e: e.tensor_tensor(out=sqo[:, :, :], in0=osb[:, :, :], in1=bc(sm[:, 11, :].unsqueeze(2), [128, 8, 128]), op=ALU.mult), reads=[('osb', 0), ('osb', 1), 'sm11', 'sqo'], writes=['sqo'])
        sc.op('dve', lambda e: e.tensor_tensor(out=sqo[:, :, :], in0=sqo[:, :, :], in1=bc(og[:, :].unsqueeze(1), [128, 8, 128]), op=ALU.mult), reads=['sqo', 'og'], writes=['sqo'])
        sc.op('dve', lambda e: e.tensor_tensor(out=ob[:, :, :], in0=sqo[:, :, :], in1=zt[b][:, :].rearrange("p (h d) -> p h d", h=8), op=ALU.mult), reads=['sqo', ('zt', b)], writes=['ob'])
        out_tail(k, C, ob[:, :, :].rearrange("p h d -> p (h d)"), wob, oT, x_d, xo_d, n, 'ob')
        sc.maybe_phase()
    sc.phase()
    k.st = old
    st2.close()


def dense_attn(k, C, B, qTh, lhsT_fn, vaug_fn, ncol, tab_fn, kts_fn, qb0_fn, last_fn, evac_fn, pen_fn=None, qcs=range(8), tag=''):
    sc = k.sc
    PS = k.ps
    Eb, PT = B['Eb'], B['PT']

    def acc(qb):
        return PS[2 + qb][:, 0:ncol], ('ps', 2 + qb)

    u = 0
    for qc in (list(qcs)[::-1] if os.environ.get('KREV') else qcs):
        kts = kts_fn(qc)
        units = []
        for kt in kts:
            qb0 = qb0_fn(qc, kt)
            units.append((kt, qb0, qc * 512 + qb0 * 128, 512 - qb0 * 128))

        def emit_s(ui, un):
            kt, qb0, q0, nq = un
            pS = PS[ui % 2]
            lh, lres = lhsT_fn(kt)
            sc.op('pe', lambda e: e.matmul(pS[:, 0:nq], lhsT=lh, rhs=qTh[0:64, q0:q0 + nq], start=True, stop=(pen_fn is None)),
                  reads=[lres, 'qTh'], writes=[('ps', ui % 2)])
            if pen_fn is not None:
                pl, pr, pres = pen_fn(kt, q0, nq)
                sc.op('pe', lambda e: e.matmul(pS[:, 0:nq], lhsT=pl, rhs=pr, start=False, stop=True), reads=pres, writes=[('ps', ui % 2)])

        if units:
            emit_s(u, units[0])
        started = set()
        for i, un in enumerate(units):
            kt, qb0, q0, nq = un
            ui = u + i
            if i + 1 < len(units):
                emit_s(ui + 1, units[i + 1])
            pS, eb, pt = PS[ui % 2], Eb[ui % 2], PT[ui % 3]
            sc.op('act', lambda e: e.activation(out=eb[:, 0:nq], in_=pS[:, 0:nq], func=AF.Exp, scale=0.125), reads=[('ps', ui % 2)], writes=[('Eb', ui % 2)])
            tb, tres = tab_fn(kt, q0, nq)
            sc.op('dve', lambda e: e.tensor_tensor(out=pt[:, 0:nq], in0=eb[:, 0:nq], in1=tb, op=ALU.mult), reads=[('Eb', ui % 2), tres], writes=[('PT', ui % 3)])
            va, vres = vaug_fn(kt)
            for qb in range(qb0, 4):
                if kt > last_fn(qc, qb):
                    continue
                a, ares = acc(qb)
                sc.op('pe', lambda e, qb=qb, a=a: e.matmul(a, lhsT=pt[:, (qb - qb0) * 128:(qb - qb0 + 1) * 128], rhs=va, start=(qb not in started), stop=(kt == last_fn(qc, qb))),
                      reads=[('PT', ui % 3)] + vres, writes=[ares])
                started.add(qb)
        u += len(units)
        evac_fn(qc, acc)
        sc.maybe_phase()


class StopHere(Exception):
    pass


def chk(tag):
    if STOP == tag:
        raise StopHere()


def build_head_tab(k, C, oh_d, width, tab, hq, relh, ohs, stg, res):
    sc = k.sc
    sc.op('dve', lambda e: e.tensor_copy(out=relh[:, :], in_=bc(C['relb'][:, hq:hq + 1], [33, 128])), reads=['relb'], writes=['relh'])
    for ci, c0 in enumerate(range(0, width, 512)):
        n = min(512, width - c0)
        j = ci % 2
        sc.dma('sp', ohs[:, 0:n], oh_d[:, c0:c0 + n], writes=['ohs'], stream=2)
        sc.op('pe', lambda e: e.matmul(k.ps[6][:, 0:n], lhsT=relh[:, :], rhs=ohs[:, 0:n], start=True, stop=True), reads=['ohs', 'relh'], writes=[('ps', 6)])
        sc.op('act', lambda e: e.activation(out=stg[j][:, 0:n], in_=k.ps[6][:, 0:n], func=AF.Exp), reads=[('ps', 6)], writes=[('tslab', j)])
        sc.dma('pool', tab.ap()[:, c0:c0 + n], stg[j][:, 0:n], reads=[('tslab', j)], writes=[res], stream=9)


def mixer_b(k, x_d, xo_d, P, C, tabs, cdn):
    sc = k.sc
    st2 = ExitStack()
    old = k.st
    k.st = st2
    PS = k.ps
    k.doff = k.dbase
    ohC, ohD, ohW = tabs
    tabC, tabD, tabW = k.dram([128, 6144]), k.dram([128, 4224]), k.dram([128, 1280])
    relh = k.sb([33, 128], F32)
    ohs = k.sb([33, 512], F32)
    if STOP == 'b0':
        copy_x(k, x_d, xo_d, C)
        k.st = old
        st2.close()
        return
    WC, WD, WW = 6144, 4224, 1280
    qTd = k.dram([16, 64, S], BF16)
    kTd = [k.dram([4, 64, S], BF16) for _ in range(4)]
    Vd = [k.dram([S, 4, 64], BF16) for _ in range(2)]
    obd = k.dram([S, D], BF16)
    wob = k.sb([128, 8, D], BF16)
    oT = k.sb([128, 8, 128], BF16)
    gates = k.sb([128, NT, 48], F32)
    gcol = k.sb([128, 4], F32)
    load_gainT(k, P['norm1'], C['gainT'])
    load_wob(k, C, P['b_w_out'], wob)
    for half in range(2):
        sc.dma('sp', gcol[half * 64:(half + 1) * 64, 0:1], P['b_q_gain'].rearrange("(d o) -> d o", o=1), writes=['gcol'], stream=2, allow_slow_non_contiguous=True)
        sc.dma('sp', gcol[half * 64:(half + 1) * 64, 1:4], P['b_k_gain'].rearrange("g d -> d g"), writes=['gcol'], stream=2, allow_slow_non_contiguous=True)
    Win = P['b_w_in']
    with ExitStack() as st3:
        k.st = st3
        hT = k.sb([128, 8, S], BF16)
        outb = k.sb([128, S], BF16)
        sqb = [k.sb([128, 512], BF16) for _ in range(2)]
        r1 = [k.sb([128, 512], F32) for _ in range(2)]
        wv = k.sb([128, 8, 256], BF16)
        vst = [k.sb([128, 256], BF16) for _ in range(2)]
        norm_tiles(k, x_d, hT, range(NT), None, C)
        hall = [('hT', s_) for s_ in range(NT)]
        chunks = [(c * 128, qTd, 2 * c, 0) for c in range(8)]
        for br, kvi, dst, gc in ((0, 0, kTd[0], None), (0, 1, kTd[1], None), (1, 0, kTd[2], 2), (2, 0, kTd[3], 3)):
            for c2 in range(2):
                chunks.append((1024 + (br * 2 + kvi) * 256 + c2 * 128, dst, 2 * c2, gc))
        for (col, dst, h0, gc) in chunks[int(os.environ.get('KCH0', 0)):int(os.environ.get('KCH1', 99))]:
            load_w_chunk(k, Win, col, 128, C['wst'][0], C['wbf'][0], C['gainT'], ('wst', 0), ('wbf', 0))
            for c in range(8):
                b = c % 2
                cs = slice(c * 512, (c + 1) * 512)
                pq, pss = PS[4], PS[5]
                for kc in range(8):
                    sc.op('pe', lambda e, kc=kc: e.matmul(pq[:, :], lhsT=C['wbf'][0][:, kc, :], rhs=hT[:, kc, cs], start=(kc == 0), stop=(kc == 7)),
                          reads=[('wbf', 0)] + hall, writes=[('ps', 4)])
                if gc is None:
                    sc.op('act', lambda e: e.activation(out=outb[:, cs], in_=pq[:, :], func=AF.Copy), reads=[('ps', 4)], writes=[('outb', c)])
                    continue
                sc.op('act', lambda e: e.activation(out=sqb[b][:, :], in_=pq[:, :], func=AF.Square), reads=[('ps', 4)], writes=[('sqb', b)])
                sc.op('pe', lambda e: e.matmul(pss[:, :], lhsT=C['blk'][:, :], rhs=sqb[b][:, :], start=True, stop=True), reads=[('sqb', b), 'const'], writes=[('ps', 5)])
                sc.op('dve', lambda e: e.tensor_scalar(out=r1[b][:, :], in0=pss[:, :], scalar1=1.0 / 64, scalar2=EPS, op0=ALU.mult, op1=ALU.add), reads=[('ps', 5)], writes=[('r1', b)])
                sc.op('act', lambda e: e.activation(out=r1[b][:, :], in_=r1[b][:, :], func=AF.Sqrt), reads=[('r1', b)], writes=[('r1', b)])
                sc.op('dve', lambda e: e.reciprocal(out=r1[b][:, :], in_=r1[b][:, :]), reads=[('r1', b)], writes=[('r1', b)])
                sc.op('dve', lambda e: e.scalar_tensor_tensor(out=outb[:, cs], in0=pq[:, :], scalar=gcol[:, gc:gc + 1], in1=r1[b][:, :], op0=ALU.mult, op1=ALU.mult),
                      reads=[('ps', 4), ('r1', b), 'gcol'], writes=[('outb', c)])
            oall = [('outb', c) for c in range(8)]
            sc.dma('pool', dst.ap()[h0, :, :], outb[0:64, :], reads=oall, writes=['featd'], stream=6)
            sc.dma('pool', dst.ap()[h0 + 1, :, :], outb[64:128, :], reads=oall, writes=['featd'], stream=6)
            sc.maybe_phase()
        for vi, col in enumerate((1024 + 3 * 256, 1024 + 5 * 256)):
            if STOP == 'b1a':
                continue
            for pc in range(2):
                load_w_chunk(k, Win, col + pc * 128, 128, C['wst'][pc], C['wbf'][pc], C['gainT'], ('wst', pc), ('wbf', pc))
                sc.op('act', lambda e, pc=pc: e.activation(out=wv[:, :, pc * 128:(pc + 1) * 128], in_=C['wbf'][pc][:, :, :], func=AF.Copy), reads=[('wbf', pc)], writes=['wv'])
            for n in range(NT):
                b = n % 2
                for kc in range(8):
                    sc.op('pe', lambda e, kc=kc: e.matmul(PS[b][:, 0:256], lhsT=hT[:, kc, n * 128:(n + 1) * 128], rhs=wv[:, kc, :], start=(kc == 0), stop=(kc == 7)),
                          reads=['wv', ('hT', n)], writes=[('ps', b)])
                sc.op('act', lambda e: e.activation(out=vst[b][:, :], in_=PS[b][:, 0:256], func=AF.Copy), reads=[('ps', b)], writes=[('vst', b)])
                sc.dma('pool', Vd[vi].ap()[n * 128:(n + 1) * 128, :, :].rearrange("p h d -> p (h d)"), vst[b][:, :], reads=[('vst', b)], writes=['Vd'], stream=7)
        load_w_chunk(k, Win, 2560, 48, C['wst2'][0], C['wbf2'][0], C['gainT'], ('wst2', 0), ('wbf2', 0))
        for n in range(NT if STOP not in ('b1a', 'b1b') else 0):
            b = n % 2
            for kc in range(8):
                sc.op('pe', lambda e, kc=kc: e.matmul(PS[b][:, 0:48], lhsT=hT[:, kc, n * 128:(n + 1) * 128], rhs=C['wbf2'][0][:, kc, 0:48], start=(kc == 0), stop=(kc == 7)),
                      reads=[('wbf2', 0), ('hT', n)], writes=[('ps', b)])
            sc.op('act', lambda e: e.activation(out=gates[:, n, :], in_=PS[b][:, 0:48], func=AF.Sigmoid), reads=[('ps', b)], writes=['gates'])
        sc.phase()
    k.st = st2
    if STOP in ('b1', 'b1a', 'b1b'):
        for _ in range(int(os.environ.get('KXP2', 0))):
            sc.phase()
        copy_x(k, x_d, xo_d, C)
        k.st = old
        st2.close()
        return
    stopped = False
    kcT = k.sb([64, 4, 256], BF16)
    vaug = k.sb([128, 2, 4, 129], BF16)
    if not os.environ.get('KNOMS'):
        sc.op('dve', lambda e: e.memset(kcT[:, :, :], 0.0), writes=['kcT'])
    if not os.environ.get('KNOMS2'):
        sc.op('dve', lambda e: e.memset(vaug[:, :, :, :], 0.0), writes=['vaug'])
    with ExitStack() as st3:
        k.st = st3
        try:
            w1b = k.sb([64, 32, 256], BF16)
            w2b = k.sb([128, 2, 64], BF16)
            posT = k.sb([64, 32], BF16)
            posn = k.sb([32, 64], F32)
            w2f = k.sb([128, 2, 64], F32)
            tT = k.sb([64, S], BF16)
            biasc = k.sb([128, 2], F32)
            xg = [k.sb([128, 256], F32) for _ in range(4)]
            gT = k.sb([128, 2, 256], BF16)
            mstage = k.sb([128, 2, 64], F32)
            chk('c000')
            sc.op('dve', lambda e: e.memset(gT[:, :, :], 0.0), writes=['gT'])
            chk('c001')
            for ct in range(2):
                sc.dma('sp', mstage[:, ct, :], cdn['M'][ct * 128:(ct + 1) * 128, :], writes=['mstage'], stream=2)
            chk('c002')
            for n4 in range(4):
                sc.op('act', lambda e, n4=n4: e.activation(out=vaug[:, :, n4, 65:129], in_=mstage[:, :, :], func=AF.Copy), reads=['mstage', 'vaug'], writes=['vaug'])
            sc.op('dve', lambda e: e.memset(vaug[:, :, :, 64:65], 1.0), reads=['vaug'], writes=['vaug'])
            chk('c00')
            for kvi in range(2):
                w1v = P['b_cmp_w1'][kvi].rearrange("(l d) h -> d l h", d=64)
                for lq in range(8):
                    wstv = C['wst'][lq % 2][0:64, :, :].rearrange("p a b -> p (a b)").rearrange("p (l h) -> p l h", h=256)
                    sc.dma('sp', wstv, w1v[:, lq * 4:(lq + 1) * 4, :], writes=[('wst', lq % 2)], stream=1)
                    sc.op('act', lambda e, lq=lq, wstv=wstv: e.activation(out=w1b[:, lq * 4:(lq + 1) * 4, :], in_=wstv, func=AF.Copy), reads=[('wst', lq % 2)], writes=['w1b'])
                chk('c01')
                sc.dma('sp', w2f[:, :, :], P['b_cmp_w2'][kvi].rearrange("(c p) d -> p c d", p=128), writes=['w2f'], stream=2)
                sc.op('dve', lambda e: e.tensor_copy(out=w2b[:, :, :], in_=w2f[:, :, :]), reads=['w2f'], writes=['w2b'])
                chk('c02')
                sc.dma('sp', posf[0:32, 0:64].rearrange("p d -> p d") if False else posn[:, :], P['b_cmp_pos'][kvi], writes=['posn'], stream=2)
                sc.op('pe', lambda e: e.matmul(PS[6][0:64, 0:32], lhsT=posn[:, :], rhs=C['identf'][0:32, 0:32], start=True, stop=True), reads=['posn', 'const'], writes=[('ps', 6)])
                sc.op('dve', lambda e: e.tensor_copy(out=posT[:, :], in_=PS[6][0:64, 0:32]), reads=[('ps', 6)], writes=['posT'])
                chk('c0')
                for hc in range(2):
                    for l in range(32):
                        sc.op('pe', lambda e, l=l, hc=hc: e.matmul(PS[6][:, hc:hc + 1], lhsT=w1b[:, l, hc * 128:(hc + 1) * 128], rhs=posT[:, l:l + 1], start=(l == 0), stop=(l == 31)),
                              reads=['w1b', 'posT'], writes=[('ps', 6)])
                sc.op('dve', lambda e: e.tensor_copy(out=biasc[:, :], in_=PS[6][:, 0:2]), reads=[('ps', 6)], writes=['biasc'])
                chk('c1')
                for n4 in range(4):
                    sc.dma('sp', tT[:, :], kTd[kvi].ap()[n4, :, :], reads=['featd'], writes=['tT'], stream=4)
                    tv = tT[:, :].rearrange("p (c r) -> p c r", r=16)
                    for hc in range(2):
                        ph = PS[hc]
                        for l in range(32):
                            sc.op('pe', lambda e, l=l, hc=hc: e.matmul(ph[:, 0:255], lhsT=w1b[:, l, hc * 128:(hc + 1) * 128], rhs=tv[:, l // 16:l // 16 + 255, l % 16], start=(l == 0), stop=(l == 31)),
                                  reads=['w1b', 'tT'], writes=[('ps', hc)])
                        x0, x2, x3, th = xg
                        sc.op('dve', lambda e: e.tensor_scalar(out=x0[:, 0:255], in0=ph[:, 0:255], scalar1=biasc[:, hc:hc + 1], scalar2=None, op0=ALU.add), reads=[('ps', hc), 'biasc'], writes=['x0'])
                        sc.op('dve', lambda e: e.tensor_tensor(out=x2[:, 0:255], in0=x0[:, 0:255], in1=x0[:, 0:255], op=ALU.mult), reads=['x0'], writes=['x2'])
                        sc.op('dve', lambda e: e.tensor_tensor(out=x3[:, 0:255], in0=x2[:, 0:255], in1=x0[:, 0:255], op=ALU.mult), reads=['x0', 'x2'], writes=['x3'])
                        sc.op('dve', lambda e: e.scalar_tensor_tensor(out=x3[:, 0:255], in0=x3[:, 0:255], scalar=0.044715, in1=x0[:, 0:255], op0=ALU.mult, op1=ALU.add), reads=['x0', 'x3'], writes=['x3'])
                        sc.op('act', lambda e: e.activation(out=th[:, 0:255], in_=x3[:, 0:255], func=AF.Tanh, scale=0.7978845608028654), reads=['x3'], writes=['th'])
                        sc.op('dve', lambda e: e.scalar_tensor_tensor(out=th[:, 0:255], in0=th[:, 0:255], scalar=1.0, in1=x0[:, 0:255], op0=ALU.add, op1=ALU.mult), reads=['th', 'x0'], writes=['th'])
                        sc.op('dve', lambda e, hc=hc: e.tensor_scalar(out=gT[:, hc, 0:255], in0=th[:, 0:255], scalar1=0.5, scalar2=None, op0=ALU.mult), reads=['th'], writes=['gT'])
                        chk('c2')
                    if kvi == 0:
                        pk = PS[2]
                        for hc in range(2):
                            sc.op('pe', lambda e, hc=hc: e.matmul(pk[0:64, 0:256], lhsT=w2b[:, hc, :], rhs=gT[:, hc, :], start=(hc == 0), stop=(hc == 1)), reads=['w2b', 'gT'], writes=[('ps', 2)])
                        x0, x2, x3, th = xg
                        sc.op('dve', lambda e: e.tensor_copy(out=x0[0:64, :], in_=pk[0:64, 0:256]), reads=[('ps', 2)], writes=['x0'])
                        sc.op('act', lambda e: e.activation(out=x2[0:64, :], in_=x0[0:64, :], func=AF.Square), reads=['x0'], writes=['x2'])
                        sc.op('pe', lambda e: e.matmul(PS[3][0:64, 0:256], lhsT=cdn['blkf'][0:64, 0:64], rhs=x2[0:64, :], start=True, stop=True), reads=['x2', 'cdn'], writes=[('ps', 3)])
                        sc.op('dve', lambda e: e.tensor_scalar(out=x3[0:64, :], in0=PS[3][0:64, 0:256], scalar1=1.0 / 64, scalar2=EPS, op0=ALU.mult, op1=ALU.add), reads=[('ps', 3)], writes=['x3'])
                        sc.op('act', lambda e: e.activation(out=x3[0:64, :], in_=x3[0:64, :], func=AF.Sqrt), reads=['x3'], writes=['x3'])
                        sc.op('dve', lambda e: e.reciprocal(out=x3[0:64, :], in_=x3[0:64, :]), reads=['x3'], writes=['x3'])
                        sc.op('dve', lambda e, n4=n4: e.scalar_tensor_tensor(out=kcT[:, n4, :], in0=x0[0:64, :], scalar=gcol[0:64, 1:2], in1=x3[0:64, :], op0=ALU.mult, op1=ALU.mult),
                              reads=['x0', 'x3', 'gcol', 'kcT'], writes=['kcT'])
                    else:
                        for ct in range(2):
                            pv = PS[2 + ct]
                            for hc in range(2):
                                sc.op('pe', lambda e, hc=hc, ct=ct: e.matmul(pv[:, 0:64], lhsT=gT[:, hc, ct * 128:(ct + 1) * 128], rhs=w2b[:, hc, :], start=(hc == 0), stop=(hc == 1)),
                                      reads=['w2b', 'gT'], writes=[('ps', 2 + ct)])
                            sc.op('act', lambda e, ct=ct, n4=n4: e.activation(out=vaug[:, ct, n4, 0:64], in_=pv[:, 0:64], func=AF.Copy), reads=[('ps', 2 + ct), 'vaug'], writes=['vaug'])
            sc.op('dve', lambda e: e.memset(kcT[:, :, 255:256], 0.0), reads=['kcT'], writes=['kcT'])
            sc.phase()
        except StopHere:
            sc.phase()
            stopped = True
    k.st = st2
    if stopped or STOP == 'b1c':
        copy_x(k, x_d, xo_d, C)
        k.st = old
        st2.close()
        return
    B = {'Eb': [k.sb([128, 512], F32) for _ in range(2)], 'PT': [k.sb([128, 512], BF16) for _ in range(3)]}
    qTh = k.sb([64, S], BF16)
    kselT = k.sb([64, S], BF16)
    kwinT = k.sb([64, S], BF16)
    V1 = [k.sb([128, NT, 65], BF16) for _ in range(2)]
    oacc = k.sb([128, NT, 4, 64], F32)
    impacc = k.sb([128, NT, 64], F32)
    penT = k.sb([64, S], BF16)
    Bmat = k.sb([64, S], BF16)
    tabm = k.sb([128, 4096 + 128], F32)
    tslab = [k.sb([128, 512], F32) for _ in range(2)]
    smz = k.sb([128, 8], F32)
    imod = k.sb([128, 64], F32)
    iwk = k.sb([128, 64], F32)
    m8 = k.sb([128, 16], F32)
    pen = k.sb([128, 64], F32)
    obst = [k.sb([128, 256], BF16) for _ in range(2)]
    keepT = k.sb([128, 128], F32)
    addT = k.sb([128, 128], F32)
    sc.dma('sp', keepT[:, :], cdn['keepT'][:, :], writes=['keepT'], stream=2)
    sc.dma('sp', addT[:, :], cdn['addT'][:, :], writes=['keepT'], stream=2)
    bst = tabm[0:64, 0:S]
    sc.dma('sp', bst, cdn['Bmat'][:, :], writes=['tabm'], stream=2)
    sc.op('dve', lambda e: e.tensor_copy(out=Bmat[:, :], in_=bst), reads=['tabm'], writes=['Bmat'])
    for vi in range(2):
        sc.op('pool', lambda e, vi=vi: e.memset(V1[vi][:, :, 64:65], 1.0), writes=[('V1', vi)])
    for n4 in range(4):
        sc.dma('sp', kselT[:, :], kTd[2].ap()[n4, :, :], reads=['featd'], writes=['kselT'], stream=4)
        sc.dma('sp', kwinT[:, :], kTd[3].ap()[n4, :, :], reads=['featd'], writes=['kwinT'], stream=4)
        for vi in range(2):
            sc.dma('sp', V1[vi][:, :, 0:64], Vd[vi].ap().rearrange("(n p) h d -> p n h d", p=128)[:, :, n4, :], reads=['Vd'], writes=[('V1', vi)], stream=4)
        for g in range(4):
            hq = 4 * n4 + g
            sc.dma('sp', qTh[:, :], qTd.ap()[hq, :, :], reads=['featd'], writes=['qTh'], stream=5)
            build_head_tab(k, C, ohC, 6144, tabC, hq, relh, ohs, tslab, 'tabC')
            slabs = {}

            def tab_c(kt, q0, nq, hq=hq):
                key = (kt, q0)
                j = len(slabs) % 2
                slabs[key] = j
                dpp = q0 - (2048 * kt + 31)
                sc.dma('sp', tslab[j][:, 0:nq], AP(tabC.h, tabC.off + 2064 + dpp, [[WC - 16, 128], [1, nq]]), reads=['tabC'], writes=[('tslab', j)], stream=8)
                return tslab[j][:, 0:nq], ('tslab', j)

            def evac_c(qc, acc, g=g, hq=hq):
                for qb in range(4):
                    qt = qc * 4 + qb
                    a, ares = acc(qb)
                    sc.op('dve', lambda e: e.tensor_scalar(out=smz[:, 0:1], in0=a[:, 64:65], scalar1=1e-30, scalar2=None, op0=ALU.max), reads=[ares], writes=['smz'])
                    sc.op('dve', lambda e: e.reciprocal(out=smz[:, 1:2], in_=smz[:, 0:1]), reads=['smz'], writes=['smz1'])
                    sc.op('dve', lambda e: e.tensor_tensor(out=smz[:, 2:3], in0=smz[:, 1:2], in1=gates[:, qt, hq * 3:hq * 3 + 1], op=ALU.mult), reads=['smz1', 'gates'], writes=['smz2'])
                    sc.op('dve', lambda e: e.tensor_scalar(out=oacc[:, qt, g, :], in0=a[:, 0:64], scalar1=smz[:, 2:3], scalar2=(1.0 if 'c' in KBR else 0.0), op0=ALU.mult, op1=ALU.mult), reads=[ares, 'smz2'], writes=[('oacc', qt)])
                    if g == 0:
                        sc.op('dve', lambda e: e.tensor_scalar(out=impacc[:, qt, :], in0=a[:, 65:129], scalar1=smz[:, 1:2], scalar2=None, op0=ALU.mult), reads=[ares, 'smz1'], writes=[('imp', qt)])
                    else:
                        sc.op('dve', lambda e: e.scalar_tensor_tensor(out=impacc[:, qt, :], in0=a[:, 65:129], scalar=smz[:, 1:2], in1=impacc[:, qt, :], op0=ALU.mult, op1=ALU.add),
                              reads=[ares, 'smz1', ('imp', qt)], writes=[('imp', qt)])

            dense_attn(k, C, B, qTh,
                       lhsT_fn=lambda kt, n4=n4: (kcT[:, n4, kt * 128:(kt + 1) * 128], 'kcT'),
                       vaug_fn=lambda kt, n4=n4: (vaug[:, kt, n4, :], ['vaug']),
                       ncol=129, tab_fn=tab_c,
                       kts_fn=lambda qc: [0] if qc < 4 else [0, 1],
                       qb0_fn=lambda qc, kt: 0,
                       last_fn=lambda qc, qb: 0 if qc < 4 else 1,
                       evac_fn=evac_c)
        if STOP == 'b2':
            continue
        sc.op('pool', lambda e: e.memset(penT[:, 0:1024], 0.0), reads=['penT'], writes=['penT'])
        for qt in range(8, NT):
            ks = slice(64 - 2 * qt, 128 - 2 * qt)
            sc.op('dve', lambda e: e.tensor_tensor(out=imod[:, :], in0=impacc[:, qt, :], in1=keepT[:, ks], op=ALU.mult), reads=[('imp', qt), 'keepT'], writes=['imod'])
            sc.op('dve', lambda e: e.tensor_tensor(out=imod[:, :], in0=imod[:, :], in1=addT[:, ks], op=ALU.add), reads=['imod', 'keepT'], writes=['imod'])
            sc.op('dve', lambda e: e.memset(imod[:, 0:1], 3e9), reads=['imod'], writes=['imod'])
            sc.op('dve', lambda e: e.max(out=m8[:, 0:8], in_=imod[:, :]), reads=['imod'], writes=['m8'])
            sc.op('dve', lambda e: e.match_replace(out=iwk[:, :], in_to_replace=m8[:, 0:8], in_values=imod[:, :], imm_value=-3e38), reads=['imod', 'm8'], writes=['iwk'])
            sc.op('dve', lambda e: e.max(out=m8[:, 8:16], in_=iwk[:, :]), reads=['iwk', 'm8'], writes=['m8b'])
            sc.op('dve', lambda e: e.tensor_reduce(out=smz[:, 4:5], in_=m8[:, 8:16], axis=AX.X, op=ALU.min), reads=['m8b'], writes=['smz4'])
            sc.op('dve', lambda e: e.tensor_scalar(out=pen[:, :], in0=imod[:, :], scalar1=smz[:, 4:5], scalar2=None, op0=ALU.is_ge), reads=['imod', 'smz4'], writes=['pen'])
            sc.op('dve', lambda e: e.tensor_scalar(out=pen[:, :], in0=pen[:, :], scalar1=-1.0, scalar2=30000.0, op0=ALU.add, op1=ALU.mult), reads=['pen'], writes=['pen'])
            sc.op('pe', lambda e: e.matmul(PS[6][0:64, 0:128], lhsT=pen[:, :], rhs=C['identf'][:, :], start=True, stop=True), reads=['pen', 'const'], writes=[('ps', 6)])
            sc.op('act', lambda e, qt=qt: e.activation(out=penT[:, qt * 128:(qt + 1) * 128], in_=PS[6][0:64, 0:128], func=AF.Copy), reads=[('ps', 6)], writes=['penT'])
        for g in range(4):
            hq = 4 * n4 + g
            sc.dma('sp', qTh[:, :], qTd.ap()[hq, :, :], reads=['featd'], writes=['qTh'], stream=5)
            build_head_tab(k, C, ohD, 4224, tabD, hq, relh, ohs, tslab, 'tabD')
            build_head_tab(k, C, ohW, 1280, tabW, hq, relh, ohs, tslab, 'tabW')
            for br in (1, 2):
                if br == 1:
                    sc.dma('sp', tabm[:, 0:4096], AP(tabD.h, tabD.off + 127, [[WD - 1, 128], [1, 4096]]), reads=['tabD'], writes=['tabm'], stream=8)
                else:
                    sc.dma('sp', tabm[:, 0:1152], AP(tabW.h, tabW.off + 127, [[WW - 1, 128], [1, 1152]]), reads=['tabW'], writes=['tabm'], stream=8)

                def evac_sw(qc, acc, g=g, hq=hq, br=br):
                    for qb in range(4):
                        qt = qc * 4 + qb
                        a, ares = acc(qb)
                        sc.op('dve', lambda e: e.reciprocal(out=smz[:, 1:2], in_=a[:, 64:65]), reads=[ares], writes=['smz1'])
                        sc.op('dve', lambda e: e.scalar_tensor_tensor(out=smz[:, 2:3], in0=smz[:, 1:2], scalar=(1.0 if 'csw'[br] in KBR else 0.0), in1=gates[:, qt, hq * 3 + br:hq * 3 + br + 1], op0=ALU.mult, op1=ALU.mult), reads=['smz1', 'gates'], writes=['smz2'])
                        sc.op('dve', lambda e: e.scalar_tensor_tensor(out=oacc[:, qt, g, :], in0=a[:, 0:64], scalar=smz[:, 2:3], in1=oacc[:, qt, g, :], op0=ALU.mult, op1=ALU.add),
                              reads=[ares, 'smz2', ('oacc', qt)], writes=[('oacc', qt)])

                ksrc, kres, vsrc = (kselT, 'kselT', 0) if br == 1 else (kwinT, 'kwinT', 1)
                dense_attn(k, C, B, qTh,
                           lhsT_fn=lambda kt, ksrc=ksrc, kres=kres: (ksrc[:, kt * 128:(kt + 1) * 128], kres),
                           vaug_fn=lambda kt, vsrc=vsrc: (V1[vsrc][:, kt, :], [('V1', vsrc)]),
                           ncol=65,
                           tab_fn=lambda kt, q0, nq: (tabm[:, q0 - kt * 128:q0 - kt * 128 + nq], 'tabm'),
                           kts_fn=(lambda qc: list(range(0, 4 * qc + 4))) if br == 1 else (lambda qc: list(range(max(0, 4 * qc - 4), 4 * qc + 4))),
                           qb0_fn=lambda qc, kt: max(0, kt - 4 * qc),
                           last_fn=lambda qc, qb: 4 * qc + qb,
                           evac_fn=evac_sw,
                           pen_fn=(lambda kt, q0, nq: (Bmat[:, kt * 128:(kt + 1) * 128], penT[:, q0:q0 + nq], ['Bmat', 'penT'])) if br == 1 else None)
        for qt in range(NT):
            b = qt % 2
            sc.op('act', lambda e: e.activation(out=obst[b][:, :], in_=oacc[:, qt, :, :].rearrange("p g d -> p (g d)"), func=AF.Copy), reads=[('oacc', qt)], writes=[('obst', b)])
            sc.dma('pool', obd.ap()[qt * 128:(qt + 1) * 128, n4 * 256:(n4 + 1) * 256], obst[b][:, :], reads=[('obst', b)], writes=['obd'], stream=6)
        sc.phase()
    obt = [k.sb([128, D], BF16) for _ in range(2)]
    for n in range(NT):
        b = n % 2
        sc.dma('sp', obt[b][:, :], obd.ap()[n * 128:(n + 1) * 128, :], reads=['obd'], writes=[('obt', b)], stream=5)
        out_tail(k, C, obt[b][:, :], wob, oT, x_d, xo_d, n, ('obt', b))
    sc.phase()
    k.st = old
    st2.close()


def alloc_common(k):
    C = {}
    C['xt'] = [k.sb([128, 1024], F32) for _ in range(2)]
    C['sq'] = [k.sb([128, 1024], F32) for _ in range(2)]
    C['xb'] = [k.sb([128, 1024], BF16) for _ in range(2)]
    C['ss'] = [k.sb([128, 4], F32) for _ in range(2)]
    C['identb'] = k.sb([128, 128], BF16)
    C['identf'] = k.sb([128, 128], F32)
    C['gainT'] = k.sb([128, 8], F32)
    C['wst'] = [k.sb([128, 8, 128], F32) for _ in range(2)]
    C['wst2'] = [k.sb([128, 8, 128], F32) for _ in range(2)]
    C['wbf'] = [k.sb([128, 8, 128], BF16) for _ in range(2)]
    C['wbf2'] = [k.sb([128, 8, 128], BF16) for _ in range(2)]
    C['blk'] = k.sb([128, 128], BF16)
    C['onesb'] = k.sb([128, 128], BF16)
    C['relb'] = k.sb([33, 16], F32)
    return C


MIX_PARAMS = {
    0: ['norm1', 'a_w_in', 'a_q_gain', 'a_k_gain', 'a_w_out'],
    1: ['norm1', 'b_w_in', 'b_q_gain', 'b_k_gain', 'b_cmp_pos', 'b_cmp_w1', 'b_cmp_w2', 'b_w_out'],
    2: ['norm1', 'c_w_in', 'c_conv_w', 'c_a_log', 'c_dt_bias', 'c_out_gain', 'c_w_out'],
}
SHAPES = {
    'norm1': [D], 'norm2': [D], 'ffn_w_gate': [D, FF], 'ffn_w_up': [D, FF], 'ffn_w_down': [FF, D],
    'a_w_in': [D, 9216], 'a_q_gain': [3, 64], 'a_k_gain': [3, 64], 'a_w_out': [D, D],
    'b_w_in': [D, 2608], 'b_q_gain': [64], 'b_k_gain': [3, 64], 'b_cmp_pos': [2, 32, 64], 'b_cmp_w1': [2, 2048, 256],
    'b_cmp_w2': [2, 256, 64], 'b_w_out': [D, D],
    'c_w_in': [D, 4112], 'c_conv_w': [4, 3072], 'c_a_log': [8], 'c_dt_bias': [8], 'c_out_gain': [128], 'c_w_out': [D, D],
}


def consts():
    c = {'identf': np.eye(128, dtype=np.float32)}
    blk = np.zeros((128, 128), np.float32)
    blk[:64, :64] = 1
    blk[64:, 64:] = 1
    c['blkf'] = blk
    for g, (window, dil) in enumerate(A_GROUPS):
        d = np.arange(384) - 127
        c[f'ohA{g}'] = onehot_rows(d * dil, (d >= 0) & (d <= 128))
    i = np.arange(6144) - 2064
    c['ohC'] = onehot_rows(i, i >= 0)
    i = np.arange(4224) - 127
    c['ohD'] = onehot_rows(i, i >= 0)
    i = np.arange(1280) - 127
    c['ohW'] = onehot_rows(i, (i >= 0) & (i <= 511))
    cs_ = np.arange(256)[:, None] * 16
    ss_ = np.arange(64)[None, :] * 64
    c['cM'] = ((cs_ < ss_ + 64) & (cs_ + 32 > ss_) & (np.arange(256)[:, None] < 255)).astype(np.float32)
    qi = np.arange(128)[:, None]
    half = (qi >= 64).astype(np.int64)
    dl = np.arange(128)[None, :] - 64
    c['ckeepT'] = (dl < half - 1).astype(np.float32)
    c['caddT'] = np.where(dl == half, 2e9, np.where(dl == half - 1, 1e9, np.where(dl > half, -1e30, 0.0))).astype(np.float32)
    c['cBmat'] = (np.arange(4096)[None, :] // 64 == np.arange(64)[:, None]).astype(np.float32)
    p = np.arange(128)
    same = (p[:, None] // 64) == (p[None, :] // 64)
    c['cTB'] = (same & (p[:, None] <= p[None, :])).astype(np.float32)
    c['cBLK'] = same.astype(np.float32)
    c['cH0'] = np.repeat((p < 64).astype(np.float32)[:, None], 128, 1)
    c['cH1'] = np.repeat((p >= 64).astype(np.float32)[:, None], 128, 1)
    c['cONES'] = np.ones((128, 128), np.float32)
    c['cMT'] = np.where(same & (p[None, :] >= p[:, None]), 0.0, -1e30).astype(np.float32)
    c['cMBS'] = np.where(same & (p[None, :] < p[:, None]), 0.0, 1e30).astype(np.float32)
    return c


LAST_INPUT_NAMES = []


def build(layers=(0, 1, 2, 3), parts=('mix', 'ffn')):
    nc = bass.Bass("TRN2", target_bir_lowering=False)
    dr = {}
    LAST_INPUT_NAMES.clear()

    def din(name, shape):
        dr[name] = nc.dram_tensor(name, list(shape), F32, kind="ExternalInput").ap()
        LAST_INPUT_NAMES.append(name)
        return dr[name]

    x_in = din('x', [S, D])
    din('rel_bias', [32, 16])
    cs = consts()
    for name, v in cs.items():
        din(name, v.shape)
    for l in layers:
        p = f'l{l}_'
        if 'mix' in parts:
            for nm in MIX_PARAMS[l % 3]:
                din(p + nm, SHAPES[nm])
        if 'ffn' in parts:
            for nm in ('norm2', 'ffn_w_gate', 'ffn_w_up', 'ffn_w_down'):
                din(p + nm, SHAPES[nm])
    out = nc.dram_tensor('out', [S, D], F32, kind="ExternalOutput").ap()
    with ExitStack() as st:
        sc = Sched(nc, st)
        k = K(nc, st, sc)
        C = alloc_common(k)
        sc.dma('sp', C['identf'][:, :], dr['identf'][:, :], writes=['const'], stream=2)
        sc.op('dve', lambda e: e.tensor_copy(out=C['identb'][:, :], in_=C['identf'][:, :]), reads=['const'], writes=['const'])
        sc.dma('sp', C['identf'][:, :], dr['blkf'][:, :], reads=['const'], writes=['const'], stream=2)
        sc.op('dve', lambda e: e.tensor_copy(out=C['blk'][:, :], in_=C['identf'][:, :]), reads=['const'], writes=['const'])
        sc.dma('sp', C['identf'][:, :], dr['identf'][:, :], reads=['const'], writes=['const'], stream=2)
        sc.op('pool', lambda e: e.memset(C['relb'][:, :], -30000.0), writes=['relb'])
        sc.dma('sp', C['relb'][0:32, :], dr['rel_bias'][:, :], reads=['relb'], writes=['relb'], stream=2)
        cd = {}
        for nm in ('TB', 'BLK', 'H0', 'H1', 'ONES', 'MT', 'MBS'):
            cd[nm] = k.sb([128, 128], F32)
            sc.dma('sp', cd[nm][:, :], dr['c' + nm][:, :], writes=['cd'], stream=2)
        sc.op('dve', lambda e: e.tensor_copy(out=C['onesb'][:, :], in_=cd['ONES'][:, :]), reads=['cd'], writes=['const'])
        tabA = None
        kinds = set(l % 3 for l in layers) if 'mix' in parts else set()
        with ExitStack() as stt:
            k.st = stt
            C['oh'] = k.sb([33, 512], F32)
            C['relrep'] = k.sb([33, 16, 128], F32)
            C['rowst'] = [k.sb([128, 512], F32) for _ in range(2)]
            sc.op('dve', lambda e: e.tensor_copy(out=C['relrep'][:, :, :], in_=bc(C['relb'][:, :].unsqueeze(2), [33, 16, 128])), reads=['relb'], writes=['relrep'])
            if 0 in kinds:
                tabA = [k.dram([16, 128, 384]) for _ in range(3)]
                for g in range(3):
                    build_bias_rows(k, C, dr[f'ohA{g}'], 384, tabA[g])
            tabs = (dr['ohC'], dr['ohD'], dr['ohW'])
            sc.phase()
        k.st = st
        k.doff += int(os.environ.get('KDB', 0))
        k.dbase = k.doff
        cdn = {'M': dr['cM'], 'keepT': dr['ckeepT'], 'addT': dr['caddT'], 'Bmat': dr['cBmat'], 'blkf': cd['BLK']}
        cur = x_in
        if os.environ.get('KINPLACE'):
            copy_x(k, x_in, out, C)
            cur = out
        for l in layers:
            p = f'l{l}_'
            P = {nm[len(p):]: ap for nm, ap in dr.items() if nm.startswith(p)}
            if 'mix' in parts:
                if l % 3 == 0:
                    mixer_a(k, cur, out, P, C, tabA)
                elif l % 3 == 2:
                    mixer_c(k, cur, out, P, C, cd)
                else:
                    mixer_b(k, cur, out, P, C, tabs, cdn)
                cur = out
            if 'ffn' in parts and not (os.environ.get('KLASTMIX') and l == layers[-1]):
                ffn_layer(k, cur, out, P['norm2'], P['ffn_w_gate'], P['ffn_w_up'], P['ffn_w_down'], C)
                cur = out
    return nc


_NC_CACHE = {}


def kernel(**inputs):
    if 'nc' not in _NC_CACHE:
        _NC_CACHE['nc'] = build()
        _NC_CACHE['names'] = list(LAST_INPUT_NAMES)
    nc = _NC_CACHE['nc']
    cs = consts()
    x = np.ascontiguousarray(np.asarray(inputs['x'], dtype=np.float32))
    shared = {}
    for nm in _NC_CACHE['names']:
        if nm == 'x':
            continue
        if nm in cs:
            shared[nm] = cs[nm]
        else:
            shared[nm] = np.ascontiguousarray(np.asarray(inputs[nm], dtype=np.float32))
    in_maps = []
    for b in range(8):
        m = dict(shared)
        m['x'] = x[b]
        in_maps.append(m)
    res = run_bass_kernel_spmd(nc, in_maps, core_ids=list(range(8)))
    return np.stack([np.asarray(r['out'], dtype=np.float32) for r in res.results], axis=0)
```
